# Optimizing a Trainium2 kernel written in Bass

```python
import math
import jax, jax.numpy as jnp
from jax import lax
import numpy as np

D_MODEL = 1024
BATCH = 4
SEQ = 4096
DEPTH = 2

GRID_W = 64
CTX_LEN = 256
N_MIXERS = 4
GROUP_W = D_MODEL // N_MIXERS
MIX_W = N_MIXERS * GROUP_W
NORM_EPS = 1e-6
GLA_HEADS = 4
GLA_DV = GROUP_W // GLA_HEADS
GLA_DK = GLA_DV // 2
GLA_GATE_RANK = 16
GLA_TAU = 16.0
GLA_CHUNK = 64
ROPE_BASE = 10000.0
NA_HEADS = 4
NA_DH = GROUP_W // NA_HEADS
NA_KR = 8
NA_KC = 16
S5_GROUP = 16
S5_GROUPS = GROUP_W // S5_GROUP
S5_STATE = 64
POOL_WINDOWS = (2, 4, 8, 16)
POOL_GW = GROUP_W // len(POOL_WINDOWS)
CTX_SPLIT = (GLA_HEADS * GLA_DK, GROUP_W, GLA_GATE_RANK, GLA_GATE_RANK, GROUP_W, GROUP_W, GROUP_W)
LAT_SPLIT = (GLA_HEADS * GLA_DK, GROUP_W, GROUP_W, MIX_W)
N_CTX_COLS = sum(CTX_SPLIT)
N_IN_COLS = N_CTX_COLS + sum(LAT_SPLIT)

kernel_name = "hybrid_parallel_head_groups_diffusion_block"


def rmsnorm(x, g):
    xf = x.astype(jnp.float32)
    y = xf * lax.rsqrt(jnp.mean(xf * xf, axis=-1, keepdims=True) + NORM_EPS)
    return (y * g.astype(jnp.float32)).astype(x.dtype)


def split_cols(p, sizes):
    idx = [int(i) for i in np.cumsum(sizes)[:-1]]
    return jnp.split(p, idx, axis=-1)


def flip_seq(a):
    return jnp.flip(a, axis=1)


def axial_rope_tables(n_tokens):
    t = jnp.arange(n_tokens)
    half = GLA_DK // 2
    freqs = ROPE_BASE ** (-jnp.arange(0, half, 2, dtype=jnp.float32) / half)
    def table(pos):
        ang = pos.astype(jnp.float32)[:, None] * freqs[None, :]
        ang = jnp.concatenate([ang, ang], axis=-1)
        return jnp.cos(ang)[:, None, :], jnp.sin(ang)[:, None, :]
    return table(t // GRID_W) + table(t % GRID_W)


def rotate_half(x):
    x1, x2 = jnp.split(x, 2, axis=-1)
    return jnp.concatenate([-x2, x1], axis=-1)


def apply_axial_rope(x, tables):
    cos_r, sin_r, cos_c, sin_c = tables
    xr, xc = jnp.split(x, 2, axis=-1)
    return jnp.concatenate([xr * cos_r + rotate_half(xr) * sin_r,
                            xc * cos_c + rotate_half(xc) * sin_c], axis=-1)


def gla_chunked(q, k, v, logg, s0):
    bsz, n_tok, n_h, _ = q.shape
    dv = v.shape[-1]
    n_chunks = n_tok // GLA_CHUNK
    def to_chunks(a):
        return a.reshape(bsz, n_chunks, GLA_CHUNK, n_h, a.shape[-1]).transpose(1, 0, 3, 2, 4)
    lower = jnp.tril(jnp.ones((GLA_CHUNK, GLA_CHUNK), dtype=bool))
    def step(state, inp):
        qc, kc, vc, gc = inp
        b = jnp.cumsum(gc, axis=-2)
        diff = b[:, :, :, None, :] - b[:, :, None, :, :]
        decay = jnp.exp(jnp.where(lower[:, :, None], diff, -jnp.inf))
        att = jnp.einsum('bhid,bhjd,bhijd->bhij', qc, kc, decay)
        o = (jnp.einsum('bhij,bhje->bhie', att, vc)
             + jnp.einsum('bhid,bhde->bhie', qc * jnp.exp(b), state))
        b_last = b[:, :, -1:, :]
        state = (jnp.exp(b_last)[:, :, 0, :, None] * state
                 + jnp.einsum('bhjd,bhje->bhde', kc * jnp.exp(b_last - b), vc))
        return state, o
    _, o = lax.scan(step, s0, (to_chunks(q), to_chunks(k), to_chunks(v), to_chunks(logg)))
    return o.transpose(1, 0, 3, 2, 4).reshape(bsz, n_tok, n_h, dv)


def gla_final_state(k, v, logg):
    b = jnp.cumsum(logg, axis=1)
    return jnp.einsum('bnhd,bnhe->bhde', k * jnp.exp(b[:, -1:] - b), v)


def gla_head_norm(o, g):
    y = o * lax.rsqrt(jnp.mean(o * o, axis=-1, keepdims=True) + NORM_EPS) * g.astype(jnp.float32)
    return y.reshape(o.shape[0], o.shape[1], GLA_HEADS * GLA_DV)


def gla_mixer(q, k, v, gf, gb, kc, vc, gfc, gbc, qc, w_gate, b_gate, g_norm, rope):
    f32 = jnp.float32
    def heads(a, d):
        return a.astype(f32).reshape(a.shape[0], a.shape[1], GLA_HEADS, d)
    def log_gate(g_lr, direction):
        z = jnp.einsum('bnr,re->bne', g_lr.astype(f32), w_gate[direction].astype(f32)) + b_gate[direction].astype(f32)
        return heads(jax.nn.log_sigmoid(z) / GLA_TAU, GLA_DK)
    kch, vch = heads(kc, GLA_DK), heads(vc, GLA_DV)
    lgc_f, lgc_b = log_gate(gfc, 0), log_gate(gbc, 1)
    s_f = gla_final_state(kch, vch, lgc_f)
    s_b = gla_final_state(flip_seq(kch), flip_seq(vch), flip_seq(lgc_b))
    qh = apply_axial_rope(heads(q, GLA_DK), rope) * GLA_DK ** -0.5
    kh = apply_axial_rope(heads(k, GLA_DK), rope)
    vh = heads(v, GLA_DV)
    o = (gla_chunked(qh, kh, vh, log_gate(gf, 0), s_f)
         + flip_seq(gla_chunked(flip_seq(qh), flip_seq(kh), flip_seq(vh), flip_seq(log_gate(gb, 1)), s_b)))
    y = gla_head_norm(o, g_norm)
    y_ctx = None
    if qc is not None:
        qch = heads(qc, GLA_DK) * GLA_DK ** -0.5
        zero = jnp.zeros_like(s_f)
        oc = (gla_chunked(qch, kch, vch, lgc_f, zero)
              + flip_seq(gla_chunked(flip_seq(qch), flip_seq(kch), flip_seq(vch), flip_seq(lgc_b), zero)))
        y_ctx = gla_head_norm(oc, g_norm)
    return y, y_ctx


def na_mixer(q, k, v, kc, vc, qc, rpb, rows):
    f32 = jnp.float32
    bsz, n_tok, _ = q.shape
    scale = NA_DH ** -0.5
    kr = min(NA_KR, rows)
    def grid(a):
        return a.astype(f32).reshape(bsz, rows, GRID_W, NA_HEADS, NA_DH).transpose(0, 3, 1, 2, 4)
    def seq_heads(a):
        return a.astype(f32).reshape(a.shape[0], a.shape[1], NA_HEADS, NA_DH).transpose(0, 2, 1, 3)
    qg, kg, vg = grid(q) * scale, grid(k), grid(v)
    kch, vch = seq_heads(kc), seq_heads(vc)
    r = jnp.arange(rows)
    row_idx = jnp.clip(r - kr // 2, 0, rows - kr)[:, None] + jnp.arange(kr)[None, :]
    col = jnp.arange(GRID_W)
    cs = jnp.clip(col - NA_KC // 2, 0, GRID_W - NA_KC)
    col_mask = (col[None, :] >= cs[:, None]) & (col[None, :] < cs[:, None] + NA_KC)
    ro = row_idx - r[:, None] + (NA_KR - 1)
    co = jnp.clip(col[None, :] - col[:, None], -(NA_KC - 1), NA_KC - 1) + (NA_KC - 1)
    bias = rpb.astype(f32)[:, ro[:, None, :, None], co[None, :, None, :]]
    k_rows, v_rows = kg[:, :, row_idx], vg[:, :, row_idx]
    s_win = jnp.einsum('bhrqd,bhrikd->bhrqik', qg, k_rows) + bias[None]
    s_win = jnp.where(col_mask[:, None, :], s_win, -jnp.inf)
    s_ctx = jnp.einsum('bhrqd,bhnd->bhrqn', qg, kch)
    n_win = kr * GRID_W
    p = jax.nn.softmax(jnp.concatenate([s_win.reshape(bsz, NA_HEADS, rows, GRID_W, n_win), s_ctx], axis=-1), axis=-1)
    o = (jnp.einsum('bhrqik,bhrikd->bhrqd', p[..., :n_win].reshape(s_win.shape), v_rows)
         + jnp.einsum('bhrqn,bhnd->bhrqd', p[..., n_win:], vch))
    y = o.transpose(0, 2, 3, 1, 4).reshape(bsz, n_tok, NA_HEADS * NA_DH)
    y_ctx = None
    if qc is not None:
        pc = jax.nn.softmax(jnp.einsum('bhnd,bhmd->bhnm', seq_heads(qc) * scale, kch), axis=-1)
        oc = jnp.einsum('bhnm,bhmd->bhnd', pc, vch)
        y_ctx = oc.transpose(0, 2, 1, 3).reshape(qc.shape[0], qc.shape[1], NA_HEADS * NA_DH)
    return y, y_ctx


def s5_discretise(lam_re, lam_im, log_dt, b_re, b_im):
    dt = jnp.exp(log_dt)[:, None]
    mag = jnp.exp(lam_re * dt)
    ang = lam_im * dt
    lb_re, lb_im = mag * jnp.cos(ang), mag * jnp.sin(ang)
    num_re, num_im = lb_re - 1.0, lb_im
    den = lam_re * lam_re + lam_im * lam_im
    coef_re = ((num_re * lam_re + num_im * lam_im) / den)[..., None]
    coef_im = ((num_im * lam_re - num_re * lam_im) / den)[..., None]
    bb_re = coef_re * b_re - coef_im * b_im
    bb_im = coef_re * b_im + coef_im * b_re
    return lb_re, lb_im, bb_re, bb_im


def complex_affine_combine(e1, e2):
    a1r, a1i, b1r, b1i = e1
    a2r, a2i, b2r, b2i = e2
    return (a1r * a2r - a1i * a2i, a1r * a2i + a1i * a2r,
            a2r * b1r - a2i * b1i + b2r, a2r * b1i + a2i * b1r + b2i)


def s5_scan(u, disc, x0, reverse):
    lb_re, lb_im, bb_re, bb_im = disc
    bu_re = jnp.einsum('bngh,gph->bngp', u, bb_re)
    bu_im = jnp.einsum('bngh,gph->bngp', u, bb_im)
    a_re = jnp.broadcast_to(lb_re, bu_re.shape)
    a_im = jnp.broadcast_to(lb_im, bu_im.shape)
    ar, ai, xr, xi = lax.associative_scan(complex_affine_combine, (a_re, a_im, bu_re, bu_im), reverse=reverse, axis=1)
    if x0 is not None:
        x0r, x0i = x0[0][:, None], x0[1][:, None]
        xr, xi = xr + ar * x0r - ai * x0i, xi + ar * x0i + ai * x0r
    return xr, xi


def s5_readout(xr, xi, c_re, c_im):
    y = (jnp.einsum('bngp,ghp->bngh', xr, c_re.astype(jnp.float32))
         - jnp.einsum('bngp,ghp->bngh', xi, c_im.astype(jnp.float32)))
    return y.reshape(y.shape[0], y.shape[1], S5_GROUPS * S5_GROUP)


def s5_glu(y, w, b):
    g = jax.nn.gelu(y)
    return g * jax.nn.sigmoid(g @ w.astype(jnp.float32) + b.astype(jnp.float32))


def s5_mixer(u, uc, ctx_out, lam_re, lam_im, log_dt, b_re, b_im, c_re, c_im, d_skip, w_glu, b_glu):
    f32 = jnp.float32
    def groups(a):
        return a.astype(f32).reshape(a.shape[0], a.shape[1], S5_GROUPS, S5_GROUP)
    ul, ucg = groups(u), groups(uc)
    d_skip = d_skip.astype(f32)
    y = u.astype(f32) * d_skip
    yc = uc.astype(f32) * d_skip if ctx_out else None
    for direction, rev in enumerate((False, True)):
        disc = s5_discretise(lam_re[direction].astype(f32), lam_im[direction].astype(f32),
                             log_dt[direction].astype(f32), b_re[direction].astype(f32), b_im[direction].astype(f32))
        xcr, xci = s5_scan(ucg, disc, None, rev)
        fin = 0 if rev else -1
        xr, xi = s5_scan(ul, disc, (xcr[:, fin], xci[:, fin]), rev)
        y = y + s5_readout(xr, xi, c_re[direction], c_im[direction])
        if ctx_out:
            yc = yc + s5_readout(xcr, xci, c_re[direction], c_im[direction])
    return s5_glu(y, w_glu, b_glu), (s5_glu(yc, w_glu, b_glu) if ctx_out else None)


def centred_mean(u, w):
    n = u.shape[1]
    csum = jnp.concatenate([jnp.zeros_like(u[:, :1]), jnp.cumsum(u, axis=1)], axis=1)
    t = jnp.arange(n)
    lo = jnp.clip(t - w // 2, 0, n)
    hi = jnp.clip(t - w // 2 + w, 0, n)
    return (csum[:, hi] - csum[:, lo]) / (hi - lo).astype(u.dtype)[None, :, None]


def pool_mixer(u, w_pool, scale):
    uf = u.astype(jnp.float32)
    parts = jnp.split(uf, len(POOL_WINDOWS), axis=-1)
    outs = [jnp.einsum('bnc,ce->bne', centred_mean(g, w) - g, w_pool[i].astype(jnp.float32))
            for i, (g, w) in enumerate(zip(parts, POOL_WINDOWS))]
    return jnp.concatenate(outs, axis=-1) * scale.astype(jnp.float32)


def setup_inputs(seed: int = 0) -> dict:
    key = jax.random.key(seed)
    ks = jax.random.split(key, 32)
    f32 = jnp.float32
    def nrm(k, shape, std):
        return std * jax.random.normal(k, shape, f32)
    L = DEPTH
    lam_im0 = jnp.pi * jnp.arange(S5_STATE, dtype=f32)
    return {
        "x": nrm(ks[0], (BATCH, SEQ, D_MODEL), 1.0),
        "c": nrm(ks[1], (BATCH, D_MODEL), 1.0),
        "ctx": nrm(ks[2], (BATCH, CTX_LEN, D_MODEL), 1.0),
        "c_ctx": nrm(ks[3], (D_MODEL,), 1.0),
        "w_mod": nrm(ks[4], (L, D_MODEL, 3 * D_MODEL), 0.5 * D_MODEL ** -0.5),
        "b_mod": nrm(ks[5], (L, 3 * D_MODEL), 0.02),
        "g_pre": 1.0 + nrm(ks[6], (L, D_MODEL), 0.02),
        "g_post": 1.0 + nrm(ks[7], (L, D_MODEL), 0.02),
        "w_in": nrm(ks[8], (L, D_MODEL, N_IN_COLS), D_MODEL ** -0.5),
        "w_out": nrm(ks[9], (L, MIX_W, D_MODEL), MIX_W ** -0.5),
        "gla_w_gate": nrm(ks[10], (L, 2, GLA_GATE_RANK, GLA_HEADS * GLA_DK), GLA_GATE_RANK ** -0.5),
        "gla_b_gate": nrm(ks[11], (L, 2, GLA_HEADS * GLA_DK), 0.1),
        "gla_g_norm": 1.0 + nrm(ks[12], (L, GLA_DV), 0.02),
        "na_rpb": nrm(ks[13], (L, NA_HEADS, 2 * NA_KR - 1, 2 * NA_KC - 1), 0.1),
        "s5_lam_re": -0.5 + nrm(ks[14], (L, 2, S5_GROUPS, S5_STATE), 0.01),
        "s5_lam_im": lam_im0 + nrm(ks[15], (L, 2, S5_GROUPS, S5_STATE), 0.01),
        "s5_log_dt": jax.random.uniform(ks[16], (L, 2, S5_GROUPS), f32, math.log(1e-3), math.log(1e-1)),
        "s5_b_re": nrm(ks[17], (L, 2, S5_GROUPS, S5_STATE, S5_GROUP), (2.0 * S5_GROUP) ** -0.5),
        "s5_b_im": nrm(ks[18], (L, 2, S5_GROUPS, S5_STATE, S5_GROUP), (2.0 * S5_GROUP) ** -0.5),
        "s5_c_re": nrm(ks[19], (L, 2, S5_GROUPS, S5_GROUP, S5_STATE), S5_STATE ** -0.5),
        "s5_c_im": nrm(ks[20], (L, 2, S5_GROUPS, S5_GROUP, S5_STATE), S5_STATE ** -0.5),
        "s5_d": nrm(ks[21], (L, GROUP_W), 1.0),
        "s5_w_glu": nrm(ks[22], (L, GROUP_W, GROUP_W), GROUP_W ** -0.5),
        "s5_b_glu": nrm(ks[23], (L, GROUP_W), 0.02),
        "pool_w": nrm(ks[24], (L, len(POOL_WINDOWS), POOL_GW, POOL_GW), POOL_GW ** -0.5),
        "pool_scale": 1.0 + nrm(ks[25], (L, GROUP_W), 0.02),
    }


def reference(x, c, ctx, c_ctx, w_mod, b_mod, g_pre, g_post, w_in, w_out,
              gla_w_gate, gla_b_gate, gla_g_norm, na_rpb,
              s5_lam_re, s5_lam_im, s5_log_dt, s5_b_re, s5_b_im, s5_c_re, s5_c_im, s5_d, s5_w_glu, s5_b_glu,
              pool_w, pool_scale):
    dt = x.dtype
    n_tok = x.shape[1]
    rows = n_tok // GRID_W
    rope = axial_rope_tables(n_tok)
    xc = ctx
    for l in range(DEPTH):
        last = l == DEPTH - 1
        shift, scale, gate = jnp.split(jax.nn.silu(c) @ w_mod[l] + b_mod[l], 3, axis=-1)
        shift_c, scale_c, gate_c = jnp.split(jax.nn.silu(c_ctx) @ w_mod[l] + b_mod[l], 3, axis=-1)
        h = rmsnorm(x, g_pre[l]) * (1.0 + scale[:, None]) + shift[:, None]
        hc = rmsnorm(xc, g_pre[l]) * (1.0 + scale_c) + shift_c
        p_kv, p_q = jnp.split(h @ w_in[l], [N_CTX_COLS], axis=-1)
        gla_k, gla_v, gla_gf, gla_gb, na_k, na_v, s5_u = split_cols(p_kv, CTX_SPLIT)
        gla_q, na_q, pool_u, gate_cols = split_cols(p_q, LAT_SPLIT)
        if last:
            pc_kv = hc @ w_in[l][:, :N_CTX_COLS]
            gla_qc = na_qc = pool_uc = gate_cols_c = None
        else:
            pc_kv, pc_q = jnp.split(hc @ w_in[l], [N_CTX_COLS], axis=-1)
            gla_qc, na_qc, pool_uc, gate_cols_c = split_cols(pc_q, LAT_SPLIT)
        gla_kc, gla_vc, gla_gfc, gla_gbc, na_kc, na_vc, s5_uc = split_cols(pc_kv, CTX_SPLIT)
        y_gla, yc_gla = gla_mixer(gla_q, gla_k, gla_v, gla_gf, gla_gb, gla_kc, gla_vc, gla_gfc, gla_gbc, gla_qc,
                                  gla_w_gate[l], gla_b_gate[l], gla_g_norm[l], rope)
        y_na, yc_na = na_mixer(na_q, na_k, na_v, na_kc, na_vc, na_qc, na_rpb[l], rows)
        y_s5, yc_s5 = s5_mixer(s5_u, s5_uc, not last, s5_lam_re[l], s5_lam_im[l], s5_log_dt[l], s5_b_re[l],
                               s5_b_im[l], s5_c_re[l], s5_c_im[l], s5_d[l], s5_w_glu[l], s5_b_glu[l])
        y_pool = pool_mixer(pool_u, pool_w[l], pool_scale[l])
        y = jnp.concatenate([y_gla, y_na, y_s5, y_pool], axis=-1).astype(dt) * jax.nn.silu(gate_cols)
        x = x + gate[:, None] * rmsnorm(y @ w_out[l], g_post[l])
        if not last:
            yc_pool = pool_mixer(pool_uc, pool_w[l], pool_scale[l])
            yc = jnp.concatenate([yc_gla, yc_na, yc_s5, yc_pool], axis=-1).astype(dt) * jax.nn.silu(gate_cols_c)
            xc = xc + gate_c * rmsnorm(yc @ w_out[l], g_post[l])
    return x
```

```python
import numpy as np
import ml_dtypes
from contextlib import ExitStack
import concourse.bass as bass
import concourse.mybir as mybir
from concourse.bass_utils import run_bass_kernel_spmd

F32 = mybir.dt.float32
BF16 = mybir.dt.bfloat16
I32 = mybir.dt.int32
ALU = mybir.AluOpType
AF = mybir.ActivationFunctionType
AX = mybir.AxisListType

NCTX = 256
NLAT = 4096
T = NCTX + NLAT
D = 1024
NTILE = T // 128
EPS = 1e-6
PI = float(np.pi)
TWO_PI = float(2 * np.pi)
NCH = T // 8
NCH_CTX = NCTX // 8
import os
NOSCHED = bool(os.environ.get('MK_NOSCHED'))


class Sched:
    NDMA = 12

    def __init__(self, nc):
        self.nc = nc
        self.eng = {"pe": nc.tensor, "dve": nc.vector, "act": nc.scalar,
                    "pool": nc.gpsimd, "sp": nc.sync}
        self.sem = {k: nc.alloc_semaphore("sem_" + k) for k in self.eng}
        self.cnt = {k: 0 for k in self.eng}
        self.seen = {k: {} for k in self.eng}
        self.dq = {}
        for q in ("sp", "pool", "act"):
            self.dq[q] = {"sems": [nc.alloc_semaphore(f"dq_{q}_{i}") for i in range(self.NDMA)], "n": 0}
        self.lastw = {}
        self.readers = {}
        self.ntens = 0
        self.rr = 0
        self.pending = []

    def _wait(self, eng, ev):
        if ev is None:
            return
        if ev[0] == "e":
            _, src, val = ev
            if src == "pe" and eng == "pe":
                return
            key = ("e", src)
            sem = self.sem[src]
        else:
            _, q, slot, val = ev
            key = ("d", q, slot)
            sem = self.dq[q]["sems"][slot]
        if self.seen[eng].get(key, 0) >= val:
            return
        self.seen[eng][key] = val
        self.eng[eng].wait_ge(sem, val)

    def _deps(self, eng, reads, writes):
        for t in reads:
            self._wait(eng, self.lastw.get(t))
        for t in writes:
            self._wait(eng, self.lastw.get(t))
            for ev in self.readers.get(t, {}).values():
                self._wait(eng, ev)

    def _commit(self, ev, reads, writes):
        for t in reads:
            d = self.readers.setdefault(t, {})
            d[ev[0:2] if ev[0] == "e" else ev[0:3]] = ev
        for t in writes:
            self.lastw[t] = ev
            self.readers[t] = {}

    WHOLE_DRAM = ("RP",)

    @classmethod
    def _names(cls, aps):
        out = []
        for a in aps:
            if a is None or isinstance(a, (int, float)):
                continue
            nm = a.tensor.name
            if "DRam" in type(a.tensor).__name__ and nm not in cls.WHOLE_DRAM:
                nm = f"{nm}@{a.offset}:{tuple(map(tuple, a.ap))}"
            out.append(nm)
        return out

    LAT = float(os.environ.get('MK_LAT', '0.35'))

    def op(self, eng, method, *args, reads=(), writes=(), **kw):
        r = self._names(reads)
        w = self._names(writes)
        cost = self._cost(eng, method, args, kw)
        self.pending.append(("op", eng, method, args, kw, r, w, cost))

    def dma(self, out, in_, q=None, **kw):
        if q is None:
            q = "sp"
        r = self._names([in_])
        w = self._names([out])
        nbytes = 1
        for d_ in out.shape:
            nbytes *= d_
        nbytes *= 4
        self.pending.append(("dma", q, None, (out, in_), kw, r, w, 0.15, 2.0 + nbytes / 150e3))

    @staticmethod
    def _free(ap):
        n = 1
        for d_ in ap.shape[1:]:
            n *= d_
        return n

    def _cost(self, eng, method, args, kw):
        try:
            if eng == "pe":
                rhs = args[2]
                n = max(64, self._free(rhs))
                c = n / 2400.0
                if rhs.dtype == F32:
                    c *= 4
                return c + 0.07
            n = self._free(args[0])
            c = n / 960.0 * (1.5 if eng == "dve" else 1.3) + float(os.environ.get('MK_OVH', '0.1'))
            if eng == "pool":
                c = n / 400.0 + 0.2
            if method == "tensor_tensor_scan":
                c = 2 * n / 960.0 + 0.1
            return c
        except Exception:
            return 0.3

    def flush(self):
        ops = self.pending
        self.pending = []
        n = len(ops)
        if n == 0:
            return
        lastw = {}
        readers = {}
        preds = [None] * n
        for i, o in enumerate(ops):
            ps = set()
            for t in o[5]:
                j = lastw.get(t)
                if j is not None:
                    ps.add(j)
            for t in o[6]:
                j = lastw.get(t)
                if j is not None:
                    ps.add(j)
                ps.update(readers.get(t, ()))
            for t in o[5]:
                readers.setdefault(t, []).append(i)
            for t in o[6]:
                lastw[t] = i
                readers[t] = []
            ps.discard(i)
            preds[i] = ps
        succs = [[] for _ in range(n)]
        indeg = [0] * n
        for i in range(n):
            indeg[i] = len(preds[i])
            for p in preds[i]:
                succs[p].append(i)
        fin = [0.0] * n
        rdy = [0.0] * n
        crit = [-1] * n
        epred = [-1] * n
        elast = {e: -1 for e in self.eng}
        stt_ = [0.0] * n
        eng_free = {e: 0.0 for e in self.eng}
        ready = {e: [] for e in self.eng}
        import heapq
        for i in range(n):
            if indeg[i] == 0:
                heapq.heappush(ready[ops[i][1]], i)
        order = []
        remaining = n
        WINDOW = int(os.environ.get('MK_WINDOW', '24'))
        if NOSCHED:
            order = list(range(n))
            remaining = 0
        while remaining:
            best = None
            for e, lst in ready.items():
                if not lst:
                    continue
                cand = heapq.nsmallest(WINDOW, lst)
                for i in cand:
                    st = max(eng_free[e], rdy[i])
                    key = (st, i)
                    if best is None or key < best[0]:
                        best = (key, e, i)
            (st, i), e, _ = best
            ready[e].remove(i)
            heapq.heapify(ready[e])
            o = ops[i]
            dur = o[7]
            epred[i] = elast[e] if eng_free[e] > rdy[i] else -2
            elast[e] = i
            stt_[i] = st
            eng_free[e] = st + dur
            fin[i] = st + (o[8] if o[0] == "dma" else dur)
            order.append(i)
            remaining -= 1
            for s_ in succs[i]:
                if fin[i] + self.LAT > rdy[s_]:
                    rdy[s_] = fin[i] + self.LAT
                    crit[s_] = i
                indeg[s_] -= 1
                if indeg[s_] == 0:
                    heapq.heappush(ready[ops[s_][1]], s_)
        if os.environ.get('MK_CRIT') and n > 2000:
            i = max(range(n), key=lambda k: fin[k])
            chain = []
            while i >= 0:
                chain.append(i)
                i = epred[i] if epred[i] >= 0 else crit[i]
            agg = {}
            for k in chain:
                o = ops[k]
                key = (o[1], o[2] or 'dma', 'engwait' if epred[k] >= 0 else 'data')
                a = agg.setdefault(key, [0, 0.0]); a[0] += 1; a[1] += o[7]
            print('CRIT n=%d len=%d makespan=%.1f' % (n, len(chain), max(fin)))
            for key, a in sorted(agg.items(), key=lambda kv: -kv[1][1])[:14]:
                print('   ', key, a[0], round(a[1], 1))
        if os.environ.get('MK_VERBOSE'):
            busy = {}
            for o in ops:
                busy[o[1]] = busy.get(o[1], 0.0) + o[7]
            print('FLUSH n=%d makespan_us=%.1f busy=%s' % (n, max(fin) if not NOSCHED else -1, {k: round(v) for k, v in busy.items()}), flush=True)
        for i in order:
            o = ops[i]
            if o[0] == "op":
                self._emit_op(o[1], o[2], o[3], o[4], o[5], o[6])
            else:
                self._emit_dma(o[1], o[3][0], o[3][1], o[4], o[5], o[6])

    def _emit_op(self, eng, method, args, kw, r, w):
        self._deps(eng, r, w)
        ins = getattr(self.eng[eng], method)(*args, **kw)
        self.cnt[eng] += 1
        ins.then_inc(self.sem[eng], 1)
        self._commit(("e", eng, self.cnt[eng]), r, w)
        return ins

    def _emit_dma(self, q, out, in_, kw, r, w):
        d = self.dq[q]
        n = d["n"]
        slot = n % self.NDMA
        val = 16 * (n // self.NDMA + 1)
        if n >= self.NDMA:
            self._wait(q, ("d", q, slot, val - 16))
        self._deps(q, r, w)
        ins = self.eng[q].dma_start(out=out, in_=in_, **kw)
        ins.then_inc(d["sems"][slot], 16)
        d["n"] = n + 1
        self._commit(("d", q, slot, val), r, w)
        return ins

    def mm(self, out, lhsT, rhs, start=True, stop=True, **kw):
        return self.op("pe", "matmul", out, lhsT, rhs, start=start, stop=stop,
                       reads=[lhsT, rhs], writes=[out], **kw)

    def act(self, out, in_, func, bias=None, scale=None, accum_out=None):
        kw = {}
        rd = [in_]
        if bias is not None:
            kw["bias"] = bias
            rd.append(bias)
        if scale is not None:
            kw["scale"] = scale
            rd.append(scale)
        wr = [out]
        if accum_out is not None:
            kw["accum_out"] = accum_out
            wr.append(accum_out)
        return self.op("act", "activation", out, in_, func, reads=rd, writes=wr, **kw)

    def tt(self, out, in0, in1, op, eng="dve"):
        return self.op(eng, "tensor_tensor", out, in0, in1, op, reads=[in0, in1], writes=[out])

    def ts(self, out, in0, s1, s2, op0, op1=None, eng="dve"):
        kw = {}
        if op1 is not None:
            kw["op1"] = op1
        return self.op(eng, "tensor_scalar", out, in0, s1, s2, op0, reads=[in0, s1, s2], writes=[out], **kw)

    def stt(self, out, in0, scalar, in1, op0, op1, eng="dve"):
        return self.op(eng, "scalar_tensor_tensor", out, in0, scalar, in1, op0, op1,
                       reads=[in0, scalar, in1], writes=[out])

    def copy(self, out, in_, eng="dve"):
        if eng == "act":
            return self.act(out, in_, AF.Copy)
        return self.op(eng, "tensor_copy", out, in_, reads=[in_], writes=[out])

    def evac(self, out, in_):
        self.rr += 1
        return self.copy(out, in_, eng=("dve" if self.rr % 2 else "act"))

    def memset(self, ap, val, eng="dve"):
        return self.op(eng, "memset", ap, val, reads=[], writes=[ap])

    def scan(self, out, d0, d1, initial, op0=ALU.mult, op1=ALU.add):
        return self.op("dve", "tensor_tensor_scan", out, d0, d1, initial, op0, op1,
                       reads=[d0, d1, initial], writes=[out])

    def recip(self, out, in_):
        return self.op("dve", "reciprocal", out, in_, reads=[in_], writes=[out])

    def barrier(self):
        self.flush()
        for e in self.eng:
            for src in self.eng:
                if self.cnt[src] > 0:
                    self._wait(e, ("e", src, self.cnt[src]))
            for q, d in self.dq.items():
                n = d["n"]
                for slot in range(min(n, self.NDMA)):
                    last_n = ((n - 1 - slot) // self.NDMA) * self.NDMA + slot
                    self._wait(e, ("d", q, slot, 16 * (last_n // self.NDMA + 1)))


class Pool:
    def __init__(self, S):
        self.S = S
        self.es = ExitStack()

    def t(self, name, shape, dtype=F32):
        self.S.ntens += 1
        h = self.es.enter_context(self.S.nc.sbuf_tensor(f"{name}_{self.S.ntens}", list(shape), dtype))
        return h.ap() if hasattr(h, "ap") else h

    def close(self):
        self.S.barrier()
        self.es.close()


def dram_rows_bcast(t_ap, offset, n, parts=128):
    return bass.AP(t_ap.tensor, offset, [[0, parts], [1, n]])


def make_consts():
    c = {}
    c["ident"] = np.eye(128, dtype=np.float32)
    c["anti"] = np.eye(128, dtype=np.float32)[::-1].copy()
    c["anti64"] = np.eye(64, dtype=np.float32)[::-1].copy()
    c["anti32"] = np.eye(32, dtype=np.float32)[::-1].copy()
    sel = np.zeros((2, 2, 128), np.float32)
    sel[0, 0] = 1
    sel[1, 1] = 1
    c["sel"] = sel
    selm = np.zeros((2, 128), np.float32)
    selm[0, :64] = 1
    selm[1, 64:] = 1
    c["selm"] = selm
    j = np.arange(128)
    mf = (j[:, None] <= j[None, :]).astype(np.float32)
    c["maskf"] = mf
    c["maskb"] = mf.T.copy()
    tok = np.arange(NLAT)
    rowp = (tok // 64).astype(np.float32)
    colp = (tok % 64).astype(np.float32)
    pos = np.zeros((128, T), np.float32)
    freq = np.zeros((128, 1), np.float32)
    half = 16
    fr = 10000.0 ** (-np.arange(0, half, 2, dtype=np.float32) / half)
    for p in range(128):
        d = p % 32
        pos[p, NCTX:] = rowp if d < 16 else colp
        freq[p, 0] = fr[d % 8]
    c["pos"] = pos
    c["freq"] = freq
    hm = np.zeros((128, 4, 128), np.float32)
    bdm = np.zeros((128, 4, 64), np.float32)
    for h in range(4):
        hm[h * 32:(h + 1) * 32, h, :] = 1
        bdm[h * 32:(h + 1) * 32, h, :] = 1
    c["hm"] = hm
    c["bdm"] = bdm.reshape(128, 256)
    col = np.arange(64)
    cs = np.clip(col - 8, 0, 48)
    ok = (col[:, None] >= cs[None, :]) & (col[:, None] < cs[None, :] + 16)
    cm = np.where(ok, 0.0, -30000.0).astype(np.float32)
    c["colmask"] = np.concatenate([cm, cm], 0)
    c["negblk"] = np.full((128, 64), -30000.0, np.float32)
    s_idx = np.repeat(np.arange(8), 16)
    c["tmaskf"] = (s_idx[:, None] <= s_idx[None, :]).astype(np.float32)
    c["tmaskb"] = (s_idx[:, None] >= s_idx[None, :]).astype(np.float32)
    c["ciota"] = np.tile(np.arange(NCH, dtype=np.float32)[None, :], (128, 1))
    rc = np.zeros((4, PT_PAD), np.float32)
    for i, w in enumerate((2, 4, 8, 16)):
        for (n, off) in ((NCTX, PAD_CTX), (NLAT, PAD_LAT)):
            t = np.arange(n)
            lo = np.clip(t - w // 2, 0, n)
            hi = np.clip(t - w // 2 + w, 0, n)
            rc[i, off:off + n] = 1.0 / (hi - lo)
    c["poolrc"] = rc
    return c


PAD_CTX = 16
PAD_LAT = 16 + NCTX + 32
PT_PAD = PAD_LAT + NLAT + 16

CONST_SHAPES = None

WEIGHT_NAMES = ["w_mod", "b_mod", "g_pre", "g_post", "w_in", "w_out", "gla_w_gate", "gla_b_gate",
                "gla_g_norm", "na_rpb", "s5_lam_re", "s5_lam_im", "s5_log_dt", "s5_b_re", "s5_b_im",
                "s5_c_re", "s5_c_im", "s5_d", "s5_w_glu", "s5_b_glu", "pool_w", "pool_scale"]
WEIGHT_SHAPES = {
    "w_mod": (2, 1024, 3072), "b_mod": (2, 3072), "g_pre": (2, 1024), "g_post": (2, 1024),
    "w_in": (2, 1024, 2848), "w_out": (2, 1024, 1024), "gla_w_gate": (2, 2, 16, 128),
    "gla_b_gate": (2, 2, 128), "gla_g_norm": (2, 64), "na_rpb": (2, 4, 15, 31),
    "s5_lam_re": (2, 2, 16, 64), "s5_lam_im": (2, 2, 16, 64), "s5_log_dt": (2, 2, 16),
    "s5_b_re": (2, 2, 16, 64, 16), "s5_b_im": (2, 2, 16, 64, 16), "s5_c_re": (2, 2, 16, 16, 64),
    "s5_c_im": (2, 2, 16, 16, 64), "s5_d": (2, 256), "s5_w_glu": (2, 256, 256), "s5_b_glu": (2, 256),
    "pool_w": (2, 4, 64, 64), "pool_scale": (2, 256),
}

FM_BLOCKS = [
    (1184, 128, 0), (2848, 128, 128), (0, 128, 256), (2976, 128, 384), (384, 32, 512),
    (1312, 128, 544), (1440, 128, 672), (416, 128, 800), (544, 128, 928), (1568, 128, 1056), (1696, 128, 1184)]
PF_ROWS = 1312
TM_CHUNKS = [
    (128, 256, 0, False), (672, 256, 256, False), (928, 256, 512, False),
    (1824, 512, 768, True), (2336, 512, 1280, True)]
PT_COLS = 1792
NWCOL = 3104
GROUPS = [(0, 256)] + [(256 + 512 * i, 512) for i in range(8)]


class _Stop(Exception):
    pass


def build(debug=False, nlayers=2, stop_after=None):
    nc = bass.Bass("TRN2", target_bir_lowering=False)
    S = Sched(nc)

    def dram(name, shape, dtype=F32, kind="Internal"):
        return nc.dram_tensor(name, list(shape), dtype, kind=kind).ap()

    dbg_kind = "ExternalOutput" if debug else "Internal"
    xin = dram("xin", [T, D], kind="ExternalInput")
    cc = dram("cc", [128, 8, 2], kind="ExternalInput")
    W = {n: dram(n, WEIGHT_SHAPES[n], kind="ExternalInput") for n in WEIGHT_NAMES}
    consts = make_consts()
    C = {n: dram("k_" + n, v.shape, kind="ExternalInput") for n, v in consts.items()}
    out = dram("out", [NLAT, D], kind="ExternalOutput")
    xs = dram("xs", [T, D], kind=dbg_kind)
    PF = dram("PF", [PF_ROWS, T], kind=dbg_kind)
    PT = dram("PT", [T, PT_COLS], kind=dbg_kind)
    YS = dram("YS", [T, 1024], kind=dbg_kind)
    OG = dram("OG", [T, 256])
    OGB = dram("OGB", [T, 256])
    YSF = dram("YSF", [T, 256], kind=dbg_kind)
    YSB = dram("YSB", [T, 256], kind=dbg_kind)
    COS = dram("COS", [128, T])
    SIN = dram("SIN", [128, T])
    RP = dram("RP", [60, 160])
    MODR = dram("MODR", [2, 2, 3072])

    PS = []
    for i in range(8):
        h = nc.alloc_psum_tensor(f"psum{i}", [128, 512], F32)
        PS.append(h.ap() if hasattr(h, "ap") else h)

    G = Pool(S)
    ident = G.t("ident", [128, 128]); S.dma(ident, C["ident"])
    identb = G.t("identb", [128, 128], BF16); S.copy(identb, ident)
    anti = G.t("anti", [128, 128]); S.dma(anti, C["anti"])
    antib = G.t("antib", [128, 128], BF16); S.copy(antib, anti)
    anti32 = G.t("anti32", [32, 32]); S.dma(anti32, C["anti32"])
    anti32b = G.t("anti32b", [32, 32], BF16); S.copy(anti32b, anti32)
    anti64 = G.t("anti64", [64, 64]); S.dma(anti64, C["anti64"])
    ones1 = G.t("ones1", [128, 1]); S.memset(ones1, 1.0)
    MOD = [G.t("mod0", [128, 3072]), G.t("mod1", [128, 3072])]
    gpost = G.t("gpost", [128, 1024])

    rr_cache = {}

    def range_reduce(P, out_s, out_c, ang, shape, slot=0):
        key = (id(P), tuple(shape), slot)
        if key not in rr_cache:
            rr_cache[key] = (P.t("rr_ki", shape, I32), P.t("rr_kf", shape), P.t("rr_ph", shape))
        ki, kf, ph = rr_cache[key]
        S.ts(ki, ang, 1.0 / TWO_PI, None, ALU.mult)
        S.copy(kf, ki)
        S.stt(ph, kf, -TWO_PI, ang, ALU.mult, ALU.add)
        S.ts(ph, ph, -PI, PI, ALU.max, ALU.min)
        S.act(out_s, ph, AF.Sin)
        S.act(kf, ph, AF.Sin, scale=0.5)
        S.act(kf, kf, AF.Square)
        S.act(out_c, kf, AF.Identity, bias=1.0, scale=-2.0)

    P = Pool(S)
    freq = P.t("freq", [128, 1]); S.dma(freq, C["freq"])
    for (t0, n) in [(0, 1088), (1088, 1088), (2176, 1088), (3264, 1088)]:
        pos = P.t("pos", [128, n]); S.dma(pos, C["pos"][:, t0:t0 + n])
        ang = P.t("ang", [128, n])
        S.ts(ang, pos, freq, None, ALU.mult)
        sn = P.t("sn", [128, n]); cs_ = P.t("cs", [128, n])
        range_reduce(P, sn, cs_, ang, [128, n])
        S.dma(SIN[:, t0:t0 + n], sn)
        S.dma(COS[:, t0:t0 + n], cs_)
    cst = P.t("cst", [128, 8, 2]); S.dma(cst, cc)
    css = P.t("css", [128, 8, 2]); S.act(css, cst, AF.Silu)
    wmb = [P.t("wm0", [128, 3072]), P.t("wm1", [128, 3072])]
    bm2 = [P.t("bm0", [2, 512]), P.t("bm1", [2, 512])]
    modr2 = [P.t("modr0", [2, 512]), P.t("modr1", [2, 512])]
    for l2 in range(nlayers):
        for k in range(8):
            wm = wmb[k % 2]
            S.dma(wm, W["w_mod"][l2, k * 128:(k + 1) * 128, :], q=("sp" if k % 2 == 0 else "pool"))
            for n in range(6):
                S.mm(PS[n][0:2, :], css[:, k, :], wm[:, n * 512:(n + 1) * 512], start=(k == 0), stop=(k == 7))
        for n in range(6):
            bm = bm2[n % 2]; modr = modr2[n % 2]
            S.dma(bm, bass.AP(W["b_mod"].tensor, l2 * 3072 + n * 512, [[0, 2], [1, 512]]))
            S.tt(modr, PS[n][0:2, :], bm, ALU.add)
            S.dma(MODR[l2, :, n * 512:(n + 1) * 512], modr, q="act")
    P.close()

    try:
      for l in range(nlayers):
        x_src = xin if l == 0 else xs
        last = (l == nlayers - 1) and not debug

        P = Pool(S)
        selt = P.t("selt", [2, 2, 128]); S.dma(selt, C["sel"])
        modr2 = [P.t("modr0", [2, 512]), P.t("modr1", [2, 512])]
        for n in range(6):
            modr = modr2[n % 2]
            S.dma(modr, MODR[l, :, n * 512:(n + 1) * 512], q="sp")
            for r in range(2):
                pb = PS[6 + r]
                S.mm(pb, selt[:, r, :], modr)
                S.evac(MOD[r][:, n * 512:(n + 1) * 512], pb)
        gpre = P.t("gpre", [128, 1024])
        S.dma(gpre, dram_rows_bcast(W["g_pre"], l * 1024, 1024))
        S.dma(gpost, dram_rows_bcast(W["g_post"], l * 1024, 1024))
        for r in range(2):
            S.stt(MOD[r][:, 1024:2048], MOD[r][:, 1024:2048], 1.0, gpre, ALU.add, ALU.mult)
        PM = P
        P = Pool(S)
        Wb = P.t("Wb", [128, 8, NWCOL], BF16)
        wst = [P.t("wst0", [128, 2848]), P.t("wst1", [128, 2848])]
        for k in range(8):
            st = wst[k % 2]
            S.dma(st, W["w_in"][l, k * 128:(k + 1) * 128, :], q=("sp" if k % 2 == 0 else "pool"))
            S.copy(Wb[:, k, 0:1424], st[:, 0:1424], eng="dve")
            S.copy(Wb[:, k, 1424:2848], st[:, 1424:2848], eng="act")
            for (c0, r0) in ((1184, 2848), (0, 2976)):
                sv = st[:, c0:c0 + 128].rearrange("p (a t e) -> p a t e", a=8, t=2, e=8)
                dv = Wb[:, k, r0:r0 + 128].rearrange("p (a t e) -> p a t e", a=8, t=2, e=8)
                S.ts(dv[:, :, 0, :], sv[:, :, 1, :], -1.0, None, ALU.mult, eng="pool")
                S.copy(dv[:, :, 1, :], sv[:, :, 0, :], eng="pool")
        xt_b = [P.t("xt0", [128, 1024]), P.t("xt1", [128, 1024])]
        junk = P.t("junk", [128, 1024])
        h32 = P.t("h32", [128, 1024])
        hb_b = [P.t("hb0", [128, 1024], BF16), P.t("hb1", [128, 1024], BF16)]
        hT_b = [P.t("hT0", [128, 8, 512], BF16), P.t("hT1", [128, 8, 512], BF16)]
        fst_b = [P.t(f"fst{i}", [128, 512]) for i in range(4)]
        ropec = [P.t("ropec0", [128, 512]), P.t("ropec1", [128, 512])]
        ropes = [P.t("ropes0", [128, 512]), P.t("ropes1", [128, 512])]
        tst_b = [P.t("tst0", [128, PT_COLS]), P.t("tst1", [128, PT_COLS])]
        stat = [P.t("stat0", [128, 4]), P.t("stat1", [128, 4])]
        nt = 0
        nf = 0
        for gi, (tok0, n) in enumerate(GROUPS):
            hT = hT_b[gi % 2]
            r = 0 if tok0 >= NCTX else 1
            for ti in range(n // 128):
                xt = xt_b[nt % 2]; hb = hb_b[nt % 2]; sv_ = stat[nt % 2]
                S.dma(xt, x_src[tok0 + ti * 128: tok0 + (ti + 1) * 128, :], q="sp")
                S.act(junk, xt, AF.Square, accum_out=sv_[:, 0:1])
                S.ts(sv_[:, 1:2], sv_[:, 0:1], 1.0 / D, EPS, ALU.mult, ALU.add)
                S.act(sv_[:, 2:3], sv_[:, 1:2], AF.Sqrt)
                S.recip(sv_[:, 3:4], sv_[:, 2:3])
                S.stt(h32, xt, sv_[:, 3:4], MOD[r][:, 1024:2048], ALU.mult, ALU.mult)
                S.tt(hb, h32, MOD[r][:, 0:1024], ALU.add, eng="pool")
                for half in range(2):
                    pt_ = PS[half]
                    for kk in range(4):
                        k = half * 4 + kk
                        S.mm(pt_[:, kk * 128:(kk + 1) * 128], hb[:, k * 128:(k + 1) * 128], identb)
                    S.evac(hT[:, half * 4:(half + 1) * 4, ti * 128:(ti + 1) * 128],
                           pt_.rearrange("p (a b) -> p a b", a=4))
                nt += 1
            rc_ = ropec[gi % 2]; rs_ = ropes[gi % 2]
            S.dma(rc_[:, 0:n], COS[:, tok0:tok0 + n], q="sp")
            S.dma(rs_[:, 0:n], SIN[:, tok0:tok0 + n], q="sp")
            held = None
            for bi, (wc, ncol, prow) in enumerate(FM_BLOCKS):
                pb = PS[2 + (bi % 3)]
                for k in range(8):
                    S.mm(pb[0:ncol, 0:n], Wb[:, k, wc:wc + ncol], hT[:, k, 0:n], start=(k == 0), stop=(k == 7))
                fs = fst_b[nf % 4]; nf += 1
                S.evac(fs[0:ncol, 0:n], pb[0:ncol, 0:n])
                if bi in (0, 2):
                    held = (fs, prow)
                    continue
                if bi in (1, 3):
                    f0, prow0 = held
                    S.tt(f0[:, 0:n], f0[:, 0:n], rc_[:, 0:n], ALU.mult, eng="pool")
                    S.tt(fs[:, 0:n], fs[:, 0:n], rs_[:, 0:n], ALU.mult, eng="pool")
                    S.tt(f0[:, 0:n], f0[:, 0:n], fs[:, 0:n], ALU.add)
                    S.dma(PF[prow0:prow0 + 128, tok0:tok0 + n], f0[:, 0:n], q="act")
                    continue
                S.dma(PF[prow:prow + ncol, tok0:tok0 + n], fs[0:ncol, 0:n], q="act")
            for ti in range(n // 128):
                ts_ = tst_b[ti % 2]
                for ci, (wc, ncol, pcol, silu) in enumerate(TM_CHUNKS):
                    pb = PS[5 + (ci % 3)]
                    for k in range(8):
                        S.mm(pb[:, 0:ncol], hT[:, k, ti * 128:(ti + 1) * 128], Wb[:, k, wc:wc + ncol],
                             start=(k == 0), stop=(k == 7))
                    if silu:
                        S.act(ts_[:, pcol:pcol + ncol], pb[:, 0:ncol], AF.Silu)
                    else:
                        S.copy(ts_[:, pcol:pcol + ncol], pb[:, 0:ncol], eng="dve")
                S.dma(PT[tok0 + ti * 128: tok0 + (ti + 1) * 128, :], ts_, q="act")
        P.close()
        PM.close()

        if stop_after == 'A':
            raise _Stop()
        P = Pool(S)
        zt = P.t("zt", [60, 160]); S.memset(zt, 0.0)
        S.dma(RP, zt)
        S.dma(RP[:, 64:95], W["na_rpb"][l].rearrange("h r c -> (h r) c"))
        Gall = P.t("Gall", [64, 60, 2, 64])
        for dup in range(2):
            S.dma(Gall[:, :, dup, :], bass.AP(RP.tensor, 16, [[1, 64], [160, 60], [1, 64]]))
        colm = P.t("colm", [128, 64]); S.dma(colm, C["colmask"])
        a64 = anti64
        BT = P.t("BT", [128, 4, 15, 64])
        for h in range(4):
            for r8 in range(0, 15, 8):
                nr = min(8, 15 - r8)
                pb = PS[(h * 2 + r8 // 8) % 4]
                for j in range(nr):
                    ro = r8 + j
                    S.mm(pb[:, j * 64:(j + 1) * 64], Gall[:, h * 15 + ro].rearrange("p a b -> p (a b)"), a64)
                for j in range(nr):
                    ro = r8 + j
                    S.tt(BT[:, h, 14 - ro, :], pb[:, j * 64:(j + 1) * 64], colm, ALU.add)
        negblk = P.t("negblk", [128, 64]); S.dma(negblk, C["negblk"])
        NCOMP = 40
        comp_tiles = [P.t(f"comp{i}", [128, 128], BF16) for i in range(NCOMP)]
        comp_map = {}

        def get_comp(h, blocks):
            key = (h, blocks)
            if key in comp_map:
                return comp_map[key]
            idx = len(comp_map)
            assert idx < NCOMP
            tl = comp_tiles[idx]
            for (a, b_), (valid, ro) in zip(((0, 0), (0, 1), (1, 0), (1, 1)), blocks):
                dst = tl[a * 64:(a + 1) * 64, b_ * 64:(b_ + 1) * 64]
                if valid:
                    S.copy(dst, BT[a * 64:(a + 1) * 64, h, 14 - ro, :], eng="pool")
                else:
                    S.copy(dst, negblk[a * 64:(a + 1) * 64, :], eng="pool")
            comp_map[key] = tl
            return tl

        KT = P.t("KT", [128, 2, T], BF16)
        QT = P.t("QT", [128, 2, T], BF16)
        Vb = P.t("Vb", [128, NTILE, 4, 65], BF16)
        S.memset(Vb.rearrange("p a b c -> p (a b c)"), 1.0, eng="pool")
        ldq = [P.t("ldq0", [128, 1088]), P.t("ldq1", [128, 1088])]
        nl = 0
        for c2 in range(2):
            for t0 in range(0, T, 1088):
                b = ldq[nl % 2]; nl += 1
                S.dma(b, PF[544 + c2 * 128: 544 + (c2 + 1) * 128, t0:t0 + 1088], q="sp")
                S.ts(QT[:, c2, t0:t0 + 1088], b, 0.125, None, ALU.mult)
                b = ldq[nl % 2]; nl += 1
                S.dma(b, PF[800 + c2 * 128: 800 + (c2 + 1) * 128, t0:t0 + 1088], q="sp")
                S.copy(KT[:, c2, t0:t0 + 1088], b, eng="act")
        ldv = [P.t("ldv0", [128, 256]), P.t("ldv1", [128, 256])]
        for ti in range(NTILE):
            b = ldv[ti % 2]
            S.dma(b, PT[ti * 128:(ti + 1) * 128, 256:512], q="sp")
            S.copy(Vb[:, ti, :, 0:64], b.rearrange("p (a b) -> p a b", a=4), eng=("dve" if ti % 2 == 0 else "act"))
        Pb = [P.t(f"Pb{i}", [128, 7, 128], BF16) for i in range(3)]
        on_ = [P.t("on0", [128, 4, 65]), P.t("on1", [128, 4, 65])]
        yn = [P.t("yn0", [128, 256]), P.t("yn1", [128, 256])]
        rcn = [P.t("rcn0", [128, 4, 1]), P.t("rcn1", [128, 4, 1])]
        kt_cache = {}

        def na_ktiles(qt):
            if qt in kt_cache:
                return kt_cache[qt]
            if qt < 2:
                ktiles = [(0, None), (1, None)]
            else:
                r0 = (qt - 2) * 2
                rows_needed = set()
                for b_ in range(2):
                    stt_ = min(max(r0 + b_ - 4, 0), 56)
                    rows_needed.update(range(stt_, stt_ + 8))
                kts = sorted(set(r // 2 for r in rows_needed))
                ktiles = []
                for kt in kts:
                    blocks = []
                    for a_ in range(2):
                        for b_ in range(2):
                            krow = kt * 2 + a_; qrow = r0 + b_
                            stt_ = min(max(qrow - 4, 0), 56)
                            valid = stt_ <= krow < stt_ + 8
                            blocks.append((valid, krow - qrow + 7))
                    ktiles.append((kt + 2, tuple(blocks)))
                ktiles += [(0, None), (1, None)]
            kt_cache[qt] = ktiles
            return ktiles

        items = [(qt, h) for qt in range(NTILE) for h in range(4)]

        def na_scores(n):
            qt, h = items[n]
            ktiles = na_ktiles(qt); nk = len(ktiles)
            c2 = h // 2; pp = (h % 2) * 64
            psA = PS[(n % 2) * 2]; psB = PS[(n % 2) * 2 + 1]
            pbuf = Pb[n % 3]
            for idx, (kt, blocks) in enumerate(ktiles):
                pdst = (psA if idx < 4 else psB)[:, (idx % 4) * 128:(idx % 4 + 1) * 128]
                S.mm(pdst, KT[pp:pp + 64, c2, kt * 128:(kt + 1) * 128], QT[pp:pp + 64, c2, qt * 128:(qt + 1) * 128],
                     start=True, stop=(blocks is None))
                if blocks is not None:
                    S.mm(pdst, identb, get_comp(h, blocks), start=False, stop=True)
            n1 = min(nk, 4)
            S.act(pbuf[:, 0:n1, :], psA[:, 0:n1 * 128].rearrange("p (a b) -> p a b", a=n1), AF.Exp)
            if nk > 4:
                S.act(pbuf[:, 4:nk, :], psB[:, 0:(nk - 4) * 128].rearrange("p (a b) -> p a b", a=nk - 4), AF.Exp)

        def na_pv(n):
            qt, h = items[n]
            ktiles = na_ktiles(qt); nk = len(ktiles)
            pO = PS[4 + (qt % 2)]
            pbuf = Pb[n % 3]
            for idx, (kt, blocks) in enumerate(ktiles):
                S.mm(pO[:, h * 65:(h + 1) * 65], pbuf[:, idx, :], Vb[:, kt, h, :], start=(idx == 0), stop=(idx == nk - 1))
            if h == 3:
                ob = on_[qt % 2]
                S.evac(ob, pO[:, 0:260].rearrange("p (a b) -> p a b", a=4))
                S.recip(rcn[qt % 2], ob[:, :, 64:65])
                S.tt(yn[qt % 2].rearrange("p (a b) -> p a b", a=4), ob[:, :, 0:64],
                     rcn[qt % 2].to_broadcast([128, 4, 64]), ALU.mult)
                S.dma(YS[qt * 128:(qt + 1) * 128, 256:512], yn[qt % 2], q="act")

        for step in range(len(items) + 1):
            if step < len(items):
                na_scores(step)
            if step >= 1:
                na_pv(step - 1)
        PP = Pool(S)
        wpl = PP.t("wpl", [128, 2, 64]); wplb = PP.t("wplb", [128, 2, 64], BF16)
        S.dma(wpl, W["pool_w"][l].rearrange("(a b) c e -> (b c) a e", b=2))
        S.copy(wplb, wpl)
        pscale = PP.t("pscale", [128, 256]); S.dma(pscale, dram_rows_bcast(W["pool_scale"], l * 256, 256))
        HALO = 32
        SEGW = PAD_LAT + 2048 + HALO
        segs = [(0, SEGW, [(PAD_CTX, 0, NCTX), (PAD_LAT, NCTX, NCTX + 2048 + HALO)], list(range(0, 18))),
                (PAD_LAT + 2048 - HALO, PT_PAD - (PAD_LAT + 2048 - HALO), [(0, NCTX + 2048 - HALO, T)], list(range(18, NTILE)))]
        pU = PP.t("pU", [128, SEGW]); prc = PP.t("prc", [128, SEGW])
        psA_ = PP.t("psA", [128, SEGW]); psB_ = PP.t("psB", [128, SEGW])
        pdb = [PP.t("pdb0", [128, SEGW], BF16), PP.t("pdb1", [128, SEGW], BF16)]
        pst = [PP.t("pst0", [128, 256]), PP.t("pst1", [128, 256])]
        for (seg0, seglen, pieces, tiles_) in segs:
            for tl in range(2):
                U = pU; rc = prc; sA = psA_; sB = psB_
                S.memset(U, 0.0, eng="pool")
                for (loff, c0, c1) in pieces:
                    S.dma(U[:, loff:loff + (c1 - c0)], PF[1056 + tl * 128: 1056 + (tl + 1) * 128, c0:c1], q="sp")
                for hh in range(2):
                    S.dma(rc[hh * 64:(hh + 1) * 64, 0:seglen],
                          dram_rows_bcast(C["poolrc"], (tl * 2 + hh) * PT_PAD + seg0, seglen, parts=64), q="sp")
                S.memset(sA, 0.0, eng="pool"); S.memset(sB, 0.0, eng="pool")
                L0, L1 = 12, seglen - 12
                fins = [None, None]
                for hh in range(2):
                    w = (2, 4, 8, 16)[tl * 2 + hh]
                    ps_ = slice(hh * 64, (hh + 1) * 64)
                    eng = "dve" if hh == 0 else "pool"
                    S.tt(sA[ps_, L0:L1], U[ps_, L0 - 1:L1 - 1], U[ps_, L0:L1], ALU.add, eng=eng)
                    cur, oth = sA, sB
                    sh = 1
                    ww = 2
                    while ww < w:
                        S.tt(oth[ps_, L0:L1], cur[ps_, L0 - sh:L1 - sh], cur[ps_, L0 + sh:L1 + sh], ALU.add, eng=eng)
                        cur, oth = oth, cur
                        sh *= 2
                        ww *= 2
                    S.tt(oth[ps_, 0:seglen], cur[ps_, 0:seglen], rc[ps_, 0:seglen], ALU.mult, eng=eng)
                    S.tt(oth[ps_, 0:seglen], oth[ps_, 0:seglen], U[ps_, 0:seglen], ALU.subtract, eng=eng)
                    fins[hh] = oth
                S.copy(pdb[tl][0:64, 0:seglen], fins[0][0:64, 0:seglen], eng="act")
                S.copy(pdb[tl][64:128, 0:seglen], fins[1][64:128, 0:seglen], eng="act")
            for ti in tiles_:
                goff = (PAD_CTX + ti * 128) if ti < 2 else (PAD_LAT + (ti - 2) * 128)
                off = goff - seg0
                pbs = (PS[6], PS[7])
                for i in range(4):
                    tl, hh = i // 2, i % 2
                    ps_ = slice(hh * 64, (hh + 1) * 64)
                    S.mm(pbs[hh][:, tl * 64:(tl + 1) * 64], pdb[tl][ps_, off:off + 128], wplb[ps_, tl, :])
                for hh in range(2):
                    S.tt(pst[ti % 2].rearrange("p (tl hh e) -> p tl hh e", tl=2, hh=2)[:, :, hh, :],
                         pbs[hh][:, 0:128].rearrange("p (tl e) -> p tl e", tl=2),
                         pscale.rearrange("p (tl hh e) -> p tl hh e", tl=2, hh=2)[:, :, hh, :], ALU.mult)
                S.dma(YS[ti * 128:(ti + 1) * 128, 768:1024], pst[ti % 2], q="act")
        PP.close()
        P.close()

        if stop_after == 'N':
            raise _Stop()
        PU = Pool(S)
        UTf = PU.t("UTf", [128, 16, NCH], BF16)
        UTb = PU.t("UTb", [128, 16, NCH], BF16)
        P = Pool(S)
        hm = P.t("hm", [128, 4, 128]); S.dma(hm, C["hm"])
        hmb = P.t("hmb", [128, 4, 128], BF16); S.copy(hmb, hm)
        bdm = P.t("bdm", [128, 256]); S.dma(bdm, C["bdm"])
        maskt = [P.t("maskf", [128, 128]), P.t("maskb", [128, 128])]
        S.dma(maskt[0], C["maskf"]); S.dma(maskt[1], C["maskb"])
        gnorm = P.t("gnorm", [128, 4, 64])
        for h in range(4):
            S.dma(gnorm[:, h, :], dram_rows_bcast(W["gla_g_norm"], l * 64, 64))
        OGd = [OG, OGB]
        for d in range(2):
            wg = P.t("wg", [16, 128]); negb = P.t("negb", [128, 1])
            Sbd = P.t("Sbd", [128, 256]); Sbdb = P.t("Sbdb", [128, 256], BF16); stmp = P.t("stmp", [128, 256])
            glr = P.t("glr", [16, 512])
            qr2 = [P.t("qr0", [128, 512]), P.t("qr1", [128, 512])]
            kr2 = [P.t("kr0", [128, 512]), P.t("kr1", [128, 512])]
            e1 = P.t("e1", [128, 512]); sp_ = P.t("sp", [128, 512]); cs_ = P.t("cs", [128, 512]); cb = P.t("cb", [128, 512])
            EQ = P.t("EQ", [128, 512]); EK = P.t("EK", [128, 512]); EH = P.t("EH", [128, 512])
            tots2 = [P.t("tots0", [128, 4, 3]), P.t("tots1", [128, 4, 3])]
            qtb2 = [P.t("qtb0", [128, 512], BF16), P.t("qtb1", [128, 512], BF16)]
            ktb2 = [P.t("ktb0", [128, 512], BF16), P.t("ktb1", [128, 512], BF16)]
            khb2 = [P.t("khb0", [128, 512], BF16), P.t("khb1", [128, 512], BF16)]
            Qbd = [P.t("Qbd0", [128, 4, 128], BF16), P.t("Qbd1", [128, 4, 128], BF16)]
            attm = [P.t("attm0", [128, 4, 128], BF16), P.t("attm1", [128, 4, 128], BF16)]
            vt = [P.t("vt0", [128, 256]), P.t("vt1", [128, 256])]
            vb = [P.t("vb0", [128, 256], BF16), P.t("vb1", [128, 256], BF16)]
            khT = [P.t("khT0", [128, 128], BF16), P.t("khT1", [128, 128], BF16)]
            osb = [P.t("osb0", [128, 256]), P.t("osb1", [128, 256])]
            ps_att = PS[1 + d]; ps_po = PS[3 + d]; ps_st = PS[6 + d]
            S.dma(wg, W["gla_w_gate"][l, d])
            S.dma(negb, bass.AP(W["gla_b_gate"].tensor, (l * 2 + d) * 128, [[1, 128], [1, 1]]))
            S.ts(negb, negb, -1.0, None, ALU.mult)
            S.memset(Sbd, 0.0); S.memset(Sbdb, 0.0)
            gorder = list(range(9)) if d == 0 else [0] + list(range(8, 0, -1))
            nck = 0
            for gix, gi in enumerate(gorder):
                tots = tots2[gix % 2]; qtb = qtb2[gix % 2]; ktb = ktb2[gix % 2]; khb = khb2[gix % 2]
                tok0, n = GROUPS[gi]
                ncg = n // 128
                qr = qr2[gix % 2]; kr = kr2[gix % 2]
                S.dma(qr[:, 0:n], PF[0:128, tok0:tok0 + n], q="sp")
                S.dma(kr[:, 0:n], PF[256:384, tok0:tok0 + n], q="sp")
                S.dma(glr[:, 0:n], PF[512 + 16 * d: 528 + 16 * d, tok0:tok0 + n], q="sp")
                S.mm(PS[0][:, 0:n], wg, glr[:, 0:n])
                S.act(e1[:, 0:n], PS[0][:, 0:n], AF.Exp, bias=negb, scale=-1.0)
                S.act(sp_[:, 0:n], e1[:, 0:n], AF.Ln, bias=1.0)
                for c in range(ncg):
                    sl = slice(c * 128, (c + 1) * 128)
                    S.scan(cs_[:, sl], ones1.to_broadcast([128, 128]), sp_[:, sl], 0.0)
                lastc = cs_[:, 0:n].rearrange("p (c k) -> p c k", k=128)[:, :, 127]
                S.ts(tots[:, 0:ncg, 0], lastc, -1.0 / 16, None, ALU.mult)
                S.ts(tots[:, 0:ncg, 1], lastc, 1.0 / 16, None, ALU.mult)
                S.act(tots[:, 0:ncg, 2], tots[:, 0:ncg, 0], AF.Exp)
                if d == 0:
                    S.act(EQ[:, 0:n], cs_[:, 0:n], AF.Exp, scale=-1.0 / 16)
                    S.act(EK[:, 0:n], cs_[:, 0:n], AF.Exp, scale=1.0 / 16)
                    for c in range(ncg):
                        sl = slice(c * 128, (c + 1) * 128)
                        S.act(EH[:, sl], cs_[:, sl], AF.Exp, scale=1.0 / 16, bias=tots[:, c, 0:1])
                else:
                    S.tt(cb[:, 0:n], cs_[:, 0:n], sp_[:, 0:n], ALU.subtract)
                    S.act(EH[:, 0:n], cb[:, 0:n], AF.Exp, scale=-1.0 / 16)
                    for c in range(ncg):
                        sl = slice(c * 128, (c + 1) * 128)
                        S.act(EQ[:, sl], cb[:, sl], AF.Exp, scale=1.0 / 16, bias=tots[:, c, 0:1])
                        S.act(EK[:, sl], cb[:, sl], AF.Exp, scale=-1.0 / 16, bias=tots[:, c, 1:2])
                S.stt(qtb[:, 0:n], qr[:, 0:n], 32.0 ** -0.5, EQ[:, 0:n], ALU.mult, ALU.mult)
                S.tt(ktb[:, 0:n], kr[:, 0:n], EK[:, 0:n], ALU.mult, eng="pool")
                S.tt(khb[:, 0:n], kr[:, 0:n], EH[:, 0:n], ALU.mult, eng="pool")
                corder = list(range(ncg)) if d == 0 else list(range(ncg - 1, -1, -1))
                for c in corder:
                    sl = slice(c * 128, (c + 1) * 128)
                    tk = tok0 + c * 128
                    b = nck % 2; nck += 1
                    S.dma(vt[b], PT[tk:tk + 128, 0:256], q="sp")
                    S.copy(vb[b], vt[b], eng="act")
                    S.tt(Qbd[b], qtb[:, sl].unsqueeze(1).to_broadcast([128, 4, 128]), hmb, ALU.mult, eng="pool")
                    S.mm(ps_att, ktb[:, sl], Qbd[b].rearrange("p a b -> p (a b)"))
                    S.tt(attm[b], ps_att.rearrange("p (a b) -> p a b", a=4),
                         maskt[d].unsqueeze(1).to_broadcast([128, 4, 128]), ALU.mult)
                    S.mm(ps_po[:, 0:256], qtb[:, sl], Sbdb, start=True, stop=False)
                    for h in range(4):
                        S.mm(ps_po[:, h * 64:(h + 1) * 64], attm[b][:, h, :], vb[b][:, h * 64:(h + 1) * 64],
                             start=False, stop=(h == 3))
                    S.mm(PS[5][:, 0:128], khb[:, sl], identb)
                    S.copy(khT[b], PS[5][:, 0:128], eng="act")
                    S.mm(ps_st[:, 0:256], khT[b], vb[b])
                    S.tt(stmp, ps_st[:, 0:256], bdm, ALU.mult)
                    S.stt(Sbd, Sbd, tots[:, c, 2:3], stmp, ALU.mult, ALU.add)
                    S.copy(Sbdb, Sbd, eng="act")
                    S.copy(osb[b], ps_po[:, 0:256], eng="act")
                    S.dma(OGd[d][tk:tk + 128, :], osb[b], q="act")
        NCB = 3
        cf = [P.t(f"cf{i}", [128, 256]) for i in range(NCB)]
        cbw = [P.t(f"cbw{i}", [128, 256]) for i in range(NCB)]
        csq = [P.t(f"csq{i}", [128, 256]) for i in range(NCB)]
        chs = [P.t(f"chs{i}", [128, 4, 3]) for i in range(NCB)]
        for ti in range(NTILE):
            b = ti % NCB
            tsl = slice(ti * 128, (ti + 1) * 128)
            S.dma(cf[b], OG[tsl, :], q="sp")
            S.dma(cbw[b], OGB[tsl, :], q="sp")
            ob = cf[b]; hst = chs[b]
            S.tt(ob, ob, cbw[b], ALU.add, eng="pool")
            S.act(csq[b], ob, AF.Square)
            S.op("dve", "tensor_reduce", hst[:, :, 0], csq[b].rearrange("p (a b) -> p a b", a=4), AX.X, ALU.add,
                 reads=[csq[b]], writes=[hst])
            S.ts(hst[:, :, 1], hst[:, :, 0], 1.0 / 64, EPS, ALU.mult, ALU.add)
            S.act(hst[:, :, 2], hst[:, :, 1], AF.Sqrt)
            S.recip(hst[:, :, 1], hst[:, :, 2])
            o3 = ob.rearrange("p (a b) -> p a b", a=4)
            S.tt(o3, o3, hst[:, :, 1:2].to_broadcast([128, 4, 64]), ALU.mult)
            S.tt(o3, o3, gnorm, ALU.mult, eng="pool")
            S.dma(YS[tsl, 0:256], ob, q="act")
        ublk = [P.t("ublk0", [128, 8, 256]), P.t("ublk1", [128, 8, 256])]
        ublb = [P.t("ublb0", [128, 16, 128], BF16), P.t("ublb1", [128, 16, 128], BF16)]
        blocks = [(0, 32, 0)]
        cpos = NCH_CTX
        while cpos < NCH:
            nb = min(128, NCH - cpos)
            blocks.append((cpos, nb, NCH_CTX + (NCH - (cpos + nb))))
            cpos += nb
        for bi, (c0, nb, bp) in enumerate(blocks):
            ub = ublk[bi % 2]; ubb = ublb[bi % 2]
            S.dma(ub[0:nb], PT[c0 * 8:(c0 + nb) * 8, 512:768].rearrange("(c s) f -> c s f", s=8), q="sp")
            S.copy(ubb[0:nb].rearrange("c g (s h) -> c g s h", s=8), ub[0:nb].rearrange("c s (g h) -> c g s h", g=16))
            for (UTx, perm, pos0) in ((UTf, identb, c0), (UTb, antib, bp)):
                if perm is identb:
                    pm = identb[0:nb, 0:nb]
                else:
                    pm = antib if nb == 128 else anti32b
                for g4 in range(4):
                    pb = PS[0] if g4 % 2 == 0 else PS[5]
                    for gg in range(4):
                        g = g4 * 4 + gg
                        S.mm(pb[:, gg * 128: gg * 128 + nb], ubb[0:nb, g, :], pm)
                    S.evac(UTx[:, g4 * 4:(g4 + 1) * 4, pos0:pos0 + nb],
                           pb.rearrange("p (a b) -> p a b", a=4)[:, :, 0:nb])
        P.close()

        if stop_after == 'G':
            raise _Stop()
        P0 = Pool(S)
        Toep = P0.t("Toep", [128, 2, 16, 128], BF16)
        WstR = P0.t("WstR", [128, 16, 128], BF16)
        WstI = P0.t("WstI", [128, 16, 128], BF16)
        CdR = P0.t("CdR", [128, 16, 128], BF16)
        CdI = P0.t("CdI", [128, 16, 128], BF16)
        th8 = P0.t("th8", [128, 16]); r8t = P0.t("r8t", [128, 16])
        P = Pool(S)
        lamL = P.t("lamL", [16, 2, 128])
        S.dma(lamL[:, 0, :], W["s5_lam_re"][l].rearrange("d (gp m) p -> (d gp) (m p)", m=2))
        S.dma(lamL[:, 1, :], W["s5_lam_im"][l].rearrange("d (gp m) p -> (d gp) (m p)", m=2))
        i16 = ident[0:16, 0:16]
        lam = P.t("lam", [128, 2, 16])
        for ri in range(2):
            S.mm(PS[0][:, ri * 16:(ri + 1) * 16], lamL[:, ri, :], i16)
        S.evac(lam, PS[0][:, 0:32].rearrange("p (a b) -> p a b", a=2))
        ld = P.t("ld", [2, 2, 8])
        S.dma(ld, W["s5_log_dt"][l].rearrange("d (gp m) -> m d gp", m=2), allow_slow_non_contiguous=True)
        selm = P.t("selm", [2, 128]); S.dma(selm, C["selm"])
        S.mm(PS[1][:, 0:16], selm, ld.rearrange("m d g -> m (d g)"))
        dt_ = P.t("dt", [128, 16]); S.act(dt_, PS[1][:, 0:16], AF.Exp)
        sc = {}
        for nm in ["lrd", "mag", "imag", "ang", "sn", "cs", "lbr", "lbi", "ilr", "ili", "den", "rden",
                   "nr", "t1", "t2", "cor", "coi", "th8", "r8"]:
            sc[nm] = P.t("s5_" + nm, [128, 16])
        S.tt(sc["lrd"], lam[:, 0, :], dt_, ALU.mult)
        S.act(sc["mag"], sc["lrd"], AF.Exp)
        S.act(sc["imag"], sc["lrd"], AF.Exp, scale=-1.0)
        S.act(sc["r8"], sc["lrd"], AF.Exp, scale=8.0)
        S.tt(sc["ang"], lam[:, 1, :], dt_, ALU.mult)
        range_reduce(P, sc["sn"], sc["cs"], sc["ang"], [128, 16])
        S.tt(sc["lbr"], sc["mag"], sc["cs"], ALU.mult)
        S.tt(sc["lbi"], sc["mag"], sc["sn"], ALU.mult)
        S.tt(sc["ilr"], sc["imag"], sc["cs"], ALU.mult)
        S.stt(sc["ili"], sc["imag"], -1.0, sc["sn"], ALU.mult, ALU.mult)
        S.tt(sc["den"], lam[:, 0, :], lam[:, 0, :], ALU.mult)
        S.tt(sc["t1"], lam[:, 1, :], lam[:, 1, :], ALU.mult)
        S.tt(sc["den"], sc["den"], sc["t1"], ALU.add)
        S.recip(sc["rden"], sc["den"])
        S.ts(sc["nr"], sc["lbr"], -1.0, None, ALU.add)
        S.tt(sc["t1"], sc["nr"], lam[:, 0, :], ALU.mult)
        S.tt(sc["t2"], sc["lbi"], lam[:, 1, :], ALU.mult)
        S.tt(sc["t1"], sc["t1"], sc["t2"], ALU.add)
        S.tt(sc["cor"], sc["t1"], sc["rden"], ALU.mult)
        S.tt(sc["t1"], sc["lbi"], lam[:, 0, :], ALU.mult)
        S.tt(sc["t2"], sc["nr"], lam[:, 1, :], ALU.mult)
        S.tt(sc["t1"], sc["t1"], sc["t2"], ALU.subtract)
        S.tt(sc["coi"], sc["t1"], sc["rden"], ALU.mult)
        kq = P.t("s5_kq", [128, 16], I32); kqf = P.t("s5_kqf", [128, 16])
        S.ts(kq, sc["ang"], 8.0 / TWO_PI, None, ALU.mult)
        S.copy(kqf, kq)
        S.ts(sc["t1"], sc["ang"], 8.0, None, ALU.mult)
        S.stt(sc["th8"], kqf, -TWO_PI, sc["t1"], ALU.mult, ALU.add)
        pwr = P.t("pwr", [128, 16, 9]); pwi = P.t("pwi", [128, 16, 9])
        ipr = P.t("ipr", [128, 16, 9]); ipi = P.t("ipi", [128, 16, 9])
        for (ar, ai, br, bi) in ((pwr, pwi, sc["lbr"], sc["lbi"]), (ipr, ipi, sc["ilr"], sc["ili"])):
            S.memset(ar[:, :, 0], 1.0); S.memset(ai[:, :, 0], 0.0)
            for tau in range(1, 9):
                S.tt(sc["t1"], ar[:, :, tau - 1], br, ALU.mult)
                S.tt(sc["t2"], ai[:, :, tau - 1], bi, ALU.mult)
                S.tt(ar[:, :, tau], sc["t1"], sc["t2"], ALU.subtract)
                S.tt(sc["t1"], ar[:, :, tau - 1], bi, ALU.mult)
                S.tt(sc["t2"], ai[:, :, tau - 1], br, ALU.mult)
                S.tt(ai[:, :, tau], sc["t1"], sc["t2"], ALU.add)
        rpr = P.t("rpr", [128, 16, 8]); rpi = P.t("rpi", [128, 16, 8])
        for t_ in range(8):
            S.copy(rpr[:, :, t_], pwr[:, :, 8 - t_]); S.copy(rpi[:, :, t_], pwi[:, :, 8 - t_], eng="pool")
        Br = P.t("Br", [128, 16, 16]); Bi = P.t("Bi", [128, 16, 16])
        for (dst, nm) in ((Br, "s5_b_re"), (Bi, "s5_b_im")):
            S.dma(dst.rearrange("p (d g) h -> p d g h", d=2),
                  W[nm][l].rearrange("d (gp m) p h -> (m p) d gp h", m=2))
        Bbr = P.t("Bbr", [128, 16, 16]); Bbi = P.t("Bbi", [128, 16, 16]); tb = P.t("tb", [128, 16, 16])
        cor_b = sc["cor"].unsqueeze(2).to_broadcast([128, 16, 16])
        coi_b = sc["coi"].unsqueeze(2).to_broadcast([128, 16, 16])
        S.tt(Bbr, Br, cor_b, ALU.mult); S.tt(tb, Bi, coi_b, ALU.mult); S.tt(Bbr, Bbr, tb, ALU.subtract)
        S.tt(Bbi, Bi, cor_b, ALU.mult); S.tt(tb, Br, coi_b, ALU.mult); S.tt(Bbi, Bbi, tb, ALU.add)
        CL = P.t("CL", [16, 2, 32, 64])
        S.dma(CL[:, 0], W["s5_c_re"][l].rearrange("d g h p -> h (d g) p"))
        S.dma(CL[:, 1], W["s5_c_im"][l].rearrange("d g h p -> h (d g) p"))
        Cr = P.t("Cr", [128, 16, 16]); Ci = P.t("Ci", [128, 16, 16])
        for ri, dst in ((0, Cr), (1, Ci)):
            for dd in range(2):
                for gp in range(8):
                    for m in range(2):
                        g = gp * 2 + m
                        S.mm(PS[2 + ri][m * 64:(m + 1) * 64, (dd * 8 + gp) * 16:(dd * 8 + gp + 1) * 16],
                             CL[:, ri, dd * 16 + g, :], i16)
            S.evac(dst, PS[2 + ri][:, 0:256].rearrange("p (a b) -> p a b", a=16))
        Ar = P.t("Ar", [128, 16, 8, 16]); Ai = P.t("Ai", [128, 16, 8, 16])
        Cqr = P.t("Cqr", [128, 16, 8, 16]); Cqi = P.t("Cqi", [128, 16, 8, 16])
        Cdr = P.t("Cdr", [128, 16, 8, 16]); Cdi = P.t("Cdi", [128, 16, 8, 16])
        Wsr = P.t("Wsr", [128, 16, 8, 16]); Wsi = P.t("Wsi", [128, 16, 8, 16])
        t4a = P.t("t4a", [128, 8, 8, 16]); t4b = P.t("t4b", [128, 8, 8, 16])

        def cmul(outr, outi, pr, pi_, xr, xi, neg_im=False):
            S.tt(outr, pr, xr, ALU.mult); S.tt(t4a, pi_, xi, ALU.mult, eng="pool")
            S.tt(outr, outr, t4a, ALU.subtract)
            S.tt(outi, pr, xi, ALU.mult); S.tt(t4b, pi_, xr, ALU.mult, eng="pool")
            S.tt(outi, outi, t4b, ALU.add)
            if neg_im:
                S.ts(outi, outi, -1.0, None, ALU.mult, eng="pool")

        for dd in range(2):
            ds_ = slice(dd * 8, (dd + 1) * 8)

            def pw_b(tr, ti_, lo, step):
                a_ = tr[:, ds_, lo:lo + 8]; b_ = ti_[:, ds_, lo:lo + 8]
                return (a_.unsqueeze(3).to_broadcast([128, 8, 8, 16]), b_.unsqueeze(3).to_broadcast([128, 8, 8, 16]))

            def x_b(xr, xi):
                return (xr[:, ds_, :].unsqueeze(2).to_broadcast([128, 8, 8, 16]),
                        xi[:, ds_, :].unsqueeze(2).to_broadcast([128, 8, 8, 16]))
            bbr_, bbi_ = x_b(Bbr, Bbi)
            cr_, ci_ = x_b(Cr, Ci)
            if dd == 0:
                pa = pw_b(ipr, ipi, 0, 1)
                pc = pw_b(pwr, pwi, 0, 1)
                pd = pw_b(pwr, pwi, 1, 1)
            else:
                pa = pw_b(pwr, pwi, 0, 1)
                pc = pw_b(ipr, ipi, 0, 1)
                pd = pw_b(rpr, rpi, 0, 1)
            cmul(Ar[:, ds_], Ai[:, ds_], pa[0], pa[1], bbr_, bbi_)
            cmul(Cqr[:, ds_], Cqi[:, ds_], pc[0], pc[1], cr_, ci_, neg_im=True)
            cmul(Cdr[:, ds_], Cdi[:, ds_], pd[0], pd[1], cr_, ci_, neg_im=True)
            if dd == 0:
                p7r = pwr[:, ds_, 7:8].unsqueeze(3).to_broadcast([128, 8, 8, 16])
                p7i = pwi[:, ds_, 7:8].unsqueeze(3).to_broadcast([128, 8, 8, 16])
                S.tt(Wsr[:, ds_], Ar[:, ds_], p7r, ALU.mult); S.tt(t4a, Ai[:, ds_], p7i, ALU.mult)
                S.tt(Wsr[:, ds_], Wsr[:, ds_], t4a, ALU.subtract)
                S.tt(Wsi[:, ds_], Ar[:, ds_], p7i, ALU.mult); S.tt(t4b, Ai[:, ds_], p7r, ALU.mult)
                S.tt(Wsi[:, ds_], Wsi[:, ds_], t4b, ALU.add)
            else:
                S.copy(Wsr[:, ds_], Ar[:, ds_]); S.copy(Wsi[:, ds_], Ai[:, ds_], eng="pool")
        S.copy(th8, sc["th8"]); S.copy(r8t, sc["r8"])
        S.copy(CdR, Cdr.rearrange("p a b c -> p a (b c)"))
        S.copy(CdI, Cdi.rearrange("p a b c -> p a (b c)"), eng="pool")
        tmk = [P.t("tmf", [128, 128]), P.t("tmb", [128, 128])]
        S.dma(tmk[0], C["tmaskf"]); S.dma(tmk[1], C["tmaskb"])
        for dd in range(2):
            for gp in range(8):
                dg = dd * 8 + gp
                for m in range(2):
                    g = gp * 2 + m
                    ms = slice(m * 64, (m + 1) * 64)
                    pb = PS[(g % 2)]
                    S.mm(pb[:, 0:128], Ar[ms, dg].rearrange("p a b -> p (a b)"), Cqr[ms, dg].rearrange("p a b -> p (a b)"),
                         start=True, stop=False)
                    S.mm(pb[:, 0:128], Ai[ms, dg].rearrange("p a b -> p (a b)"), Cqi[ms, dg].rearrange("p a b -> p (a b)"),
                         start=False, stop=True)
                    S.tt(Toep[:, dd, g, :], pb[:, 0:128], tmk[dd], ALU.mult)
                pb = PS[2 + (gp % 2)]
                S.mm(pb[:, 0:128], Wsr[:, dg].rearrange("p a b -> p (a b)"), ident)
                S.mm(pb[:, 128:256], Wsi[:, dg].rearrange("p a b -> p (a b)"), ident)
                S.copy(WstR[:, dg, :], pb[:, 0:128], eng="act")
                S.copy(WstI[:, dg, :], pb[:, 128:256], eng="act")
        P.close()
        P = Pool(S)
        iota = P.t("iota", [128, NCH]); S.dma(iota, C["ciota"])
        NSB = 2
        sset = []
        for i_ in range(NSB):
            sset.append(dict(
                angc=P.t("angc", [128, NCH]), snc=P.t("snc", [128, NCH]), csc=P.t("csc", [128, NCH]),
                r8b=P.t("r8b", [128, 1]), xr=P.t("xr", [128, NCH]), xi=P.t("xi", [128, NCH]),
                ta=P.t("ta", [128, NCH]), tb=P.t("tbb", [128, NCH]), zr=P.t("zr", [128, NCH]), zi=P.t("zi", [128, NCH]),
                xpr=P.t("xpr", [128, NCH], BF16), xpi=P.t("xpi", [128, NCH], BF16)))
        yst = [P.t("yst0", [128, 8, 256]), P.t("yst1", [128, 8, 256])]
        YB = P.t("YB", [128, 5, 8, 256])
        HALF = NCH // 2
        it_s = 0
        for dd in range(2):
            UTx = UTf if dd == 0 else UTb
            for gp in range(8):
                dg = dd * 8 + gp
                B_ = sset[it_s % NSB]; slot_ = it_s % NSB; it_s += 1
                angc, snc, csc, r8b = B_["angc"], B_["snc"], B_["csc"], B_["r8b"]
                xr_, xi_, ta, tbb, zr, zi, xpr, xpi = B_["xr"], B_["xi"], B_["ta"], B_["tb"], B_["zr"], B_["zi"], B_["xpr"], B_["xpi"]
                for hf in range(2):
                    cs0 = hf * HALF
                    for m in range(2):
                        g = gp * 2 + m
                        S.mm(PS[hf][m * 64:(m + 1) * 64, 0:HALF], WstR[:, dg, m * 64:(m + 1) * 64], UTx[:, g, cs0:cs0 + HALF])
                        S.mm(PS[2 + hf][m * 64:(m + 1) * 64, 0:HALF], WstI[:, dg, m * 64:(m + 1) * 64], UTx[:, g, cs0:cs0 + HALF])
                S.act(angc, iota, AF.Copy, scale=th8[:, dg:dg + 1])
                range_reduce(P, snc, csc, angc, [128, NCH], slot=slot_)
                for hf in range(2):
                    sl = slice(hf * HALF, (hf + 1) * HALF)
                    S.tt(xr_[:, sl], PS[hf][:, 0:HALF], csc[:, sl], ALU.mult)
                    S.tt(ta[:, sl], PS[2 + hf][:, 0:HALF], snc[:, sl], ALU.mult)
                    S.tt(xi_[:, sl], PS[2 + hf][:, 0:HALF], csc[:, sl], ALU.mult)
                    S.tt(tbb[:, sl], PS[hf][:, 0:HALF], snc[:, sl], ALU.mult)
                S.tt(xr_, xr_, ta, ALU.add, eng="pool")
                S.tt(xi_, xi_, tbb, ALU.subtract, eng="pool")
                S.copy(r8b, r8t[:, dg:dg + 1], eng="act")
                S.scan(zr, r8b.to_broadcast([128, NCH]), xr_, 0.0)
                S.scan(zi, r8b.to_broadcast([128, NCH]), xi_, 0.0)
                S.tt(ta, zr, csc, ALU.mult); S.tt(tbb, zi, snc, ALU.mult, eng="pool")
                S.memset(xpr[:, 0:1], 0.0, eng="pool"); S.memset(xpi[:, 0:1], 0.0, eng="pool")
                S.tt(xpr[:, 1:NCH], ta[:, 0:NCH - 1], tbb[:, 0:NCH - 1], ALU.subtract)
                S.tt(ta, zr, snc, ALU.mult, eng="pool"); S.tt(tbb, zi, csc, ALU.mult, eng="pool")
                S.tt(xpi[:, 1:NCH], ta[:, 0:NCH - 1], tbb[:, 0:NCH - 1], ALU.add)
                for m in range(2):
                    g = gp * 2 + m
                    ms = slice(m * 64, (m + 1) * 64)
                    for bi, (c0, nb, bp) in enumerate(blocks):
                        pos0 = c0 if dd == 0 else bp
                        pb = PS[4 + ((bi + m) % 4)]
                        S.mm(pb[0:nb, 0:128], UTx[:, g, pos0:pos0 + nb], Toep[:, dd, g, :], start=True, stop=False)
                        S.mm(pb[0:nb, 0:128], xpr[ms, pos0:pos0 + nb], CdR[ms, dg, :], start=False, stop=False)
                        S.mm(pb[0:nb, 0:128], xpi[ms, pos0:pos0 + nb], CdI[ms, dg, :], start=False, stop=True)
                        S.copy(YB[0:nb, bi, :, g * 16:(g + 1) * 16], pb[0:nb, 0:128].rearrange("c (t h) -> c t h", t=8), eng="act")
            for bi, (c0, nb, bp) in enumerate(blocks):
                if dd == 0:
                    S.dma(YSF[c0 * 8:(c0 + nb) * 8, :].rearrange("(c t) f -> c t f", t=8), YB[0:nb, bi], q="act")
                else:
                    clast = (NCH_CTX - 1 - bp) if bi == 0 else (NCH - 1 - (bp - NCH_CTX))
                    cfirst = clast - nb + 1
                    stg = yst[bi % 2]
                    aF = anti if nb == 128 else anti32
                    for q4 in range(4):
                        pbq = PS[4 + q4]
                        S.mm(pbq[0:nb, :], aF, YB[0:nb, bi].rearrange("c t f -> c (t f)")[:, q4 * 512:(q4 + 1) * 512])
                        S.evac(stg[0:nb].rearrange("c t f -> c (t f)")[:, q4 * 512:(q4 + 1) * 512], pbq[0:nb, :])
                    S.dma(YSB[cfirst * 8:(cfirst + nb) * 8, :].rearrange("(c t) f -> c t f", t=8), stg[0:nb], q="act")
        P.close()
        P0.close()
        PU.close()

        if stop_after == 'S':
            raise _Stop()
        if stop_after == 'P':
            raise _Stop()
        P = Pool(S)
        Wo = P.t("Wo", [128, 8, 1024], BF16)
        wos = [P.t("wos0", [128, 1024]), P.t("wos1", [128, 1024])]
        for k in range(8):
            S.dma(wos[k % 2], W["w_out"][l, k * 128:(k + 1) * 128, :], q=("sp" if k % 2 == 0 else "pool"))
            S.copy(Wo[:, k, :], wos[k % 2], eng=("dve" if k % 2 == 0 else "act"))
        wgl = P.t("wgl", [128, 2, 256]); wglb = P.t("wglb", [128, 2, 256], BF16)
        S.dma(wgl, W["s5_w_glu"][l].rearrange("(k p) n -> p k n", p=128))
        S.copy(wglb, wgl)
        bgl = P.t("bgl", [128, 256]); S.dma(bgl, dram_rows_bcast(W["s5_b_glu"], l * 256, 256))
        dsk = P.t("dsk", [128, 256]); S.dma(dsk, dram_rows_bcast(W["s5_d"], l * 256, 256))
        GG = [P.t("GG0", [128, 1024]), P.t("GG1", [128, 1024])]
        for r_ in range(2):
            S.tt(GG[r_], gpost, MOD[r_][:, 2048:3072], ALU.mult, eng="pool")
        NB = 4
        ys_b = [P.t(f"ys{i}", [128, 1024]) for i in range(NB)]
        gt_b = [P.t(f"gt{i}", [128, 1024]) for i in range(NB)]
        x_b = [P.t(f"xo{i}", [128, 1024]) for i in range(NB)]
        s5a = [P.t(f"s5a{i}", [128, 3, 256]) for i in range(NB)]
        y5_ = [P.t(f"y5{i}", [128, 256]) for i in range(NB)]
        g5_ = [P.t(f"g5{i}", [128, 256]) for i in range(NB)]
        t5_ = [P.t(f"t5{i}", [128, 256]) for i in range(NB)]
        g5b_ = [P.t(f"g5b{i}", [128, 256], BF16) for i in range(NB)]
        g5T_ = [P.t(f"g5T{i}", [128, 2, 128], BF16) for i in range(NB)]
        ybf_ = [P.t(f"ybf{i}", [128, 1024], BF16) for i in range(NB)]
        yT_ = [P.t(f"yT{i}", [128, 8, 128], BF16) for i in range(NB)]
        zt__ = [P.t(f"zt{i}", [128, 1024]) for i in range(NB)]
        junk_ = [P.t(f"junk{i}", [128, 512], BF16) for i in range(NB)]
        st__ = [P.t(f"st{i}", [128, 4]) for i in range(NB)]
        tiles_o = [ti for ti in range(NTILE) if not (last and ti < 2)]

        def o_stage1(n):
            ti = tiles_o[n]; b = n % NB
            tsl = slice(ti * 128, (ti + 1) * 128)
            y5, g5, t5, g5b = y5_[b], g5_[b], t5_[b], g5b_[b]
            S.dma(ys_b[b][:, 0:512], YS[tsl, 0:512], q="sp")
            S.dma(ys_b[b][:, 768:1024], YS[tsl, 768:1024], q="sp")
            S.dma(gt_b[b], PT[tsl, 768:1792], q="sp")
            S.dma(x_b[b], x_src[tsl, :], q="sp")
            S.dma(s5a[b][:, 0, :], YSF[tsl, :], q="sp")
            S.dma(s5a[b][:, 1, :], YSB[tsl, :], q="sp")
            S.dma(s5a[b][:, 2, :], PT[tsl, 512:768], q="sp")
            S.tt(s5a[b][:, 0, :], s5a[b][:, 0, :], s5a[b][:, 1, :], ALU.add, eng="pool")
            S.tt(y5, s5a[b][:, 2, :], dsk, ALU.mult)
            S.tt(y5, y5, s5a[b][:, 0, :], ALU.add)
            S.tt(t5, y5, y5, ALU.mult, eng="pool")
            S.ts(t5, t5, 0.044715, 1.0, ALU.mult, ALU.add, eng="pool")
            S.tt(t5, t5, y5, ALU.mult, eng="pool")
            S.act(t5, t5, AF.Sigmoid, scale=2.0 * float(np.sqrt(2.0 / np.pi)))
            S.tt(g5, y5, t5, ALU.mult)
            S.copy(g5b, g5, eng="pool")
            pg = PS[n % 2]
            for k in range(2):
                S.mm(pg[:, k * 128:(k + 1) * 128], g5b[:, k * 128:(k + 1) * 128], identb)

        def o_stage2(n):
            b = n % NB
            S.copy(g5T_[b], PS[n % 2][:, 0:256].rearrange("p (a b) -> p a b", a=2), eng="act")
            pz = PS[2 + n % 2]
            for k in range(2):
                S.mm(pz[:, 0:256], g5T_[b][:, k, :], wglb[:, k, :], start=(k == 0), stop=(k == 1))

        def o_stage3(n):
            b = n % NB
            t5, g5 = t5_[b], g5_[b]
            pz = PS[2 + n % 2]
            S.tt(t5, pz[:, 0:256], bgl, ALU.add)
            S.act(t5, t5, AF.Sigmoid)
            S.tt(ys_b[b][:, 512:768], g5, t5, ALU.mult)
            S.tt(ybf_[b], ys_b[b], gt_b[b], ALU.mult, eng="pool")
            for half in range(2):
                pt_ = PS[4 + half]
                for kk in range(4):
                    k = half * 4 + kk
                    S.mm(pt_[:, kk * 128:(kk + 1) * 128], ybf_[b][:, k * 128:(k + 1) * 128], identb)
                S.copy(yT_[b][:, half * 4:(half + 1) * 4, :], pt_.rearrange("p (a b) -> p a b", a=4), eng="act")

        def o_stage4(n):
            ti = tiles_o[n]; b = n % NB
            rsel = 1 if ti < 2 else 0
            tsl = slice(ti * 128, (ti + 1) * 128)
            st_, zt_ = st__[b], zt__[b]
            for nn in range(2):
                po = PS[6 + nn]
                for k in range(8):
                    S.mm(po, yT_[b][:, k, :], Wo[:, k, nn * 512:(nn + 1) * 512], start=(k == 0), stop=(k == 7))
                S.act(junk_[b], po, AF.Square, accum_out=st_[:, nn:nn + 1])
            S.tt(st_[:, 2:3], st_[:, 0:1], st_[:, 1:2], ALU.add)
            S.ts(st_[:, 2:3], st_[:, 2:3], 1.0 / D, EPS, ALU.mult, ALU.add)
            S.act(st_[:, 3:4], st_[:, 2:3], AF.Sqrt)
            S.recip(st_[:, 2:3], st_[:, 3:4])
            for nn in range(2):
                po = PS[6 + nn]
                S.stt(zt_[:, nn * 512:(nn + 1) * 512], po, st_[:, 2:3], GG[rsel][:, nn * 512:(nn + 1) * 512], ALU.mult, ALU.mult)
            S.tt(zt_, zt_, x_b[b], ALU.add, eng="pool")
            if last:
                S.dma(out[(ti - 2) * 128:(ti - 1) * 128, :], zt_, q="act")
            else:
                S.dma(xs[tsl, :], zt_, q="act")

        stages_o = [o_stage1, o_stage2, o_stage3, o_stage4]
        nit = len(tiles_o)
        for step in range(nit + len(stages_o) - 1):
            for si, fn in enumerate(stages_o):
                n = step - si
                if 0 <= n < nit:
                    fn(n)
        P.close()

    except _Stop:
        pass
    S.barrier()
    G.es.close()
    return nc


_CACHE = {}


def kernel(**inputs):
    consts = make_consts()
    B = inputs["x"].shape[0]
    if "nc" not in _CACHE:
        _CACHE["nc"] = build(debug=False)
    nc = _CACHE["nc"]
    in_maps = []
    ncores = 8
    for core in range(ncores):
        b = core % B
        m = {}
        m["xin"] = np.ascontiguousarray(np.concatenate([inputs["ctx"][b], inputs["x"][b]], axis=0).astype(np.float32))
        ccv = np.stack([inputs["c"][b], inputs["c_ctx"]], axis=-1).astype(np.float32)
        m["cc"] = np.ascontiguousarray(ccv.reshape(8, 128, 2).transpose(1, 0, 2))
        for n in WEIGHT_NAMES:
            m[n] = np.ascontiguousarray(np.asarray(inputs[n], dtype=np.float32))
        for n, v in consts.items():
            m["k_" + n] = v
        in_maps.append(m)
    res = run_bass_kernel_spmd(nc, in_maps, core_ids=list(range(ncores)))
    outs = [res.results[b]["out"] for b in range(B)]
    return np.stack(outs, axis=0).astype(np.float32)
```

```python
import numpy as np
import ml_dtypes
from contextlib import ExitStack
import concourse.bass as bass
import concourse.mybir as mybir
from concourse.bass_utils import run_bass_kernel_spmd

F32 = mybir.dt.float32
BF16 = mybir.dt.bfloat16
I32 = mybir.dt.int32
ALU = mybir.AluOpType
AF = mybir.ActivationFunctionType
AX = mybir.AxisListType

NCTX = 256
NLAT = 4096
T = NCTX + NLAT
D = 1024
NTILE = T // 128
EPS = 1e-6
PI = float(np.pi)
TWO_PI = float(2 * np.pi)
NCH = T // 8
NCH_CTX = NCTX // 8
import os
NOSCHED = bool(os.environ.get('MK_NOSCHED'))


class Sched:
    NDMA = 12

    def __init__(self, nc):
        self.nc = nc
        self.eng = {"pe": nc.tensor, "dve": nc.vector, "act": nc.scalar,
                    "pool": nc.gpsimd, "sp": nc.sync}
        self.sem = {k: nc.alloc_semaphore("sem_" + k) for k in self.eng}
        self.cnt = {k: 0 for k in self.eng}
        self.seen = {k: {} for k in self.eng}
        self.dq = {}
        for q in ("sp", "pool", "act"):
            self.dq[q] = {"sems": [nc.alloc_semaphore(f"dq_{q}_{i}") for i in range(self.NDMA)], "n": 0}
        self.lastw = {}
        self.readers = {}
        self.ntens = 0
        self.rr = 0
        self.pending = []

    def _wait(self, eng, ev):
        if ev is None:
            return
        if ev[0] == "e":
            _, src, val = ev
            if src == "pe" and eng == "pe":
                return
            key = ("e", src)
            sem = self.sem[src]
        else:
            _, q, slot, val = ev
            key = ("d", q, slot)
            sem = self.dq[q]["sems"][slot]
        if self.seen[eng].get(key, 0) >= val:
            return
        self.seen[eng][key] = val
        self.eng[eng].wait_ge(sem, val)

    def _deps(self, eng, reads, writes):
        for t in reads:
            self._wait(eng, self.lastw.get(t))
        for t in writes:
            self._wait(eng, self.lastw.get(t))
            for ev in self.readers.get(t, {}).values():
                self._wait(eng, ev)

    def _commit(self, ev, reads, writes):
        for t in reads:
            d = self.readers.setdefault(t, {})
            d[ev[0:2] if ev[0] == "e" else ev[0:3]] = ev
        for t in writes:
            self.lastw[t] = ev
            self.readers[t] = {}

    WHOLE_DRAM = ("RP",)

    @classmethod
    def _names(cls, aps):
        out = []
        for a in aps:
            if a is None or isinstance(a, (int, float)):
                continue
            nm = a.tensor.name
            if "DRam" in type(a.tensor).__name__ and nm not in cls.WHOLE_DRAM:
                nm = f"{nm}@{a.offset}:{tuple(map(tuple, a.ap))}"
            out.append(nm)
        return out

    LAT = float(os.environ.get('MK_LAT', '0.35'))

    def op(self, eng, method, *args, reads=(), writes=(), **kw):
        r = self._names(reads)
        w = self._names(writes)
        cost = self._cost(eng, method, args, kw)
        self.pending.append(("op", eng, method, args, kw, r, w, cost))

    def dma(self, out, in_, q=None, **kw):
        if q is None:
            q = "sp"
        r = self._names([in_])
        w = self._names([out])
        nbytes = 1
        for d_ in out.shape:
            nbytes *= d_
        nbytes *= 4
        self.pending.append(("dma", q, None, (out, in_), kw, r, w, 0.15, 2.0 + nbytes / 150e3))

    @staticmethod
    def _free(ap):
        n = 1
        for d_ in ap.shape[1:]:
            n *= d_
        return n

    def _cost(self, eng, method, args, kw):
        try:
            if eng == "pe":
                rhs = args[2]
                n = max(64, self._free(rhs))
                c = n / 2400.0
                if rhs.dtype == F32:
                    c *= 4
                return c + 0.07
            n = self._free(args[0])
            c = n / 960.0 * (1.5 if eng == "dve" else 1.3) + float(os.environ.get('MK_OVH', '0.1'))
            if eng == "pool":
                c = n / 400.0 + 0.2
            if method == "tensor_tensor_scan":
                c = 2 * n / 960.0 + 0.1
            return c
        except Exception:
            return 0.3

    def flush(self):
        ops = self.pending
        self.pending = []
        n = len(ops)
        if n == 0:
            return
        lastw = {}
        readers = {}
        preds = [None] * n
        for i, o in enumerate(ops):
            ps = set()
            for t in o[5]:
                j = lastw.get(t)
                if j is not None:
                    ps.add(j)
            for t in o[6]:
                j = lastw.get(t)
                if j is not None:
                    ps.add(j)
                ps.update(readers.get(t, ()))
            for t in o[5]:
                readers.setdefault(t, []).append(i)
            for t in o[6]:
                lastw[t] = i
                readers[t] = []
            ps.discard(i)
            preds[i] = ps
        succs = [[] for _ in range(n)]
        indeg = [0] * n
        for i in range(n):
            indeg[i] = len(preds[i])
            for p in preds[i]:
                succs[p].append(i)
        fin = [0.0] * n
        rdy = [0.0] * n
        crit = [-1] * n
        epred = [-1] * n
        elast = {e: -1 for e in self.eng}
        stt_ = [0.0] * n
        eng_free = {e: 0.0 for e in self.eng}
        ready = {e: [] for e in self.eng}
        import heapq
        for i in range(n):
            if indeg[i] == 0:
                heapq.heappush(ready[ops[i][1]], i)
        order = []
        remaining = n
        WINDOW = int(os.environ.get('MK_WINDOW', '24'))
        if NOSCHED:
            order = list(range(n))
            remaining = 0
        while remaining:
            best = None
            for e, lst in ready.items():
                if not lst:
                    continue
                cand = heapq.nsmallest(WINDOW, lst)
                for i in cand:
                    st = max(eng_free[e], rdy[i])
                    key = (st, i)
                    if best is None or key < best[0]:
                        best = (key, e, i)
            (st, i), e, _ = best
            ready[e].remove(i)
            heapq.heapify(ready[e])
            o = ops[i]
            dur = o[7]
            epred[i] = elast[e] if eng_free[e] > rdy[i] else -2
            elast[e] = i
            stt_[i] = st
            eng_free[e] = st + dur
            fin[i] = st + (o[8] if o[0] == "dma" else dur)
            order.append(i)
            remaining -= 1
            for s_ in succs[i]:
                if fin[i] + self.LAT > rdy[s_]:
                    rdy[s_] = fin[i] + self.LAT
                    crit[s_] = i
                indeg[s_] -= 1
                if indeg[s_] == 0:
                    heapq.heappush(ready[ops[s_][1]], s_)
        if os.environ.get('MK_CRIT') and n > 2000:
            i = max(range(n), key=lambda k: fin[k])
            chain = []
            while i >= 0:
                chain.append(i)
                i = epred[i] if epred[i] >= 0 else crit[i]
            agg = {}
            for k in chain:
                o = ops[k]
                key = (o[1], o[2] or 'dma', 'engwait' if epred[k] >= 0 else 'data')
                a = agg.setdefault(key, [0, 0.0]); a[0] += 1; a[1] += o[7]
            print('CRIT n=%d len=%d makespan=%.1f' % (n, len(chain), max(fin)))
            for key, a in sorted(agg.items(), key=lambda kv: -kv[1][1])[:14]:
                print('   ', key, a[0], round(a[1], 1))
        if os.environ.get('MK_VERBOSE'):
            busy = {}
            for o in ops:
                busy[o[1]] = busy.get(o[1], 0.0) + o[7]
            print('FLUSH n=%d makespan_us=%.1f busy=%s' % (n, max(fin) if not NOSCHED else -1, {k: round(v) for k, v in busy.items()}), flush=True)
        for i in order:
            o = ops[i]
            if o[0] == "op":
                self._emit_op(o[1], o[2], o[3], o[4], o[5], o[6])
            else:
                self._emit_dma(o[1], o[3][0], o[3][1], o[4], o[5], o[6])

    def _emit_op(self, eng, method, args, kw, r, w):
        self._deps(eng, r, w)
        ins = getattr(self.eng[eng], method)(*args, **kw)
        self.cnt[eng] += 1
        ins.then_inc(self.sem[eng], 1)
        self._commit(("e", eng, self.cnt[eng]), r, w)
        return ins

    def _emit_dma(self, q, out, in_, kw, r, w):
        d = self.dq[q]
        n = d["n"]
        slot = n % self.NDMA
        val = 16 * (n // self.NDMA + 1)
        if n >= self.NDMA:
            self._wait(q, ("d", q, slot, val - 16))
        self._deps(q, r, w)
        ins = self.eng[q].dma_start(out=out, in_=in_, **kw)
        ins.then_inc(d["sems"][slot], 16)
        d["n"] = n + 1
        self._commit(("d", q, slot, val), r, w)
        return ins

    def mm(self, out, lhsT, rhs, start=True, stop=True, **kw):
        return self.op("pe", "matmul", out, lhsT, rhs, start=start, stop=stop,
                       reads=[lhsT, rhs], writes=[out], **kw)

    def act(self, out, in_, func, bias=None, scale=None, accum_out=None):
        kw = {}
        rd = [in_]
        if bias is not None:
            kw["bias"] = bias
            rd.append(bias)
        if scale is not None:
            kw["scale"] = scale
            rd.append(scale)
        wr = [out]
        if accum_out is not None:
            kw["accum_out"] = accum_out
            wr.append(accum_out)
        return self.op("act", "activation", out, in_, func, reads=rd, writes=wr, **kw)

    def tt(self, out, in0, in1, op, eng="dve"):
        return self.op(eng, "tensor_tensor", out, in0, in1, op, reads=[in0, in1], writes=[out])

    def ts(self, out, in0, s1, s2, op0, op1=None, eng="dve"):
        kw = {}
        if op1 is not None:
            kw["op1"] = op1
        return self.op(eng, "tensor_scalar", out, in0, s1, s2, op0, reads=[in0, s1, s2], writes=[out], **kw)

    def stt(self, out, in0, scalar, in1, op0, op1, eng="dve"):
        return self.op(eng, "scalar_tensor_tensor", out, in0, scalar, in1, op0, op1,
                       reads=[in0, scalar, in1], writes=[out])

    def copy(self, out, in_, eng="dve"):
        if eng == "act":
            return self.act(out, in_, AF.Copy)
        return self.op(eng, "tensor_copy", out, in_, reads=[in_], writes=[out])

    def evac(self, out, in_):
        self.rr += 1
        return self.copy(out, in_, eng=("dve" if self.rr % 2 else "act"))

    def memset(self, ap, val, eng="dve"):
        return self.op(eng, "memset", ap, val, reads=[], writes=[ap])

    def scan(self, out, d0, d1, initial, op0=ALU.mult, op1=ALU.add):
        return self.op("dve", "tensor_tensor_scan", out, d0, d1, initial, op0, op1,
                       reads=[d0, d1, initial], writes=[out])

    def recip(self, out, in_):
        return self.op("dve", "reciprocal", out, in_, reads=[in_], writes=[out])

    def barrier(self):
        self.flush()
        for e in self.eng:
            for src in self.eng:
                if self.cnt[src] > 0:
                    self._wait(e, ("e", src, self.cnt[src]))
            for q, d in self.dq.items():
                n = d["n"]
                for slot in range(min(n, self.NDMA)):
                    last_n = ((n - 1 - slot) // self.NDMA) * self.NDMA + slot
                    self._wait(e, ("d", q, slot, 16 * (last_n // self.NDMA + 1)))


class Pool:
    def __init__(self, S):
        self.S = S
        self.es = ExitStack()

    def t(self, name, shape, dtype=F32):
        self.S.ntens += 1
        h = self.es.enter_context(self.S.nc.sbuf_tensor(f"{name}_{self.S.ntens}", list(shape), dtype))
        return h.ap() if hasattr(h, "ap") else h

    def close(self):
        self.S.barrier()
        self.es.close()


def dram_rows_bcast(t_ap, offset, n, parts=128):
    return bass.AP(t_ap.tensor, offset, [[0, parts], [1, n]])


def make_consts():
    c = {}
    c["ident"] = np.eye(128, dtype=np.float32)
    c["anti"] = np.eye(128, dtype=np.float32)[::-1].copy()
    c["anti64"] = np.eye(64, dtype=np.float32)[::-1].copy()
    c["anti32"] = np.eye(32, dtype=np.float32)[::-1].copy()
    sel = np.zeros((2, 2, 128), np.float32)
    sel[0, 0] = 1
    sel[1, 1] = 1
    c["sel"] = sel
    selm = np.zeros((2, 128), np.float32)
    selm[0, :64] = 1
    selm[1, 64:] = 1
    c["selm"] = selm
    j = np.arange(128)
    mf = (j[:, None] <= j[None, :]).astype(np.float32)
    c["maskf"] = mf
    c["maskb"] = mf.T.copy()
    tok = np.arange(NLAT)
    rowp = (tok // 64).astype(np.float32)
    colp = (tok % 64).astype(np.float32)
    pos = np.zeros((128, T), np.float32)
    freq = np.zeros((128, 1), np.float32)
    half = 16
    fr = 10000.0 ** (-np.arange(0, half, 2, dtype=np.float32) / half)
    for p in range(128):
        d = p % 32
        pos[p, NCTX:] = rowp if d < 16 else colp
        freq[p, 0] = fr[d % 8]
    c["pos"] = pos
    c["freq"] = freq
    hm = np.zeros((128, 4, 128), np.float32)
    bdm = np.zeros((128, 4, 64), np.float32)
    for h in range(4):
        hm[h * 32:(h + 1) * 32, h, :] = 1
        bdm[h * 32:(h + 1) * 32, h, :] = 1
    c["hm"] = hm
    c["bdm"] = bdm.reshape(128, 256)
    col = np.arange(64)
    cs = np.clip(col - 8, 0, 48)
    ok = (col[:, None] >= cs[None, :]) & (col[:, None] < cs[None, :] + 16)
    cm = np.where(ok, 0.0, -30000.0).astype(np.float32)
    c["colmask"] = np.concatenate([cm, cm], 0)
    c["negblk"] = np.full((128, 64), -30000.0, np.float32)
    s_idx = np.repeat(np.arange(8), 16)
    c["tmaskf"] = (s_idx[:, None] <= s_idx[None, :]).astype(np.float32)
    c["tmaskb"] = (s_idx[:, None] >= s_idx[None, :]).astype(np.float32)
    c["ciota"] = np.tile(np.arange(NCH, dtype=np.float32)[None, :], (128, 1))
    rc = np.zeros((4, PT_PAD), np.float32)
    for i, w in enumerate((2, 4, 8, 16)):
        for (n, off) in ((NCTX, PAD_CTX), (NLAT, PAD_LAT)):
            t = np.arange(n)
            lo = np.clip(t - w // 2, 0, n)
            hi = np.clip(t - w // 2 + w, 0, n)
            rc[i, off:off + n] = 1.0 / (hi - lo)
    c["poolrc"] = rc
    return c


PAD_CTX = 16
PAD_LAT = 16 + NCTX + 32
PT_PAD = PAD_LAT + NLAT + 16

CONST_SHAPES = None

WEIGHT_NAMES = ["w_mod", "b_mod", "g_pre", "g_post", "w_in", "w_out", "gla_w_gate", "gla_b_gate",
                "gla_g_norm", "na_rpb", "s5_lam_re", "s5_lam_im", "s5_log_dt", "s5_b_re", "s5_b_im",
                "s5_c_re", "s5_c_im", "s5_d", "s5_w_glu", "s5_b_glu", "pool_w", "pool_scale"]
WEIGHT_SHAPES = {
    "w_mod": (2, 1024, 3072), "b_mod": (2, 3072), "g_pre": (2, 1024), "g_post": (2, 1024),
    "w_in": (2, 1024, 2848), "w_out": (2, 1024, 1024), "gla_w_gate": (2, 2, 16, 128),
    "gla_b_gate": (2, 2, 128), "gla_g_norm": (2, 64), "na_rpb": (2, 4, 15, 31),
    "s5_lam_re": (2, 2, 16, 64), "s5_lam_im": (2, 2, 16, 64), "s5_log_dt": (2, 2, 16),
    "s5_b_re": (2, 2, 16, 64, 16), "s5_b_im": (2, 2, 16, 64, 16), "s5_c_re": (2, 2, 16, 16, 64),
    "s5_c_im": (2, 2, 16, 16, 64), "s5_d": (2, 256), "s5_w_glu": (2, 256, 256), "s5_b_glu": (2, 256),
    "pool_w": (2, 4, 64, 64), "pool_scale": (2, 256),
}

FM_BLOCKS = [
    (1184, 128, 0), (2848, 128, 128), (0, 128, 256), (2976, 128, 384), (384, 32, 512),
    (1312, 128, 544), (1440, 128, 672), (416, 128, 800), (544, 128, 928), (1568, 128, 1056), (1696, 128, 1184)]
PF_ROWS = 1312
TM_CHUNKS = [
    (128, 256, 0, False), (672, 256, 256, False), (928, 256, 512, False),
    (1824, 512, 768, True), (2336, 512, 1280, True)]
PT_COLS = 1792
NWCOL = 3104
GROUPS = [(0, 256)] + [(256 + 512 * i, 512) for i in range(8)]


class _Stop(Exception):
    pass


def build(debug=False, nlayers=2, stop_after=None):
    nc = bass.Bass("TRN2", target_bir_lowering=False)
    S = Sched(nc)

    def dram(name, shape, dtype=F32, kind="Internal"):
        return nc.dram_tensor(name, list(shape), dtype, kind=kind).ap()

    dbg_kind = "ExternalOutput" if debug else "Internal"
    xin = dram("xin", [T, D], kind="ExternalInput")
    cc = dram("cc", [128, 8, 2], kind="ExternalInput")
    W = {n: dram(n, WEIGHT_SHAPES[n], kind="ExternalInput") for n in WEIGHT_NAMES}
    consts = make_consts()
    C = {n: dram("k_" + n, v.shape, kind="ExternalInput") for n, v in consts.items()}
    out = dram("out", [NLAT, D], kind="ExternalOutput")
    xs = dram("xs", [T, D], kind=dbg_kind)
    PF = dram("PF", [PF_ROWS, T], kind=dbg_kind)
    PT = dram("PT", [T, PT_COLS], kind=dbg_kind)
    YS = dram("YS", [T, 1024], kind=dbg_kind)
    OG = dram("OG", [T, 256])
    OGB = dram("OGB", [T, 256])
    YSF = dram("YSF", [T, 256], kind=dbg_kind)
    YSB = dram("YSB", [T, 256], kind=dbg_kind)
    COS = dram("COS", [128, T])
    SIN = dram("SIN", [128, T])
    RP = dram("RP", [60, 160])

    PS = []
    for i in range(8):
        h = nc.alloc_psum_tensor(f"psum{i}", [128, 512], F32)
        PS.append(h.ap() if hasattr(h, "ap") else h)

    G = Pool(S)
    ident = G.t("ident", [128, 128]); S.dma(ident, C["ident"])
    identb = G.t("identb", [128, 128], BF16); S.copy(identb, ident)
    anti = G.t("anti", [128, 128]); S.dma(anti, C["anti"])
    antib = G.t("antib", [128, 128], BF16); S.copy(antib, anti)
    anti32 = G.t("anti32", [32, 32]); S.dma(anti32, C["anti32"])
    anti32b = G.t("anti32b", [32, 32], BF16); S.copy(anti32b, anti32)
    anti64 = G.t("anti64", [64, 64]); S.dma(anti64, C["anti64"])
    ones1 = G.t("ones1", [128, 1]); S.memset(ones1, 1.0)
    MOD = [G.t("mod0", [128, 3072]), G.t("mod1", [128, 3072])]
    gpost = G.t("gpost", [128, 1024])

    rr_cache = {}

    def range_reduce(P, out_s, out_c, ang, shape, slot=0):
        key = (id(P), tuple(shape), slot)
        if key not in rr_cache:
            rr_cache[key] = (P.t("rr_ki", shape, I32), P.t("rr_kf", shape), P.t("rr_ph", shape))
        ki, kf, ph = rr_cache[key]
        S.ts(ki, ang, 1.0 / TWO_PI, None, ALU.mult)
        S.copy(kf, ki)
        S.stt(ph, kf, -TWO_PI, ang, ALU.mult, ALU.add)
        S.ts(ph, ph, -PI, PI, ALU.max, ALU.min)
        S.act(out_s, ph, AF.Sin)
        S.act(kf, ph, AF.Sin, scale=0.5)
        S.act(kf, kf, AF.Square)
        S.act(out_c, kf, AF.Identity, bias=1.0, scale=-2.0)

    P = Pool(S)
    freq = P.t("freq", [128, 1]); S.dma(freq, C["freq"])
    for (t0, n) in [(0, 1088), (1088, 1088), (2176, 1088), (3264, 1088)]:
        pos = P.t("pos", [128, n]); S.dma(pos, C["pos"][:, t0:t0 + n])
        ang = P.t("ang", [128, n])
        S.ts(ang, pos, freq, None, ALU.mult)
        sn = P.t("sn", [128, n]); cs_ = P.t("cs", [128, n])
        range_reduce(P, sn, cs_, ang, [128, n])
        S.dma(SIN[:, t0:t0 + n], sn)
        S.dma(COS[:, t0:t0 + n], cs_)
    P.close()

    try:
      for l in range(nlayers):
        x_src = xin if l == 0 else xs
        last = (l == nlayers - 1) and not debug

        P = Pool(S)
        cst = P.t("cst", [128, 8, 2]); S.dma(cst, cc)
        css = P.t("css", [128, 8, 2]); S.act(css, cst, AF.Silu)
        wmb = [P.t("wm0", [128, 3072]), P.t("wm1", [128, 3072])]
        for k in range(8):
            wm = wmb[k % 2]
            S.dma(wm, W["w_mod"][l, k * 128:(k + 1) * 128, :], q=("sp" if k % 2 == 0 else "pool"))
            for n in range(6):
                S.mm(PS[n][0:2, :], css[:, k, :], wm[:, n * 512:(n + 1) * 512], start=(k == 0), stop=(k == 7))
        selt = P.t("selt", [2, 2, 128]); S.dma(selt, C["sel"])
        bm2 = [P.t("bm0", [2, 512]), P.t("bm1", [2, 512])]
        modr2 = [P.t("modr0", [2, 512]), P.t("modr1", [2, 512])]
        for n in range(6):
            bm = bm2[n % 2]; modr = modr2[n % 2]
            S.dma(bm, bass.AP(W["b_mod"].tensor, l * 3072 + n * 512, [[0, 2], [1, 512]]))
            S.tt(modr, PS[n][0:2, :], bm, ALU.add)
            for r in range(2):
                pb = PS[6 + r]
                S.mm(pb, selt[:, r, :], modr)
                S.evac(MOD[r][:, n * 512:(n + 1) * 512], pb)
        gpre = P.t("gpre", [128, 1024])
        S.dma(gpre, dram_rows_bcast(W["g_pre"], l * 1024, 1024))
        S.dma(gpost, dram_rows_bcast(W["g_post"], l * 1024, 1024))
        for r in range(2):
            S.stt(MOD[r][:, 1024:2048], MOD[r][:, 1024:2048], 1.0, gpre, ALU.add, ALU.mult)
        PM = P
        P = Pool(S)
        Wb = P.t("Wb", [128, 8, NWCOL], BF16)
        wst = [P.t("wst0", [128, 2848]), P.t("wst1", [128, 2848])]
        for k in range(8):
            st = wst[k % 2]
            S.dma(st, W["w_in"][l, k * 128:(k + 1) * 128, :], q=("sp" if k % 2 == 0 else "pool"))
            S.copy(Wb[:, k, 0:1424], st[:, 0:1424], eng="dve")
            S.copy(Wb[:, k, 1424:2848], st[:, 1424:2848], eng="act")
            for (c0, r0) in ((1184, 2848), (0, 2976)):
                sv = st[:, c0:c0 + 128].rearrange("p (a t e) -> p a t e", a=8, t=2, e=8)
                dv = Wb[:, k, r0:r0 + 128].rearrange("p (a t e) -> p a t e", a=8, t=2, e=8)
                S.ts(dv[:, :, 0, :], sv[:, :, 1, :], -1.0, None, ALU.mult, eng="pool")
                S.copy(dv[:, :, 1, :], sv[:, :, 0, :], eng="pool")
        xt_b = [P.t("xt0", [128, 1024]), P.t("xt1", [128, 1024])]
        junk = P.t("junk", [128, 1024])
        h32 = P.t("h32", [128, 1024])
        hb_b = [P.t("hb0", [128, 1024], BF16), P.t("hb1", [128, 1024], BF16)]
        hT_b = [P.t("hT0", [128, 8, 512], BF16), P.t("hT1", [128, 8, 512], BF16)]
        fst_b = [P.t(f"fst{i}", [128, 512]) for i in range(4)]
        ropec = [P.t("ropec0", [128, 512]), P.t("ropec1", [128, 512])]
        ropes = [P.t("ropes0", [128, 512]), P.t("ropes1", [128, 512])]
        tst_b = [P.t("tst0", [128, PT_COLS]), P.t("tst1", [128, PT_COLS])]
        stat = [P.t("stat0", [128, 4]), P.t("stat1", [128, 4])]
        nt = 0
        nf = 0
        for gi, (tok0, n) in enumerate(GROUPS):
            hT = hT_b[gi % 2]
            r = 0 if tok0 >= NCTX else 1
            for ti in range(n // 128):
                xt = xt_b[nt % 2]; hb = hb_b[nt % 2]; sv_ = stat[nt % 2]
                S.dma(xt, x_src[tok0 + ti * 128: tok0 + (ti + 1) * 128, :], q="sp")
                S.act(junk, xt, AF.Square, accum_out=sv_[:, 0:1])
                S.ts(sv_[:, 1:2], sv_[:, 0:1], 1.0 / D, EPS, ALU.mult, ALU.add)
                S.act(sv_[:, 2:3], sv_[:, 1:2], AF.Sqrt)
                S.recip(sv_[:, 3:4], sv_[:, 2:3])
                S.stt(h32, xt, sv_[:, 3:4], MOD[r][:, 1024:2048], ALU.mult, ALU.mult)
                S.tt(hb, h32, MOD[r][:, 0:1024], ALU.add, eng="pool")
                for half in range(2):
                    pt_ = PS[half]
                    for kk in range(4):
                        k = half * 4 + kk
                        S.mm(pt_[:, kk * 128:(kk + 1) * 128], hb[:, k * 128:(k + 1) * 128], identb)
                    S.evac(hT[:, half * 4:(half + 1) * 4, ti * 128:(ti + 1) * 128],
                           pt_.rearrange("p (a b) -> p a b", a=4))
                nt += 1
            rc_ = ropec[gi % 2]; rs_ = ropes[gi % 2]
            S.dma(rc_[:, 0:n], COS[:, tok0:tok0 + n], q="sp")
            S.dma(rs_[:, 0:n], SIN[:, tok0:tok0 + n], q="sp")
            held = None
            for bi, (wc, ncol, prow) in enumerate(FM_BLOCKS):
                pb = PS[2 + (bi % 3)]
                for k in range(8):
                    S.mm(pb[0:ncol, 0:n], Wb[:, k, wc:wc + ncol], hT[:, k, 0:n], start=(k == 0), stop=(k == 7))
                fs = fst_b[nf % 4]; nf += 1
                S.evac(fs[0:ncol, 0:n], pb[0:ncol, 0:n])
                if bi in (0, 2):
                    held = (fs, prow)
                    continue
                if bi in (1, 3):
                    f0, prow0 = held
                    S.tt(f0[:, 0:n], f0[:, 0:n], rc_[:, 0:n], ALU.mult, eng="pool")
                    S.tt(fs[:, 0:n], fs[:, 0:n], rs_[:, 0:n], ALU.mult, eng="pool")
                    S.tt(f0[:, 0:n], f0[:, 0:n], fs[:, 0:n], ALU.add)
                    S.dma(PF[prow0:prow0 + 128, tok0:tok0 + n], f0[:, 0:n], q="act")
                    continue
                S.dma(PF[prow:prow + ncol, tok0:tok0 + n], fs[0:ncol, 0:n], q="act")
            for ti in range(n // 128):
                ts_ = tst_b[ti % 2]
                for ci, (wc, ncol, pcol, silu) in enumerate(TM_CHUNKS):
                    pb = PS[5 + (ci % 3)]
                    for k in range(8):
                        S.mm(pb[:, 0:ncol], hT[:, k, ti * 128:(ti + 1) * 128], Wb[:, k, wc:wc + ncol],
                             start=(k == 0), stop=(k == 7))
                    if silu:
                        S.act(ts_[:, pcol:pcol + ncol], pb[:, 0:ncol], AF.Silu)
                    else:
                        S.copy(ts_[:, pcol:pcol + ncol], pb[:, 0:ncol], eng="dve")
                S.dma(PT[tok0 + ti * 128: tok0 + (ti + 1) * 128, :], ts_, q="act")
        P.close()
        PM.close()

        if stop_after == 'A':
            raise _Stop()
        P = Pool(S)
        zt = P.t("zt", [60, 160]); S.memset(zt, 0.0)
        S.dma(RP, zt)
        S.dma(RP[:, 64:95], W["na_rpb"][l].rearrange("h r c -> (h r) c"))
        Gall = P.t("Gall", [64, 60, 2, 64])
        for dup in range(2):
            S.dma(Gall[:, :, dup, :], bass.AP(RP.tensor, 16, [[1, 64], [160, 60], [1, 64]]))
        colm = P.t("colm", [128, 64]); S.dma(colm, C["colmask"])
        a64 = anti64
        BT = P.t("BT", [128, 4, 15, 64])
        for h in range(4):
            for r8 in range(0, 15, 8):
                nr = min(8, 15 - r8)
                pb = PS[(h * 2 + r8 // 8) % 4]
                for j in range(nr):
                    ro = r8 + j
                    S.mm(pb[:, j * 64:(j + 1) * 64], Gall[:, h * 15 + ro].rearrange("p a b -> p (a b)"), a64)
                for j in range(nr):
                    ro = r8 + j
                    S.tt(BT[:, h, 14 - ro, :], pb[:, j * 64:(j + 1) * 64], colm, ALU.add)
        negblk = P.t("negblk", [128, 64]); S.dma(negblk, C["negblk"])
        NCOMP = 40
        comp_tiles = [P.t(f"comp{i}", [128, 128], BF16) for i in range(NCOMP)]
        comp_map = {}

        def get_comp(h, blocks):
            key = (h, blocks)
            if key in comp_map:
                return comp_map[key]
            idx = len(comp_map)
            assert idx < NCOMP
            tl = comp_tiles[idx]
            for (a, b_), (valid, ro) in zip(((0, 0), (0, 1), (1, 0), (1, 1)), blocks):
                dst = tl[a * 64:(a + 1) * 64, b_ * 64:(b_ + 1) * 64]
                if valid:
                    S.copy(dst, BT[a * 64:(a + 1) * 64, h, 14 - ro, :], eng="pool")
                else:
                    S.copy(dst, negblk[a * 64:(a + 1) * 64, :], eng="pool")
            comp_map[key] = tl
            return tl

        KT = P.t("KT", [128, 2, T], BF16)
        QT = P.t("QT", [128, 2, T], BF16)
        Vb = P.t("Vb", [128, NTILE, 4, 65], BF16)
        S.memset(Vb.rearrange("p a b c -> p (a b c)"), 1.0, eng="pool")
        ldq = [P.t("ldq0", [128, 1088]), P.t("ldq1", [128, 1088])]
        nl = 0
        for c2 in range(2):
            for t0 in range(0, T, 1088):
                b = ldq[nl % 2]; nl += 1
                S.dma(b, PF[544 + c2 * 128: 544 + (c2 + 1) * 128, t0:t0 + 1088], q="sp")
                S.ts(QT[:, c2, t0:t0 + 1088], b, 0.125, None, ALU.mult)
                b = ldq[nl % 2]; nl += 1
                S.dma(b, PF[800 + c2 * 128: 800 + (c2 + 1) * 128, t0:t0 + 1088], q="sp")
                S.copy(KT[:, c2, t0:t0 + 1088], b, eng="act")
        ldv = [P.t("ldv0", [128, 256]), P.t("ldv1", [128, 256])]
        for ti in range(NTILE):
            b = ldv[ti % 2]
            S.dma(b, PT[ti * 128:(ti + 1) * 128, 256:512], q="sp")
            S.copy(Vb[:, ti, :, 0:64], b.rearrange("p (a b) -> p a b", a=4), eng=("dve" if ti % 2 == 0 else "act"))
        Pb = [P.t(f"Pb{i}", [128, 7, 128], BF16) for i in range(3)]
        on_ = [P.t("on0", [128, 4, 65]), P.t("on1", [128, 4, 65])]
        yn = [P.t("yn0", [128, 256]), P.t("yn1", [128, 256])]
        rcn = [P.t("rcn0", [128, 4, 1]), P.t("rcn1", [128, 4, 1])]
        kt_cache = {}

        def na_ktiles(qt):
            if qt in kt_cache:
                return kt_cache[qt]
            if qt < 2:
                ktiles = [(0, None), (1, None)]
            else:
                r0 = (qt - 2) * 2
                rows_needed = set()
                for b_ in range(2):
                    stt_ = min(max(r0 + b_ - 4, 0), 56)
                    rows_needed.update(range(stt_, stt_ + 8))
                kts = sorted(set(r // 2 for r in rows_needed))
                ktiles = []
                for kt in kts:
                    blocks = []
                    for a_ in range(2):
                        for b_ in range(2):
                            krow = kt * 2 + a_; qrow = r0 + b_
                            stt_ = min(max(qrow - 4, 0), 56)
                            valid = stt_ <= krow < stt_ + 8
                            blocks.append((valid, krow - qrow + 7))
                    ktiles.append((kt + 2, tuple(blocks)))
                ktiles += [(0, None), (1, None)]
            kt_cache[qt] = ktiles
            return ktiles

        items = [(qt, h) for qt in range(NTILE) for h in range(4)]

        def na_scores(n):
            qt, h = items[n]
            ktiles = na_ktiles(qt); nk = len(ktiles)
            c2 = h // 2; pp = (h % 2) * 64
            psA = PS[(n % 2) * 2]; psB = PS[(n % 2) * 2 + 1]
            pbuf = Pb[n % 3]
            for idx, (kt, blocks) in enumerate(ktiles):
                pdst = (psA if idx < 4 else psB)[:, (idx % 4) * 128:(idx % 4 + 1) * 128]
                S.mm(pdst, KT[pp:pp + 64, c2, kt * 128:(kt + 1) * 128], QT[pp:pp + 64, c2, qt * 128:(qt + 1) * 128],
                     start=True, stop=(blocks is None))
                if blocks is not None:
                    S.mm(pdst, identb, get_comp(h, blocks), start=False, stop=True)
            n1 = min(nk, 4)
            S.act(pbuf[:, 0:n1, :], psA[:, 0:n1 * 128].rearrange("p (a b) -> p a b", a=n1), AF.Exp)
            if nk > 4:
                S.act(pbuf[:, 4:nk, :], psB[:, 0:(nk - 4) * 128].rearrange("p (a b) -> p a b", a=nk - 4), AF.Exp)

        def na_pv(n):
            qt, h = items[n]
            ktiles = na_ktiles(qt); nk = len(ktiles)
            pO = PS[4 + (qt % 2)]
            pbuf = Pb[n % 3]
            for idx, (kt, blocks) in enumerate(ktiles):
                S.mm(pO[:, h * 65:(h + 1) * 65], pbuf[:, idx, :], Vb[:, kt, h, :], start=(idx == 0), stop=(idx == nk - 1))
            if h == 3:
                ob = on_[qt % 2]
                S.evac(ob, pO[:, 0:260].rearrange("p (a b) -> p a b", a=4))
                S.recip(rcn[qt % 2], ob[:, :, 64:65])
                S.tt(yn[qt % 2].rearrange("p (a b) -> p a b", a=4), ob[:, :, 0:64],
                     rcn[qt % 2].to_broadcast([128, 4, 64]), ALU.mult)
                S.dma(YS[qt * 128:(qt + 1) * 128, 256:512], yn[qt % 2], q="act")

        for step in range(len(items) + 1):
            if step < len(items):
                na_scores(step)
            if step >= 1:
                na_pv(step - 1)
        PP = Pool(S)
        wpl = PP.t("wpl", [128, 2, 64]); wplb = PP.t("wplb", [128, 2, 64], BF16)
        S.dma(wpl, W["pool_w"][l].rearrange("(a b) c e -> (b c) a e", b=2))
        S.copy(wplb, wpl)
        pscale = PP.t("pscale", [128, 256]); S.dma(pscale, dram_rows_bcast(W["pool_scale"], l * 256, 256))
        HALO = 32
        SEGW = PAD_LAT + 2048 + HALO
        segs = [(0, SEGW, [(PAD_CTX, 0, NCTX), (PAD_LAT, NCTX, NCTX + 2048 + HALO)], list(range(0, 18))),
                (PAD_LAT + 2048 - HALO, PT_PAD - (PAD_LAT + 2048 - HALO), [(0, NCTX + 2048 - HALO, T)], list(range(18, NTILE)))]
        pU = PP.t("pU", [128, SEGW]); prc = PP.t("prc", [128, SEGW])
        psA_ = PP.t("psA", [128, SEGW]); psB_ = PP.t("psB", [128, SEGW])
        pdb = [PP.t("pdb0", [128, SEGW], BF16), PP.t("pdb1", [128, SEGW], BF16)]
        pst = [PP.t("pst0", [128, 256]), PP.t("pst1", [128, 256])]
        for (seg0, seglen, pieces, tiles_) in segs:
            for tl in range(2):
                U = pU; rc = prc; sA = psA_; sB = psB_
                S.memset(U, 0.0, eng="pool")
                for (loff, c0, c1) in pieces:
                    S.dma(U[:, loff:loff + (c1 - c0)], PF[1056 + tl * 128: 1056 + (tl + 1) * 128, c0:c1], q="sp")
                for hh in range(2):
                    S.dma(rc[hh * 64:(hh + 1) * 64, 0:seglen],
                          dram_rows_bcast(C["poolrc"], (tl * 2 + hh) * PT_PAD + seg0, seglen, parts=64), q="sp")
                S.memset(sA, 0.0, eng="pool"); S.memset(sB, 0.0, eng="pool")
                L0, L1 = 12, seglen - 12
                fins = [None, None]
                for hh in range(2):
                    w = (2, 4, 8, 16)[tl * 2 + hh]
                    ps_ = slice(hh * 64, (hh + 1) * 64)
                    eng = "dve" if hh == 0 else "pool"
                    S.tt(sA[ps_, L0:L1], U[ps_, L0 - 1:L1 - 1], U[ps_, L0:L1], ALU.add, eng=eng)
                    cur, oth = sA, sB
                    sh = 1
                    ww = 2
                    while ww < w:
                        S.tt(oth[ps_, L0:L1], cur[ps_, L0 - sh:L1 - sh], cur[ps_, L0 + sh:L1 + sh], ALU.add, eng=eng)
                        cur, oth = oth, cur
                        sh *= 2
                        ww *= 2
                    S.tt(oth[ps_, 0:seglen], cur[ps_, 0:seglen], rc[ps_, 0:seglen], ALU.mult, eng=eng)
                    S.tt(oth[ps_, 0:seglen], oth[ps_, 0:seglen], U[ps_, 0:seglen], ALU.subtract, eng=eng)
                    fins[hh] = oth
                S.copy(pdb[tl][0:64, 0:seglen], fins[0][0:64, 0:seglen], eng="act")
                S.copy(pdb[tl][64:128, 0:seglen], fins[1][64:128, 0:seglen], eng="act")
            for ti in tiles_:
                goff = (PAD_CTX + ti * 128) if ti < 2 else (PAD_LAT + (ti - 2) * 128)
                off = goff - seg0
                pbs = (PS[6], PS[7])
                for i in range(4):
                    tl, hh = i // 2, i % 2
                    ps_ = slice(hh * 64, (hh + 1) * 64)
                    S.mm(pbs[hh][:, tl * 64:(tl + 1) * 64], pdb[tl][ps_, off:off + 128], wplb[ps_, tl, :])
                for hh in range(2):
                    S.tt(pst[ti % 2].rearrange("p (tl hh e) -> p tl hh e", tl=2, hh=2)[:, :, hh, :],
                         pbs[hh][:, 0:128].rearrange("p (tl e) -> p tl e", tl=2),
                         pscale.rearrange("p (tl hh e) -> p tl hh e", tl=2, hh=2)[:, :, hh, :], ALU.mult)
                S.dma(YS[ti * 128:(ti + 1) * 128, 768:1024], pst[ti % 2], q="act")
        PP.close()
        P.close()

        if stop_after == 'N':
            raise _Stop()
        PU = Pool(S)
        UTf = PU.t("UTf", [128, 16, NCH], BF16)
        UTb = PU.t("UTb", [128, 16, NCH], BF16)
        P = Pool(S)
        hm = P.t("hm", [128, 4, 128]); S.dma(hm, C["hm"])
        hmb = P.t("hmb", [128, 4, 128], BF16); S.copy(hmb, hm)
        bdm = P.t("bdm", [128, 256]); S.dma(bdm, C["bdm"])
        maskt = [P.t("maskf", [128, 128]), P.t("maskb", [128, 128])]
        S.dma(maskt[0], C["maskf"]); S.dma(maskt[1], C["maskb"])
        gnorm = P.t("gnorm", [128, 4, 64])
        for h in range(4):
            S.dma(gnorm[:, h, :], dram_rows_bcast(W["gla_g_norm"], l * 64, 64))
        OGd = [OG, OGB]
        for d in range(2):
            wg = P.t("wg", [16, 128]); negb = P.t("negb", [128, 1])
            Sbd = P.t("Sbd", [128, 256]); Sbdb = P.t("Sbdb", [128, 256], BF16); stmp = P.t("stmp", [128, 256])
            glr = P.t("glr", [16, 512])
            qr2 = [P.t("qr0", [128, 512]), P.t("qr1", [128, 512])]
            kr2 = [P.t("kr0", [128, 512]), P.t("kr1", [128, 512])]
            e1 = P.t("e1", [128, 512]); sp_ = P.t("sp", [128, 512]); cs_ = P.t("cs", [128, 512]); cb = P.t("cb", [128, 512])
            EQ = P.t("EQ", [128, 512]); EK = P.t("EK", [128, 512]); EH = P.t("EH", [128, 512])
            tots2 = [P.t("tots0", [128, 4, 3]), P.t("tots1", [128, 4, 3])]
            qtb2 = [P.t("qtb0", [128, 512], BF16), P.t("qtb1", [128, 512], BF16)]
            ktb2 = [P.t("ktb0", [128, 512], BF16), P.t("ktb1", [128, 512], BF16)]
            khb2 = [P.t("khb0", [128, 512], BF16), P.t("khb1", [128, 512], BF16)]
            Qbd = [P.t("Qbd0", [128, 4, 128], BF16), P.t("Qbd1", [128, 4, 128], BF16)]
            attm = [P.t("attm0", [128, 4, 128], BF16), P.t("attm1", [128, 4, 128], BF16)]
            vt = [P.t("vt0", [128, 256]), P.t("vt1", [128, 256])]
            vb = [P.t("vb0", [128, 256], BF16), P.t("vb1", [128, 256], BF16)]
            khT = [P.t("khT0", [128, 128], BF16), P.t("khT1", [128, 128], BF16)]
            osb = [P.t("osb0", [128, 256]), P.t("osb1", [128, 256])]
            ps_att = PS[1 + d]; ps_po = PS[3 + d]; ps_st = PS[6 + d]
            S.dma(wg, W["gla_w_gate"][l, d])
            S.dma(negb, bass.AP(W["gla_b_gate"].tensor, (l * 2 + d) * 128, [[1, 128], [1, 1]]))
            S.ts(negb, negb, -1.0, None, ALU.mult)
            S.memset(Sbd, 0.0); S.memset(Sbdb, 0.0)
            gorder = list(range(9)) if d == 0 else [0] + list(range(8, 0, -1))
            nck = 0
            for gix, gi in enumerate(gorder):
                tots = tots2[gix % 2]; qtb = qtb2[gix % 2]; ktb = ktb2[gix % 2]; khb = khb2[gix % 2]
                tok0, n = GROUPS[gi]
                ncg = n // 128
                qr = qr2[gix % 2]; kr = kr2[gix % 2]
                S.dma(qr[:, 0:n], PF[0:128, tok0:tok0 + n], q="sp")
                S.dma(kr[:, 0:n], PF[256:384, tok0:tok0 + n], q="sp")
                S.dma(glr[:, 0:n], PF[512 + 16 * d: 528 + 16 * d, tok0:tok0 + n], q="sp")
                S.mm(PS[0][:, 0:n], wg, glr[:, 0:n])
                S.act(e1[:, 0:n], PS[0][:, 0:n], AF.Exp, bias=negb, scale=-1.0)
                S.act(sp_[:, 0:n], e1[:, 0:n], AF.Ln, bias=1.0)
                for c in range(ncg):
                    sl = slice(c * 128, (c + 1) * 128)
                    S.scan(cs_[:, sl], ones1.to_broadcast([128, 128]), sp_[:, sl], 0.0)
                lastc = cs_[:, 0:n].rearrange("p (c k) -> p c k", k=128)[:, :, 127]
                S.ts(tots[:, 0:ncg, 0], lastc, -1.0 / 16, None, ALU.mult)
                S.ts(tots[:, 0:ncg, 1], lastc, 1.0 / 16, None, ALU.mult)
                S.act(tots[:, 0:ncg, 2], tots[:, 0:ncg, 0], AF.Exp)
                if d == 0:
                    S.act(EQ[:, 0:n], cs_[:, 0:n], AF.Exp, scale=-1.0 / 16)
                    S.act(EK[:, 0:n], cs_[:, 0:n], AF.Exp, scale=1.0 / 16)
                    for c in range(ncg):
                        sl = slice(c * 128, (c + 1) * 128)
                        S.act(EH[:, sl], cs_[:, sl], AF.Exp, scale=1.0 / 16, bias=tots[:, c, 0:1])
                else:
                    S.tt(cb[:, 0:n], cs_[:, 0:n], sp_[:, 0:n], ALU.subtract)
                    S.act(EH[:, 0:n], cb[:, 0:n], AF.Exp, scale=-1.0 / 16)
                    for c in range(ncg):
                        sl = slice(c * 128, (c + 1) * 128)
                        S.act(EQ[:, sl], cb[:, sl], AF.Exp, scale=1.0 / 16, bias=tots[:, c, 0:1])
                        S.act(EK[:, sl], cb[:, sl], AF.Exp, scale=-1.0 / 16, bias=tots[:, c, 1:2])
                S.stt(qtb[:, 0:n], qr[:, 0:n], 32.0 ** -0.5, EQ[:, 0:n], ALU.mult, ALU.mult)
                S.tt(ktb[:, 0:n], kr[:, 0:n], EK[:, 0:n], ALU.mult, eng="pool")
                S.tt(khb[:, 0:n], kr[:, 0:n], EH[:, 0:n], ALU.mult, eng="pool")
                corder = list(range(ncg)) if d == 0 else list(range(ncg - 1, -1, -1))
                for c in corder:
                    sl = slice(c * 128, (c + 1) * 128)
                    tk = tok0 + c * 128
                    b = nck % 2; nck += 1
                    S.dma(vt[b], PT[tk:tk + 128, 0:256], q="sp")
                    S.copy(vb[b], vt[b], eng="act")
                    S.tt(Qbd[b], qtb[:, sl].unsqueeze(1).to_broadcast([128, 4, 128]), hmb, ALU.mult, eng="pool")
                    S.mm(ps_att, ktb[:, sl], Qbd[b].rearrange("p a b -> p (a b)"))
                    S.tt(attm[b], ps_att.rearrange("p (a b) -> p a b", a=4),
                         maskt[d].unsqueeze(1).to_broadcast([128, 4, 128]), ALU.mult)
                    S.mm(ps_po[:, 0:256], qtb[:, sl], Sbdb, start=True, stop=False)
                    for h in range(4):
                        S.mm(ps_po[:, h * 64:(h + 1) * 64], attm[b][:, h, :], vb[b][:, h * 64:(h + 1) * 64],
                             start=False, stop=(h == 3))
                    S.mm(PS[5][:, 0:128], khb[:, sl], identb)
                    S.copy(khT[b], PS[5][:, 0:128], eng="act")
                    S.mm(ps_st[:, 0:256], khT[b], vb[b])
                    S.tt(stmp, ps_st[:, 0:256], bdm, ALU.mult)
                    S.stt(Sbd, Sbd, tots[:, c, 2:3], stmp, ALU.mult, ALU.add)
                    S.copy(Sbdb, Sbd, eng="act")
                    S.copy(osb[b], ps_po[:, 0:256], eng="act")
                    S.dma(OGd[d][tk:tk + 128, :], osb[b], q="act")
        NCB = 3
        cf = [P.t(f"cf{i}", [128, 256]) for i in range(NCB)]
        cbw = [P.t(f"cbw{i}", [128, 256]) for i in range(NCB)]
        csq = [P.t(f"csq{i}", [128, 256]) for i in range(NCB)]
        chs = [P.t(f"chs{i}", [128, 4, 3]) for i in range(NCB)]
        for ti in range(NTILE):
            b = ti % NCB
            tsl = slice(ti * 128, (ti + 1) * 128)
            S.dma(cf[b], OG[tsl, :], q="sp")
            S.dma(cbw[b], OGB[tsl, :], q="sp")
            ob = cf[b]; hst = chs[b]
            S.tt(ob, ob, cbw[b], ALU.add, eng="pool")
            S.act(csq[b], ob, AF.Square)
            S.op("dve", "tensor_reduce", hst[:, :, 0], csq[b].rearrange("p (a b) -> p a b", a=4), AX.X, ALU.add,
                 reads=[csq[b]], writes=[hst])
            S.ts(hst[:, :, 1], hst[:, :, 0], 1.0 / 64, EPS, ALU.mult, ALU.add)
            S.act(hst[:, :, 2], hst[:, :, 1], AF.Sqrt)
            S.recip(hst[:, :, 1], hst[:, :, 2])
            o3 = ob.rearrange("p (a b) -> p a b", a=4)
            S.tt(o3, o3, hst[:, :, 1:2].to_broadcast([128, 4, 64]), ALU.mult)
            S.tt(o3, o3, gnorm, ALU.mult, eng="pool")
            S.dma(YS[tsl, 0:256], ob, q="act")
        ublk = [P.t("ublk0", [128, 8, 256]), P.t("ublk1", [128, 8, 256])]
        ublb = [P.t("ublb0", [128, 16, 128], BF16), P.t("ublb1", [128, 16, 128], BF16)]
        blocks = [(0, 32, 0)]
        cpos = NCH_CTX
        while cpos < NCH:
            nb = min(128, NCH - cpos)
            blocks.append((cpos, nb, NCH_CTX + (NCH - (cpos + nb))))
            cpos += nb
        for bi, (c0, nb, bp) in enumerate(blocks):
            ub = ublk[bi % 2]; ubb = ublb[bi % 2]
            S.dma(ub[0:nb], PT[c0 * 8:(c0 + nb) * 8, 512:768].rearrange("(c s) f -> c s f", s=8), q="sp")
            S.copy(ubb[0:nb].rearrange("c g (s h) -> c g s h", s=8), ub[0:nb].rearrange("c s (g h) -> c g s h", g=16))
            for (UTx, perm, pos0) in ((UTf, identb, c0), (UTb, antib, bp)):
                if perm is identb:
                    pm = identb[0:nb, 0:nb]
                else:
                    pm = antib if nb == 128 else anti32b
                for g4 in range(4):
                    pb = PS[0] if g4 % 2 == 0 else PS[5]
                    for gg in range(4):
                        g = g4 * 4 + gg
                        S.mm(pb[:, gg * 128: gg * 128 + nb], ubb[0:nb, g, :], pm)
                    S.evac(UTx[:, g4 * 4:(g4 + 1) * 4, pos0:pos0 + nb],
                           pb.rearrange("p (a b) -> p a b", a=4)[:, :, 0:nb])
        P.close()

        if stop_after == 'G':
            raise _Stop()
        P0 = Pool(S)
        Toep = P0.t("Toep", [128, 2, 16, 128], BF16)
        WstR = P0.t("WstR", [128, 16, 128], BF16)
        WstI = P0.t("WstI", [128, 16, 128], BF16)
        CdR = P0.t("CdR", [128, 16, 128], BF16)
        CdI = P0.t("CdI", [128, 16, 128], BF16)
        th8 = P0.t("th8", [128, 16]); r8t = P0.t("r8t", [128, 16])
        P = Pool(S)
        lamL = P.t("lamL", [16, 2, 128])
        S.dma(lamL[:, 0, :], W["s5_lam_re"][l].rearrange("d (gp m) p -> (d gp) (m p)", m=2))
        S.dma(lamL[:, 1, :], W["s5_lam_im"][l].rearrange("d (gp m) p -> (d gp) (m p)", m=2))
        i16 = ident[0:16, 0:16]
        lam = P.t("lam", [128, 2, 16])
        for ri in range(2):
            S.mm(PS[0][:, ri * 16:(ri + 1) * 16], lamL[:, ri, :], i16)
        S.evac(lam, PS[0][:, 0:32].rearrange("p (a b) -> p a b", a=2))
        ld = P.t("ld", [2, 2, 8])
        S.dma(ld, W["s5_log_dt"][l].rearrange("d (gp m) -> m d gp", m=2), allow_slow_non_contiguous=True)
        selm = P.t("selm", [2, 128]); S.dma(selm, C["selm"])
        S.mm(PS[1][:, 0:16], selm, ld.rearrange("m d g -> m (d g)"))
        dt_ = P.t("dt", [128, 16]); S.act(dt_, PS[1][:, 0:16], AF.Exp)
        sc = {}
        for nm in ["lrd", "mag", "imag", "ang", "sn", "cs", "lbr", "lbi", "ilr", "ili", "den", "rden",
                   "nr", "t1", "t2", "cor", "coi", "th8", "r8"]:
            sc[nm] = P.t("s5_" + nm, [128, 16])
        S.tt(sc["lrd"], lam[:, 0, :], dt_, ALU.mult)
        S.act(sc["mag"], sc["lrd"], AF.Exp)
        S.act(sc["imag"], sc["lrd"], AF.Exp, scale=-1.0)
        S.act(sc["r8"], sc["lrd"], AF.Exp, scale=8.0)
        S.tt(sc["ang"], lam[:, 1, :], dt_, ALU.mult)
        range_reduce(P, sc["sn"], sc["cs"], sc["ang"], [128, 16])
        S.tt(sc["lbr"], sc["mag"], sc["cs"], ALU.mult)
        S.tt(sc["lbi"], sc["mag"], sc["sn"], ALU.mult)
        S.tt(sc["ilr"], sc["imag"], sc["cs"], ALU.mult)
        S.stt(sc["ili"], sc["imag"], -1.0, sc["sn"], ALU.mult, ALU.mult)
        S.tt(sc["den"], lam[:, 0, :], lam[:, 0, :], ALU.mult)
        S.tt(sc["t1"], lam[:, 1, :], lam[:, 1, :], ALU.mult)
        S.tt(sc["den"], sc["den"], sc["t1"], ALU.add)
        S.recip(sc["rden"], sc["den"])
        S.ts(sc["nr"], sc["lbr"], -1.0, None, ALU.add)
        S.tt(sc["t1"], sc["nr"], lam[:, 0, :], ALU.mult)
        S.tt(sc["t2"], sc["lbi"], lam[:, 1, :], ALU.mult)
        S.tt(sc["t1"], sc["t1"], sc["t2"], ALU.add)
        S.tt(sc["cor"], sc["t1"], sc["rden"], ALU.mult)
        S.tt(sc["t1"], sc["lbi"], lam[:, 0, :], ALU.mult)
        S.tt(sc["t2"], sc["nr"], lam[:, 1, :], ALU.mult)
        S.tt(sc["t1"], sc["t1"], sc["t2"], ALU.subtract)
        S.tt(sc["coi"], sc["t1"], sc["rden"], ALU.mult)
        kq = P.t("s5_kq", [128, 16], I32); kqf = P.t("s5_kqf", [128, 16])
        S.ts(kq, sc["ang"], 8.0 / TWO_PI, None, ALU.mult)
        S.copy(kqf, kq)
        S.ts(sc["t1"], sc["ang"], 8.0, None, ALU.mult)
        S.stt(sc["th8"], kqf, -TWO_PI, sc["t1"], ALU.mult, ALU.add)
        pwr = P.t("pwr", [128, 16, 9]); pwi = P.t("pwi", [128, 16, 9])
        ipr = P.t("ipr", [128, 16, 9]); ipi = P.t("ipi", [128, 16, 9])
        for (ar, ai, br, bi, eng_) in ((pwr, pwi, sc["lbr"], sc["lbi"], "dve"), (ipr, ipi, sc["ilr"], sc["ili"], "pool")):
            q1 = P.t("pwq1", [128, 16, 4]); q2 = P.t("pwq2", [128, 16, 4])
            S.memset(ar[:, :, 0], 1.0, eng=eng_); S.memset(ai[:, :, 0], 0.0, eng=eng_)
            S.copy(ar[:, :, 1], br, eng=eng_); S.copy(ai[:, :, 1], bi, eng=eng_)
            for (lo, cnt, kk) in ((2, 1, 1), (3, 2, 2), (5, 4, 4)):
                xr_s = ar[:, :, 1:1 + cnt]; xi_s = ai[:, :, 1:1 + cnt]
                yr_s = ar[:, :, kk:kk + 1].to_broadcast([128, 16, cnt]); yi_s = ai[:, :, kk:kk + 1].to_broadcast([128, 16, cnt])
                t1_ = q1[:, :, 0:cnt]; t2_ = q2[:, :, 0:cnt]
                S.tt(t1_, xr_s, yr_s, ALU.mult, eng=eng_)
                S.tt(t2_, xi_s, yi_s, ALU.mult, eng=eng_)
                S.tt(ar[:, :, lo:lo + cnt], t1_, t2_, ALU.subtract, eng=eng_)
                S.tt(t1_, xr_s, yi_s, ALU.mult, eng=eng_)
                S.tt(t2_, xi_s, yr_s, ALU.mult, eng=eng_)
                S.tt(ai[:, :, lo:lo + cnt], t1_, t2_, ALU.add, eng=eng_)
        rpr = P.t("rpr", [128, 16, 8]); rpi = P.t("rpi", [128, 16, 8])
        for t_ in range(8):
            S.copy(rpr[:, :, t_], pwr[:, :, 8 - t_]); S.copy(rpi[:, :, t_], pwi[:, :, 8 - t_], eng="pool")
        Br = P.t("Br", [128, 16, 16]); Bi = P.t("Bi", [128, 16, 16])
        for (dst, nm) in ((Br, "s5_b_re"), (Bi, "s5_b_im")):
            S.dma(dst.rearrange("p (d g) h -> p d g h", d=2),
                  W[nm][l].rearrange("d (gp m) p h -> (m p) d gp h", m=2))
        Bbr = P.t("Bbr", [128, 16, 16]); Bbi = P.t("Bbi", [128, 16, 16]); tb = P.t("tb", [128, 16, 16])
        cor_b = sc["cor"].unsqueeze(2).to_broadcast([128, 16, 16])
        coi_b = sc["coi"].unsqueeze(2).to_broadcast([128, 16, 16])
        S.tt(Bbr, Br, cor_b, ALU.mult); S.tt(tb, Bi, coi_b, ALU.mult); S.tt(Bbr, Bbr, tb, ALU.subtract)
        S.tt(Bbi, Bi, cor_b, ALU.mult); S.tt(tb, Br, coi_b, ALU.mult); S.tt(Bbi, Bbi, tb, ALU.add)
        CL = P.t("CL", [16, 2, 32, 64])
        S.dma(CL[:, 0], W["s5_c_re"][l].rearrange("d g h p -> h (d g) p"))
        S.dma(CL[:, 1], W["s5_c_im"][l].rearrange("d g h p -> h (d g) p"))
        Cr = P.t("Cr", [128, 16, 16]); Ci = P.t("Ci", [128, 16, 16])
        for ri, dst in ((0, Cr), (1, Ci)):
            for dd in range(2):
                for gp in range(8):
                    for m in range(2):
                        g = gp * 2 + m
                        S.mm(PS[2 + ri][m * 64:(m + 1) * 64, (dd * 8 + gp) * 16:(dd * 8 + gp + 1) * 16],
                             CL[:, ri, dd * 16 + g, :], i16)
            S.evac(dst, PS[2 + ri][:, 0:256].rearrange("p (a b) -> p a b", a=16))
        Ar = P.t("Ar", [128, 16, 8, 16]); Ai = P.t("Ai", [128, 16, 8, 16])
        Cqr = P.t("Cqr", [128, 16, 8, 16]); Cqi = P.t("Cqi", [128, 16, 8, 16])
        Cdr = P.t("Cdr", [128, 16, 8, 16]); Cdi = P.t("Cdi", [128, 16, 8, 16])
        Wsr = P.t("Wsr", [128, 16, 8, 16]); Wsi = P.t("Wsi", [128, 16, 8, 16])
        t4a = P.t("t4a", [128, 8, 8, 16]); t4b = P.t("t4b", [128, 8, 8, 16])

        def cmul(outr, outi, pr, pi_, xr, xi, neg_im=False):
            S.tt(outr, pr, xr, ALU.mult); S.tt(t4a, pi_, xi, ALU.mult, eng="pool")
            S.tt(outr, outr, t4a, ALU.subtract)
            S.tt(outi, pr, xi, ALU.mult); S.tt(t4b, pi_, xr, ALU.mult, eng="pool")
            S.tt(outi, outi, t4b, ALU.add)
            if neg_im:
                S.ts(outi, outi, -1.0, None, ALU.mult, eng="pool")

        for dd in range(2):
            ds_ = slice(dd * 8, (dd + 1) * 8)

            def pw_b(tr, ti_, lo, step):
                a_ = tr[:, ds_, lo:lo + 8]; b_ = ti_[:, ds_, lo:lo + 8]
                return (a_.unsqueeze(3).to_broadcast([128, 8, 8, 16]), b_.unsqueeze(3).to_broadcast([128, 8, 8, 16]))

            def x_b(xr, xi):
                return (xr[:, ds_, :].unsqueeze(2).to_broadcast([128, 8, 8, 16]),
                        xi[:, ds_, :].unsqueeze(2).to_broadcast([128, 8, 8, 16]))
            bbr_, bbi_ = x_b(Bbr, Bbi)
            cr_, ci_ = x_b(Cr, Ci)
            if dd == 0:
                pa = pw_b(ipr, ipi, 0, 1)
                pc = pw_b(pwr, pwi, 0, 1)
                pd = pw_b(pwr, pwi, 1, 1)
            else:
                pa = pw_b(pwr, pwi, 0, 1)
                pc = pw_b(ipr, ipi, 0, 1)
                pd = pw_b(rpr, rpi, 0, 1)
            cmul(Ar[:, ds_], Ai[:, ds_], pa[0], pa[1], bbr_, bbi_)
            cmul(Cqr[:, ds_], Cqi[:, ds_], pc[0], pc[1], cr_, ci_, neg_im=True)
            cmul(Cdr[:, ds_], Cdi[:, ds_], pd[0], pd[1], cr_, ci_, neg_im=True)
            if dd == 0:
                p7r = pwr[:, ds_, 7:8].unsqueeze(3).to_broadcast([128, 8, 8, 16])
                p7i = pwi[:, ds_, 7:8].unsqueeze(3).to_broadcast([128, 8, 8, 16])
                S.tt(Wsr[:, ds_], Ar[:, ds_], p7r, ALU.mult); S.tt(t4a, Ai[:, ds_], p7i, ALU.mult)
                S.tt(Wsr[:, ds_], Wsr[:, ds_], t4a, ALU.subtract)
                S.tt(Wsi[:, ds_], Ar[:, ds_], p7i, ALU.mult); S.tt(t4b, Ai[:, ds_], p7r, ALU.mult)
                S.tt(Wsi[:, ds_], Wsi[:, ds_], t4b, ALU.add)
            else:
                S.copy(Wsr[:, ds_], Ar[:, ds_]); S.copy(Wsi[:, ds_], Ai[:, ds_], eng="pool")
        S.copy(th8, sc["th8"]); S.copy(r8t, sc["r8"])
        S.copy(CdR, Cdr.rearrange("p a b c -> p a (b c)"))
        S.copy(CdI, Cdi.rearrange("p a b c -> p a (b c)"), eng="pool")
        tmk = [P.t("tmf", [128, 128]), P.t("tmb", [128, 128])]
        S.dma(tmk[0], C["tmaskf"]); S.dma(tmk[1], C["tmaskb"])
        for dd in range(2):
            for gp in range(8):
                dg = dd * 8 + gp
                for m in range(2):
                    g = gp * 2 + m
                    ms = slice(m * 64, (m + 1) * 64)
                    pb = PS[(g % 2)]
                    S.mm(pb[:, 0:128], Ar[ms, dg].rearrange("p a b -> p (a b)"), Cqr[ms, dg].rearrange("p a b -> p (a b)"),
                         start=True, stop=False)
                    S.mm(pb[:, 0:128], Ai[ms, dg].rearrange("p a b -> p (a b)"), Cqi[ms, dg].rearrange("p a b -> p (a b)"),
                         start=False, stop=True)
                    S.tt(Toep[:, dd, g, :], pb[:, 0:128], tmk[dd], ALU.mult)
                pb = PS[2 + (gp % 2)]
                S.mm(pb[:, 0:128], Wsr[:, dg].rearrange("p a b -> p (a b)"), ident)
                S.mm(pb[:, 128:256], Wsi[:, dg].rearrange("p a b -> p (a b)"), ident)
                S.copy(WstR[:, dg, :], pb[:, 0:128], eng="act")
                S.copy(WstI[:, dg, :], pb[:, 128:256], eng="act")
        P.close()
        P = Pool(S)
        iota = P.t("iota", [128, NCH]); S.dma(iota, C["ciota"])
        NSB = 2
        sset = []
        for i_ in range(NSB):
            sset.append(dict(
                angc=P.t("angc", [128, NCH]), snc=P.t("snc", [128, NCH]), csc=P.t("csc", [128, NCH]),
                r8b=P.t("r8b", [128, 1]), xr=P.t("xr", [128, NCH]), xi=P.t("xi", [128, NCH]),
                ta=P.t("ta", [128, NCH]), tb=P.t("tbb", [128, NCH]), zr=P.t("zr", [128, NCH]), zi=P.t("zi", [128, NCH]),
                xpr=P.t("xpr", [128, NCH], BF16), xpi=P.t("xpi", [128, NCH], BF16)))
        yst = [P.t("yst0", [128, 8, 256]), P.t("yst1", [128, 8, 256])]
        YB = P.t("YB", [128, 5, 8, 256])
        HALF = NCH // 2
        it_s = 0
        for dd in range(2):
            UTx = UTf if dd == 0 else UTb
            for gp in range(8):
                dg = dd * 8 + gp
                B_ = sset[it_s % NSB]; slot_ = it_s % NSB; it_s += 1
                angc, snc, csc, r8b = B_["angc"], B_["snc"], B_["csc"], B_["r8b"]
                xr_, xi_, ta, tbb, zr, zi, xpr, xpi = B_["xr"], B_["xi"], B_["ta"], B_["tb"], B_["zr"], B_["zi"], B_["xpr"], B_["xpi"]
                for hf in range(2):
                    cs0 = hf * HALF
                    for m in range(2):
                        g = gp * 2 + m
                        S.mm(PS[hf][m * 64:(m + 1) * 64, 0:HALF], WstR[:, dg, m * 64:(m + 1) * 64], UTx[:, g, cs0:cs0 + HALF])
                        S.mm(PS[2 + hf][m * 64:(m + 1) * 64, 0:HALF], WstI[:, dg, m * 64:(m + 1) * 64], UTx[:, g, cs0:cs0 + HALF])
                S.act(angc, iota, AF.Copy, scale=th8[:, dg:dg + 1])
                range_reduce(P, snc, csc, angc, [128, NCH], slot=slot_)
                for hf in range(2):
                    sl = slice(hf * HALF, (hf + 1) * HALF)
                    S.tt(xr_[:, sl], PS[hf][:, 0:HALF], csc[:, sl], ALU.mult)
                    S.tt(ta[:, sl], PS[2 + hf][:, 0:HALF], snc[:, sl], ALU.mult)
                    S.tt(xi_[:, sl], PS[2 + hf][:, 0:HALF], csc[:, sl], ALU.mult)
                    S.tt(tbb[:, sl], PS[hf][:, 0:HALF], snc[:, sl], ALU.mult)
                S.tt(xr_, xr_, ta, ALU.add, eng="pool")
                S.tt(xi_, xi_, tbb, ALU.subtract, eng="pool")
                S.copy(r8b, r8t[:, dg:dg + 1], eng="act")
                S.scan(zr, r8b.to_broadcast([128, NCH]), xr_, 0.0)
                S.scan(zi, r8b.to_broadcast([128, NCH]), xi_, 0.0)
                S.tt(ta, zr, csc, ALU.mult); S.tt(tbb, zi, snc, ALU.mult, eng="pool")
                S.memset(xpr[:, 0:1], 0.0, eng="pool"); S.memset(xpi[:, 0:1], 0.0, eng="pool")
                S.tt(xpr[:, 1:NCH], ta[:, 0:NCH - 1], tbb[:, 0:NCH - 1], ALU.subtract)
                S.tt(ta, zr, snc, ALU.mult, eng="pool"); S.tt(tbb, zi, csc, ALU.mult, eng="pool")
                S.tt(xpi[:, 1:NCH], ta[:, 0:NCH - 1], tbb[:, 0:NCH - 1], ALU.add)
                for m in range(2):
                    g = gp * 2 + m
                    ms = slice(m * 64, (m + 1) * 64)
                    for bi, (c0, nb, bp) in enumerate(blocks):
                        pos0 = c0 if dd == 0 else bp
                        pb = PS[4 + ((bi + m) % 4)]
                        S.mm(pb[0:nb, 0:128], UTx[:, g, pos0:pos0 + nb], Toep[:, dd, g, :], start=True, stop=False)
                        S.mm(pb[0:nb, 0:128], xpr[ms, pos0:pos0 + nb], CdR[ms, dg, :], start=False, stop=False)
                        S.mm(pb[0:nb, 0:128], xpi[ms, pos0:pos0 + nb], CdI[ms, dg, :], start=False, stop=True)
                        S.copy(YB[0:nb, bi, :, g * 16:(g + 1) * 16], pb[0:nb, 0:128].rearrange("c (t h) -> c t h", t=8), eng="act")
            for bi, (c0, nb, bp) in enumerate(blocks):
                if dd == 0:
                    S.dma(YSF[c0 * 8:(c0 + nb) * 8, :].rearrange("(c t) f -> c t f", t=8), YB[0:nb, bi], q="act")
                else:
                    clast = (NCH_CTX - 1 - bp) if bi == 0 else (NCH - 1 - (bp - NCH_CTX))
                    cfirst = clast - nb + 1
                    stg = yst[bi % 2]
                    aF = anti if nb == 128 else anti32
                    for q4 in range(4):
                        pbq = PS[4 + q4]
                        S.mm(pbq[0:nb, :], aF, YB[0:nb, bi].rearrange("c t f -> c (t f)")[:, q4 * 512:(q4 + 1) * 512])
                        S.evac(stg[0:nb].rearrange("c t f -> c (t f)")[:, q4 * 512:(q4 + 1) * 512], pbq[0:nb, :])
                    S.dma(YSB[cfirst * 8:(cfirst + nb) * 8, :].rearrange("(c t) f -> c t f", t=8), stg[0:nb], q="act")
        P.close()
        P0.close()
        PU.close()

        if stop_after == 'S':
            raise _Stop()
        if stop_after == 'P':
            raise _Stop()
        P = Pool(S)
        Wo = P.t("Wo", [128, 8, 1024], BF16)
        wos = [P.t("wos0", [128, 1024]), P.t("wos1", [128, 1024])]
        for k in range(8):
            S.dma(wos[k % 2], W["w_out"][l, k * 128:(k + 1) * 128, :], q=("sp" if k % 2 == 0 else "pool"))
            S.copy(Wo[:, k, :], wos[k % 2], eng=("dve" if k % 2 == 0 else "act"))
        wgl = P.t("wgl", [128, 2, 256]); wglb = P.t("wglb", [128, 2, 256], BF16)
        S.dma(wgl, W["s5_w_glu"][l].rearrange("(k p) n -> p k n", p=128))
        S.copy(wglb, wgl)
        bgl = P.t("bgl", [128, 256]); S.dma(bgl, dram_rows_bcast(W["s5_b_glu"], l * 256, 256))
        dsk = P.t("dsk", [128, 256]); S.dma(dsk, dram_rows_bcast(W["s5_d"], l * 256, 256))
        GG = [P.t("GG0", [128, 1024]), P.t("GG1", [128, 1024])]
        for r_ in range(2):
            S.tt(GG[r_], gpost, MOD[r_][:, 2048:3072], ALU.mult, eng="pool")
        NB = 4
        ys_b = [P.t(f"ys{i}", [128, 1024]) for i in range(NB)]
        gt_b = [P.t(f"gt{i}", [128, 1024]) for i in range(NB)]
        x_b = [P.t(f"xo{i}", [128, 1024]) for i in range(NB)]
        s5a = [P.t(f"s5a{i}", [128, 3, 256]) for i in range(NB)]
        y5_ = [P.t(f"y5{i}", [128, 256]) for i in range(NB)]
        g5_ = [P.t(f"g5{i}", [128, 256]) for i in range(NB)]
        t5_ = [P.t(f"t5{i}", [128, 256]) for i in range(NB)]
        g5b_ = [P.t(f"g5b{i}", [128, 256], BF16) for i in range(NB)]
        g5T_ = [P.t(f"g5T{i}", [128, 2, 128], BF16) for i in range(NB)]
        ybf_ = [P.t(f"ybf{i}", [128, 1024], BF16) for i in range(NB)]
        yT_ = [P.t(f"yT{i}", [128, 8, 128], BF16) for i in range(NB)]
        zt__ = [P.t(f"zt{i}", [128, 1024]) for i in range(NB)]
        junk_ = [P.t(f"junk{i}", [128, 512], BF16) for i in range(NB)]
        st__ = [P.t(f"st{i}", [128, 4]) for i in range(NB)]
        tiles_o = [ti for ti in range(NTILE) if not (last and ti < 2)]

        def o_stage1(n):
            ti = tiles_o[n]; b = n % NB
            tsl = slice(ti * 128, (ti + 1) * 128)
            y5, g5, t5, g5b = y5_[b], g5_[b], t5_[b], g5b_[b]
            S.dma(ys_b[b][:, 0:512], YS[tsl, 0:512], q="sp")
            S.dma(ys_b[b][:, 768:1024], YS[tsl, 768:1024], q="sp")
            S.dma(gt_b[b], PT[tsl, 768:1792], q="sp")
            S.dma(x_b[b], x_src[tsl, :], q="sp")
            S.dma(s5a[b][:, 0, :], YSF[tsl, :], q="sp")
            S.dma(s5a[b][:, 1, :], YSB[tsl, :], q="sp")
            S.dma(s5a[b][:, 2, :], PT[tsl, 512:768], q="sp")
            S.tt(s5a[b][:, 0, :], s5a[b][:, 0, :], s5a[b][:, 1, :], ALU.add, eng="pool")
            S.tt(y5, s5a[b][:, 2, :], dsk, ALU.mult)
            S.tt(y5, y5, s5a[b][:, 0, :], ALU.add)
            S.tt(t5, y5, y5, ALU.mult, eng="pool")
            S.ts(t5, t5, 0.044715, 1.0, ALU.mult, ALU.add, eng="pool")
            S.tt(t5, t5, y5, ALU.mult, eng="pool")
            S.act(t5, t5, AF.Sigmoid, scale=2.0 * float(np.sqrt(2.0 / np.pi)))
            S.tt(g5, y5, t5, ALU.mult)
            S.copy(g5b, g5, eng="pool")
            pg = PS[n % 2]
            for k in range(2):
                S.mm(pg[:, k * 128:(k + 1) * 128], g5b[:, k * 128:(k + 1) * 128], identb)

        def o_stage2(n):
            b = n % NB
            S.copy(g5T_[b], PS[n % 2][:, 0:256].rearrange("p (a b) -> p a b", a=2), eng="act")
            pz = PS[2 + n % 2]
            for k in range(2):
                S.mm(pz[:, 0:256], g5T_[b][:, k, :], wglb[:, k, :], start=(k == 0), stop=(k == 1))

        def o_stage3(n):
            b = n % NB
            t5, g5 = t5_[b], g5_[b]
            pz = PS[2 + n % 2]
            S.tt(t5, pz[:, 0:256], bgl, ALU.add)
            S.act(t5, t5, AF.Sigmoid)
            S.tt(ys_b[b][:, 512:768], g5, t5, ALU.mult)
            S.tt(ybf_[b], ys_b[b], gt_b[b], ALU.mult, eng="pool")
            for half in range(2):
                pt_ = PS[4 + half]
                for kk in range(4):
                    k = half * 4 + kk
                    S.mm(pt_[:, kk * 128:(kk + 1) * 128], ybf_[b][:, k * 128:(k + 1) * 128], identb)
                S.copy(yT_[b][:, half * 4:(half + 1) * 4, :], pt_.rearrange("p (a b) -> p a b", a=4), eng="act")

        def o_stage4(n):
            ti = tiles_o[n]; b = n % NB
            rsel = 1 if ti < 2 else 0
            tsl = slice(ti * 128, (ti + 1) * 128)
            st_, zt_ = st__[b], zt__[b]
            for nn in range(2):
                po = PS[6 + nn]
                for k in range(8):
                    S.mm(po, yT_[b][:, k, :], Wo[:, k, nn * 512:(nn + 1) * 512], start=(k == 0), stop=(k == 7))
                S.act(junk_[b], po, AF.Square, accum_out=st_[:, nn:nn + 1])
            S.tt(st_[:, 2:3], st_[:, 0:1], st_[:, 1:2], ALU.add)
            S.ts(st_[:, 2:3], st_[:, 2:3], 1.0 / D, EPS, ALU.mult, ALU.add)
            S.act(st_[:, 3:4], st_[:, 2:3], AF.Sqrt)
            S.recip(st_[:, 2:3], st_[:, 3:4])
            for nn in range(2):
                po = PS[6 + nn]
                S.stt(zt_[:, nn * 512:(nn + 1) * 512], po, st_[:, 2:3], GG[rsel][:, nn * 512:(nn + 1) * 512], ALU.mult, ALU.mult)
            S.tt(zt_, zt_, x_b[b], ALU.add, eng="pool")
            if last:
                S.dma(out[(ti - 2) * 128:(ti - 1) * 128, :], zt_, q="act")
            else:
                S.dma(xs[tsl, :], zt_, q="act")

        stages_o = [o_stage1, o_stage2, o_stage3, o_stage4]
        nit = len(tiles_o)
        for step in range(nit + len(stages_o) - 1):
            for si, fn in enumerate(stages_o):
                n = step - si
                if 0 <= n < nit:
                    fn(n)
        P.close()

    except _Stop:
        pass
    S.barrier()
    G.es.close()
    return nc


_CACHE = {}


def kernel(**inputs):
    consts = make_consts()
    B = inputs["x"].shape[0]
    if "nc" not in _CACHE:
        _CACHE["nc"] = build(debug=False)
    nc = _CACHE["nc"]
    in_maps = []
    ncores = 8
    for core in range(ncores):
        b = core % B
        m = {}
        m["xin"] = np.ascontiguousarray(np.concatenate([inputs["ctx"][b], inputs["x"][b]], axis=0).astype(np.float32))
        ccv = np.stack([inputs["c"][b], inputs["c_ctx"]], axis=-1).astype(np.float32)
        m["cc"] = np.ascontiguousarray(ccv.reshape(8, 128, 2).transpose(1, 0, 2))
        for n in WEIGHT_NAMES:
            m[n] = np.ascontiguousarray(np.asarray(inputs[n], dtype=np.float32))
        for n, v in consts.items():
            m["k_" + n] = v
        in_maps.append(m)
    res = run_bass_kernel_spmd(nc, in_maps, core_ids=list(range(ncores)))
    outs = [res.results[b]["out"] for b in range(B)]
    return np.stack(outs, axis=0).astype(np.float32)
```

```python
import numpy as np
import ml_dtypes
from contextlib import ExitStack
import concourse.bass as bass
import concourse.mybir as mybir
from concourse.bass_utils import run_bass_kernel_spmd

F32 = mybir.dt.float32
BF16 = mybir.dt.bfloat16
I32 = mybir.dt.int32
ALU = mybir.AluOpType
AF = mybir.ActivationFunctionType
AX = mybir.AxisListType

NCTX = 256
NLAT = 4096
T = NCTX + NLAT
D = 1024
NTILE = T // 128
EPS = 1e-6
PI = float(np.pi)
TWO_PI = float(2 * np.pi)
NCH = T // 8
NCH_CTX = NCTX // 8
import os
NOSCHED = bool(os.environ.get('MK_NOSCHED'))


class Sched:
    NDMA = 12

    def __init__(self, nc):
        self.nc = nc
        self.eng = {"pe": nc.tensor, "dve": nc.vector, "act": nc.scalar,
                    "pool": nc.gpsimd, "sp": nc.sync}
        self.sem = {k: nc.alloc_semaphore("sem_" + k) for k in self.eng}
        self.cnt = {k: 0 for k in self.eng}
        self.seen = {k: {} for k in self.eng}
        self.dq = {}
        for q in ("sp", "pool", "act"):
            self.dq[q] = {"sems": [nc.alloc_semaphore(f"dq_{q}_{i}") for i in range(self.NDMA)], "n": 0}
        self.lastw = {}
        self.readers = {}
        self.ntens = 0
        self.rr = 0
        self.pending = []

    def _wait(self, eng, ev):
        if ev is None:
            return
        if ev[0] == "e":
            _, src, val = ev
            if src == "pe" and eng == "pe":
                return
            key = ("e", src)
            sem = self.sem[src]
        else:
            _, q, slot, val = ev
            key = ("d", q, slot)
            sem = self.dq[q]["sems"][slot]
        if self.seen[eng].get(key, 0) >= val:
            return
        self.seen[eng][key] = val
        self.eng[eng].wait_ge(sem, val)

    def _deps(self, eng, reads, writes):
        for t in reads:
            self._wait(eng, self.lastw.get(t))
        for t in writes:
            self._wait(eng, self.lastw.get(t))
            for ev in self.readers.get(t, {}).values():
                self._wait(eng, ev)

    def _commit(self, ev, reads, writes):
        for t in reads:
            d = self.readers.setdefault(t, {})
            d[ev[0:2] if ev[0] == "e" else ev[0:3]] = ev
        for t in writes:
            self.lastw[t] = ev
            self.readers[t] = {}

    WHOLE_DRAM = ("RP",)

    @classmethod
    def _names(cls, aps):
        out = []
        for a in aps:
            if a is None or isinstance(a, (int, float)):
                continue
            nm = a.tensor.name
            if "DRam" in type(a.tensor).__name__ and nm not in cls.WHOLE_DRAM:
                nm = f"{nm}@{a.offset}:{tuple(map(tuple, a.ap))}"
            out.append(nm)
        return out

    LAT = float(os.environ.get('MK_LAT', '0.35'))

    def op(self, eng, method, *args, reads=(), writes=(), **kw):
        r = self._names(reads)
        w = self._names(writes)
        cost = self._cost(eng, method, args, kw)
        self.pending.append(("op", eng, method, args, kw, r, w, cost))

    def dma(self, out, in_, q=None, **kw):
        if q is None:
            q = "sp"
        r = self._names([in_])
        w = self._names([out])
        nbytes = 1
        for d_ in out.shape:
            nbytes *= d_
        nbytes *= 4
        self.pending.append(("dma", q, None, (out, in_), kw, r, w, 0.15, 2.0 + nbytes / 150e3))

    @staticmethod
    def _free(ap):
        n = 1
        for d_ in ap.shape[1:]:
            n *= d_
        return n

    def _cost(self, eng, method, args, kw):
        try:
            if eng == "pe":
                rhs = args[2]
                n = max(64, self._free(rhs))
                c = n / 2400.0
                if rhs.dtype == F32:
                    c *= 4
                return c + 0.07
            n = self._free(args[0])
            c = n / 960.0 * (1.5 if eng == "dve" else 1.3) + float(os.environ.get('MK_OVH', '0.1'))
            if eng == "pool":
                c = n / 400.0 + 0.2
            if method == "tensor_tensor_scan":
                c = 2 * n / 960.0 + 0.1
            return c
        except Exception:
            return 0.3

    def flush(self):
        ops = self.pending
        self.pending = []
        n = len(ops)
        if n == 0:
            return
        lastw = {}
        readers = {}
        preds = [None] * n
        for i, o in enumerate(ops):
            ps = set()
            for t in o[5]:
                j = lastw.get(t)
                if j is not None:
                    ps.add(j)
            for t in o[6]:
                j = lastw.get(t)
                if j is not None:
                    ps.add(j)
                ps.update(readers.get(t, ()))
            for t in o[5]:
                readers.setdefault(t, []).append(i)
            for t in o[6]:
                lastw[t] = i
                readers[t] = []
            ps.discard(i)
            preds[i] = ps
        succs = [[] for _ in range(n)]
        indeg = [0] * n
        for i in range(n):
            indeg[i] = len(preds[i])
            for p in preds[i]:
                succs[p].append(i)
        fin = [0.0] * n
        rdy = [0.0] * n
        crit = [-1] * n
        epred = [-1] * n
        elast = {e: -1 for e in self.eng}
        stt_ = [0.0] * n
        eng_free = {e: 0.0 for e in self.eng}
        ready = {e: [] for e in self.eng}
        import heapq
        for i in range(n):
            if indeg[i] == 0:
                heapq.heappush(ready[ops[i][1]], i)
        order = []
        remaining = n
        WINDOW = int(os.environ.get('MK_WINDOW', '24'))
        if NOSCHED:
            order = list(range(n))
            remaining = 0
        while remaining:
            best = None
            for e, lst in ready.items():
                if not lst:
                    continue
                cand = heapq.nsmallest(WINDOW, lst)
                for i in cand:
                    st = max(eng_free[e], rdy[i])
                    key = (st, i)
                    if best is None or key < best[0]:
                        best = (key, e, i)
            (st, i), e, _ = best
            ready[e].remove(i)
            heapq.heapify(ready[e])
            o = ops[i]
            dur = o[7]
            epred[i] = elast[e] if eng_free[e] > rdy[i] else -2
            elast[e] = i
            stt_[i] = st
            eng_free[e] = st + dur
            fin[i] = st + (o[8] if o[0] == "dma" else dur)
            order.append(i)
            remaining -= 1
            for s_ in succs[i]:
                if fin[i] + self.LAT > rdy[s_]:
                    rdy[s_] = fin[i] + self.LAT
                    crit[s_] = i
                indeg[s_] -= 1
                if indeg[s_] == 0:
                    heapq.heappush(ready[ops[s_][1]], s_)
        if os.environ.get('MK_CRIT') and n > 2000:
            i = max(range(n), key=lambda k: fin[k])
            chain = []
            while i >= 0:
                chain.append(i)
                i = epred[i] if epred[i] >= 0 else crit[i]
            agg = {}
            for k in chain:
                o = ops[k]
                key = (o[1], o[2] or 'dma', 'engwait' if epred[k] >= 0 else 'data')
                a = agg.setdefault(key, [0, 0.0]); a[0] += 1; a[1] += o[7]
            print('CRIT n=%d len=%d makespan=%.1f' % (n, len(chain), max(fin)))
            for key, a in sorted(agg.items(), key=lambda kv: -kv[1][1])[:14]:
                print('   ', key, a[0], round(a[1], 1))
        if os.environ.get('MK_VERBOSE'):
            busy = {}
            for o in ops:
                busy[o[1]] = busy.get(o[1], 0.0) + o[7]
            print('FLUSH n=%d makespan_us=%.1f busy=%s' % (n, max(fin) if not NOSCHED else -1, {k: round(v) for k, v in busy.items()}), flush=True)
        for i in order:
            o = ops[i]
            if o[0] == "op":
                self._emit_op(o[1], o[2], o[3], o[4], o[5], o[6])
            else:
                self._emit_dma(o[1], o[3][0], o[3][1], o[4], o[5], o[6])

    def _emit_op(self, eng, method, args, kw, r, w):
        self._deps(eng, r, w)
        ins = getattr(self.eng[eng], method)(*args, **kw)
        self.cnt[eng] += 1
        ins.then_inc(self.sem[eng], 1)
        self._commit(("e", eng, self.cnt[eng]), r, w)
        return ins

    def _emit_dma(self, q, out, in_, kw, r, w):
        d = self.dq[q]
        n = d["n"]
        slot = n % self.NDMA
        val = 16 * (n // self.NDMA + 1)
        if n >= self.NDMA:
            self._wait(q, ("d", q, slot, val - 16))
        self._deps(q, r, w)
        ins = self.eng[q].dma_start(out=out, in_=in_, **kw)
        ins.then_inc(d["sems"][slot], 16)
        d["n"] = n + 1
        self._commit(("d", q, slot, val), r, w)
        return ins

    def mm(self, out, lhsT, rhs, start=True, stop=True, **kw):
        return self.op("pe", "matmul", out, lhsT, rhs, start=start, stop=stop,
                       reads=[lhsT, rhs], writes=[out], **kw)

    def act(self, out, in_, func, bias=None, scale=None, accum_out=None):
        kw = {}
        rd = [in_]
        if bias is not None:
            kw["bias"] = bias
            rd.append(bias)
        if scale is not None:
            kw["scale"] = scale
            rd.append(scale)
        wr = [out]
        if accum_out is not None:
            kw["accum_out"] = accum_out
            wr.append(accum_out)
        return self.op("act", "activation", out, in_, func, reads=rd, writes=wr, **kw)

    def tt(self, out, in0, in1, op, eng="dve"):
        return self.op(eng, "tensor_tensor", out, in0, in1, op, reads=[in0, in1], writes=[out])

    def ts(self, out, in0, s1, s2, op0, op1=None, eng="dve"):
        kw = {}
        if op1 is not None:
            kw["op1"] = op1
        return self.op(eng, "tensor_scalar", out, in0, s1, s2, op0, reads=[in0, s1, s2], writes=[out], **kw)

    def stt(self, out, in0, scalar, in1, op0, op1, eng="dve"):
        return self.op(eng, "scalar_tensor_tensor", out, in0, scalar, in1, op0, op1,
                       reads=[in0, scalar, in1], writes=[out])

    def copy(self, out, in_, eng="dve"):
        if eng == "act":
            return self.act(out, in_, AF.Copy)
        return self.op(eng, "tensor_copy", out, in_, reads=[in_], writes=[out])

    def evac(self, out, in_):
        self.rr += 1
        return self.copy(out, in_, eng=("dve" if self.rr % 2 else "act"))

    def memset(self, ap, val, eng="dve"):
        return self.op(eng, "memset", ap, val, reads=[], writes=[ap])

    def scan(self, out, d0, d1, initial, op0=ALU.mult, op1=ALU.add):
        return self.op("dve", "tensor_tensor_scan", out, d0, d1, initial, op0, op1,
                       reads=[d0, d1, initial], writes=[out])

    def recip(self, out, in_):
        return self.op("dve", "reciprocal", out, in_, reads=[in_], writes=[out])

    def barrier(self):
        self.flush()
        for e in self.eng:
            for src in self.eng:
                if self.cnt[src] > 0:
                    self._wait(e, ("e", src, self.cnt[src]))
            for q, d in self.dq.items():
                n = d["n"]
                for slot in range(min(n, self.NDMA)):
                    last_n = ((n - 1 - slot) // self.NDMA) * self.NDMA + slot
                    self._wait(e, ("d", q, slot, 16 * (last_n // self.NDMA + 1)))


class Pool:
    def __init__(self, S):
        self.S = S
        self.es = ExitStack()

    def t(self, name, shape, dtype=F32):
        self.S.ntens += 1
        h = self.es.enter_context(self.S.nc.sbuf_tensor(f"{name}_{self.S.ntens}", list(shape), dtype))
        return h.ap() if hasattr(h, "ap") else h

    def close(self):
        self.S.barrier()
        self.es.close()


def dram_rows_bcast(t_ap, offset, n, parts=128):
    return bass.AP(t_ap.tensor, offset, [[0, parts], [1, n]])


def make_consts():
    c = {}
    c["ident"] = np.eye(128, dtype=np.float32)
    c["anti"] = np.eye(128, dtype=np.float32)[::-1].copy()
    c["anti64"] = np.eye(64, dtype=np.float32)[::-1].copy()
    c["anti32"] = np.eye(32, dtype=np.float32)[::-1].copy()
    sel = np.zeros((2, 2, 128), np.float32)
    sel[0, 0] = 1
    sel[1, 1] = 1
    c["sel"] = sel
    selm = np.zeros((2, 128), np.float32)
    selm[0, :64] = 1
    selm[1, 64:] = 1
    c["selm"] = selm
    j = np.arange(128)
    mf = (j[:, None] <= j[None, :]).astype(np.float32)
    c["maskf"] = mf
    c["maskb"] = mf.T.copy()
    tok = np.arange(NLAT)
    rowp = (tok // 64).astype(np.float32)
    colp = (tok % 64).astype(np.float32)
    pos = np.zeros((128, T), np.float32)
    freq = np.zeros((128, 1), np.float32)
    half = 16
    fr = 10000.0 ** (-np.arange(0, half, 2, dtype=np.float32) / half)
    for p in range(128):
        d = p % 32
        pos[p, NCTX:] = rowp if d < 16 else colp
        freq[p, 0] = fr[d % 8]
    c["pos"] = pos
    c["freq"] = freq
    hm = np.zeros((128, 4, 128), np.float32)
    bdm = np.zeros((128, 4, 64), np.float32)
    for h in range(4):
        hm[h * 32:(h + 1) * 32, h, :] = 1
        bdm[h * 32:(h + 1) * 32, h, :] = 1
    c["hm"] = hm
    c["bdm"] = bdm.reshape(128, 256)
    col = np.arange(64)
    cs = np.clip(col - 8, 0, 48)
    ok = (col[:, None] >= cs[None, :]) & (col[:, None] < cs[None, :] + 16)
    cm = np.where(ok, 0.0, -30000.0).astype(np.float32)
    c["colmask"] = np.concatenate([cm, cm], 0)
    c["negblk"] = np.full((128, 64), -30000.0, np.float32)
    s_idx = np.repeat(np.arange(8), 16)
    c["tmaskf"] = (s_idx[:, None] <= s_idx[None, :]).astype(np.float32)
    c["tmaskb"] = (s_idx[:, None] >= s_idx[None, :]).astype(np.float32)
    c["ciota"] = np.tile(np.arange(NCH, dtype=np.float32)[None, :], (128, 1))
    rc = np.zeros((4, PT_PAD), np.float32)
    for i, w in enumerate((2, 4, 8, 16)):
        for (n, off) in ((NCTX, PAD_CTX), (NLAT, PAD_LAT)):
            t = np.arange(n)
            lo = np.clip(t - w // 2, 0, n)
            hi = np.clip(t - w // 2 + w, 0, n)
            rc[i, off:off + n] = 1.0 / (hi - lo)
    c["poolrc"] = rc
    return c


PAD_CTX = 16
PAD_LAT = 16 + NCTX + 32
PT_PAD = PAD_LAT + NLAT + 16

CONST_SHAPES = None

WEIGHT_NAMES = ["w_mod", "b_mod", "g_pre", "g_post", "w_in", "w_out", "gla_w_gate", "gla_b_gate",
                "gla_g_norm", "na_rpb", "s5_lam_re", "s5_lam_im", "s5_log_dt", "s5_b_re", "s5_b_im",
                "s5_c_re", "s5_c_im", "s5_d", "s5_w_glu", "s5_b_glu", "pool_w", "pool_scale"]
WEIGHT_SHAPES = {
    "w_mod": (2, 1024, 3072), "b_mod": (2, 3072), "g_pre": (2, 1024), "g_post": (2, 1024),
    "w_in": (2, 1024, 2848), "w_out": (2, 1024, 1024), "gla_w_gate": (2, 2, 16, 128),
    "gla_b_gate": (2, 2, 128), "gla_g_norm": (2, 64), "na_rpb": (2, 4, 15, 31),
    "s5_lam_re": (2, 2, 16, 64), "s5_lam_im": (2, 2, 16, 64), "s5_log_dt": (2, 2, 16),
    "s5_b_re": (2, 2, 16, 64, 16), "s5_b_im": (2, 2, 16, 64, 16), "s5_c_re": (2, 2, 16, 16, 64),
    "s5_c_im": (2, 2, 16, 16, 64), "s5_d": (2, 256), "s5_w_glu": (2, 256, 256), "s5_b_glu": (2, 256),
    "pool_w": (2, 4, 64, 64), "pool_scale": (2, 256),
}

FM_BLOCKS = [
    (1184, 128, 0), (2848, 128, 128), (0, 128, 256), (2976, 128, 384), (384, 32, 512),
    (1312, 128, 544), (1440, 128, 672), (416, 128, 800), (544, 128, 928), (1568, 128, 1056), (1696, 128, 1184)]
PF_ROWS = 1312
TM_CHUNKS = [
    (128, 256, 0, False), (672, 256, 256, False), (928, 256, 512, False),
    (1824, 512, 768, True), (2336, 512, 1280, True)]
PT_COLS = 1792
NWCOL = 3104
GROUPS = [(0, 256)] + [(256 + 512 * i, 512) for i in range(8)]


class _Stop(Exception):
    pass


def build(debug=False, nlayers=2, stop_after=None):
    nc = bass.Bass("TRN2", target_bir_lowering=False)
    S = Sched(nc)

    def dram(name, shape, dtype=F32, kind="Internal"):
        return nc.dram_tensor(name, list(shape), dtype, kind=kind).ap()

    dbg_kind = "ExternalOutput" if debug else "Internal"
    xin = dram("xin", [T, D], kind="ExternalInput")
    cc = dram("cc", [128, 8, 2], kind="ExternalInput")
    W = {n: dram(n, WEIGHT_SHAPES[n], kind="ExternalInput") for n in WEIGHT_NAMES}
    consts = make_consts()
    C = {n: dram("k_" + n, v.shape, kind="ExternalInput") for n, v in consts.items()}
    out = dram("out", [NLAT, D], kind="ExternalOutput")
    xs = dram("xs", [T, D], kind=dbg_kind)
    PF = dram("PF", [PF_ROWS, T], kind=dbg_kind)
    PT = dram("PT", [T, PT_COLS], kind=dbg_kind)
    YS = dram("YS", [T, 1024], kind=dbg_kind)
    OG = dram("OG", [T, 256])
    OGB = dram("OGB", [T, 256])
    YSF = dram("YSF", [T, 256], kind=dbg_kind)
    YSB = dram("YSB", [T, 256], kind=dbg_kind)
    COS = dram("COS", [128, T])
    SIN = dram("SIN", [128, T])
    RP = dram("RP", [60, 160])

    PS = []
    for i in range(8):
        h = nc.alloc_psum_tensor(f"psum{i}", [128, 512], F32)
        PS.append(h.ap() if hasattr(h, "ap") else h)

    G = Pool(S)
    ident = G.t("ident", [128, 128]); S.dma(ident, C["ident"])
    identb = G.t("identb", [128, 128], BF16); S.copy(identb, ident)
    anti = G.t("anti", [128, 128]); S.dma(anti, C["anti"])
    antib = G.t("antib", [128, 128], BF16); S.copy(antib, anti)
    anti32 = G.t("anti32", [32, 32]); S.dma(anti32, C["anti32"])
    anti32b = G.t("anti32b", [32, 32], BF16); S.copy(anti32b, anti32)
    anti64 = G.t("anti64", [64, 64]); S.dma(anti64, C["anti64"])
    ones1 = G.t("ones1", [128, 1]); S.memset(ones1, 1.0)
    MOD = [G.t("mod0", [128, 3072]), G.t("mod1", [128, 3072])]
    gpost = G.t("gpost", [128, 1024])

    rr_cache = {}

    def range_reduce(P, out_s, out_c, ang, shape, slot=0):
        key = (id(P), tuple(shape), slot)
        if key not in rr_cache:
            rr_cache[key] = (P.t("rr_ki", shape, I32), P.t("rr_kf", shape), P.t("rr_ph", shape))
        ki, kf, ph = rr_cache[key]
        S.ts(ki, ang, 1.0 / TWO_PI, None, ALU.mult)
        S.copy(kf, ki)
        S.stt(ph, kf, -TWO_PI, ang, ALU.mult, ALU.add)
        S.ts(ph, ph, -PI, PI, ALU.max, ALU.min)
        S.act(out_s, ph, AF.Sin)
        S.act(kf, ph, AF.Sin, scale=0.5)
        S.act(kf, kf, AF.Square)
        S.act(out_c, kf, AF.Identity, bias=1.0, scale=-2.0)

    P = Pool(S)
    freq = P.t("freq", [128, 1]); S.dma(freq, C["freq"])
    for (t0, n) in [(0, 1088), (1088, 1088), (2176, 1088), (3264, 1088)]:
        pos = P.t("pos", [128, n]); S.dma(pos, C["pos"][:, t0:t0 + n])
        ang = P.t("ang", [128, n])
        S.ts(ang, pos, freq, None, ALU.mult)
        sn = P.t("sn", [128, n]); cs_ = P.t("cs", [128, n])
        range_reduce(P, sn, cs_, ang, [128, n])
        S.dma(SIN[:, t0:t0 + n], sn)
        S.dma(COS[:, t0:t0 + n], cs_)
    P.close()

    try:
      for l in range(nlayers):
        x_src = xin if l == 0 else xs
        last = (l == nlayers - 1) and not debug

        P = Pool(S)
        cst = P.t("cst", [128, 8, 2]); S.dma(cst, cc)
        css = P.t("css", [128, 8, 2]); S.act(css, cst, AF.Silu)
        wmb = [P.t("wm0", [128, 3072]), P.t("wm1", [128, 3072])]
        for k in range(8):
            wm = wmb[k % 2]
            S.dma(wm, W["w_mod"][l, k * 128:(k + 1) * 128, :], q=("sp" if k % 2 == 0 else "pool"))
            for n in range(6):
                S.mm(PS[n][0:2, :], css[:, k, :], wm[:, n * 512:(n + 1) * 512], start=(k == 0), stop=(k == 7))
        selt = P.t("selt", [2, 2, 128]); S.dma(selt, C["sel"])
        bm2 = [P.t("bm0", [2, 512]), P.t("bm1", [2, 512])]
        modr2 = [P.t("modr0", [2, 512]), P.t("modr1", [2, 512])]
        for n in range(6):
            bm = bm2[n % 2]; modr = modr2[n % 2]
            S.dma(bm, bass.AP(W["b_mod"].tensor, l * 3072 + n * 512, [[0, 2], [1, 512]]))
            S.tt(modr, PS[n][0:2, :], bm, ALU.add)
            for r in range(2):
                pb = PS[6 + r]
                S.mm(pb, selt[:, r, :], modr)
                S.evac(MOD[r][:, n * 512:(n + 1) * 512], pb)
        gpre = P.t("gpre", [128, 1024])
        S.dma(gpre, dram_rows_bcast(W["g_pre"], l * 1024, 1024))
        S.dma(gpost, dram_rows_bcast(W["g_post"], l * 1024, 1024))
        for r in range(2):
            S.stt(MOD[r][:, 1024:2048], MOD[r][:, 1024:2048], 1.0, gpre, ALU.add, ALU.mult)
        PM = P
        P = Pool(S)
        Wb = P.t("Wb", [128, 8, NWCOL], BF16)
        wst = [P.t("wst0", [128, 2848]), P.t("wst1", [128, 2848])]
        for k in range(8):
            st = wst[k % 2]
            S.dma(st, W["w_in"][l, k * 128:(k + 1) * 128, :], q=("sp" if k % 2 == 0 else "pool"))
            S.copy(Wb[:, k, 0:1424], st[:, 0:1424], eng="dve")
            S.copy(Wb[:, k, 1424:2848], st[:, 1424:2848], eng="act")
            for (c0, r0) in ((1184, 2848), (0, 2976)):
                sv = st[:, c0:c0 + 128].rearrange("p (a t e) -> p a t e", a=8, t=2, e=8)
                dv = Wb[:, k, r0:r0 + 128].rearrange("p (a t e) -> p a t e", a=8, t=2, e=8)
                S.ts(dv[:, :, 0, :], sv[:, :, 1, :], -1.0, None, ALU.mult, eng="pool")
                S.copy(dv[:, :, 1, :], sv[:, :, 0, :], eng="pool")
        xt_b = [P.t("xt0", [128, 1024]), P.t("xt1", [128, 1024])]
        junk = P.t("junk", [128, 1024])
        h32 = P.t("h32", [128, 1024])
        hb_b = [P.t("hb0", [128, 1024], BF16), P.t("hb1", [128, 1024], BF16)]
        hT_b = [P.t("hT0", [128, 8, 512], BF16), P.t("hT1", [128, 8, 512], BF16)]
        fst_b = [P.t(f"fst{i}", [128, 512]) for i in range(4)]
        ropec = [P.t("ropec0", [128, 512]), P.t("ropec1", [128, 512])]
        ropes = [P.t("ropes0", [128, 512]), P.t("ropes1", [128, 512])]
        tst_b = [P.t("tst0", [128, PT_COLS]), P.t("tst1", [128, PT_COLS])]
        stat = [P.t("stat0", [128, 4]), P.t("stat1", [128, 4])]
        nt = 0
        nf = 0
        for gi, (tok0, n) in enumerate(GROUPS):
            hT = hT_b[gi % 2]
            r = 0 if tok0 >= NCTX else 1
            for ti in range(n // 128):
                xt = xt_b[nt % 2]; hb = hb_b[nt % 2]; sv_ = stat[nt % 2]
                S.dma(xt, x_src[tok0 + ti * 128: tok0 + (ti + 1) * 128, :], q="sp")
                S.act(junk, xt, AF.Square, accum_out=sv_[:, 0:1])
                S.ts(sv_[:, 1:2], sv_[:, 0:1], 1.0 / D, EPS, ALU.mult, ALU.add)
                S.act(sv_[:, 2:3], sv_[:, 1:2], AF.Sqrt)
                S.recip(sv_[:, 3:4], sv_[:, 2:3])
                S.stt(h32, xt, sv_[:, 3:4], MOD[r][:, 1024:2048], ALU.mult, ALU.mult)
                S.tt(hb, h32, MOD[r][:, 0:1024], ALU.add, eng="pool")
                for half in range(2):
                    pt_ = PS[half]
                    for kk in range(4):
                        k = half * 4 + kk
                        S.mm(pt_[:, kk * 128:(kk + 1) * 128], hb[:, k * 128:(k + 1) * 128], identb)
                    S.evac(hT[:, half * 4:(half + 1) * 4, ti * 128:(ti + 1) * 128],
                           pt_.rearrange("p (a b) -> p a b", a=4))
                nt += 1
            rc_ = ropec[gi % 2]; rs_ = ropes[gi % 2]
            S.dma(rc_[:, 0:n], COS[:, tok0:tok0 + n], q="sp")
            S.dma(rs_[:, 0:n], SIN[:, tok0:tok0 + n], q="sp")
            held = None
            for bi, (wc, ncol, prow) in enumerate(FM_BLOCKS):
                pb = PS[2 + (bi % 3)]
                for k in range(8):
                    S.mm(pb[0:ncol, 0:n], Wb[:, k, wc:wc + ncol], hT[:, k, 0:n], start=(k == 0), stop=(k == 7))
                fs = fst_b[nf % 4]; nf += 1
                S.evac(fs[0:ncol, 0:n], pb[0:ncol, 0:n])
                if bi in (0, 2):
                    held = (fs, prow)
                    continue
                if bi in (1, 3):
                    f0, prow0 = held
                    S.tt(f0[:, 0:n], f0[:, 0:n], rc_[:, 0:n], ALU.mult, eng="pool")
                    S.tt(fs[:, 0:n], fs[:, 0:n], rs_[:, 0:n], ALU.mult, eng="pool")
                    S.tt(f0[:, 0:n], f0[:, 0:n], fs[:, 0:n], ALU.add)
                    S.dma(PF[prow0:prow0 + 128, tok0:tok0 + n], f0[:, 0:n], q="act")
                    continue
                S.dma(PF[prow:prow + ncol, tok0:tok0 + n], fs[0:ncol, 0:n], q="act")
            for ti in range(n // 128):
                ts_ = tst_b[ti % 2]
                for ci, (wc, ncol, pcol, silu) in enumerate(TM_CHUNKS):
                    pb = PS[5 + (ci % 3)]
                    for k in range(8):
                        S.mm(pb[:, 0:ncol], hT[:, k, ti * 128:(ti + 1) * 128], Wb[:, k, wc:wc + ncol],
                             start=(k == 0), stop=(k == 7))
                    if silu:
                        S.act(ts_[:, pcol:pcol + ncol], pb[:, 0:ncol], AF.Silu)
                    else:
                        S.copy(ts_[:, pcol:pcol + ncol], pb[:, 0:ncol], eng="dve")
                S.dma(PT[tok0 + ti * 128: tok0 + (ti + 1) * 128, :], ts_, q="act")
        P.close()
        PM.close()

        if stop_after == 'A':
            raise _Stop()
        P = Pool(S)
        zt = P.t("zt", [60, 160]); S.memset(zt, 0.0)
        S.dma(RP, zt)
        S.dma(RP[:, 64:95], W["na_rpb"][l].rearrange("h r c -> (h r) c"))
        Gall = P.t("Gall", [64, 60, 2, 64])
        for dup in range(2):
            S.dma(Gall[:, :, dup, :], bass.AP(RP.tensor, 16, [[1, 64], [160, 60], [1, 64]]))
        colm = P.t("colm", [128, 64]); S.dma(colm, C["colmask"])
        a64 = anti64
        BT = P.t("BT", [128, 4, 15, 64])
        for h in range(4):
            for r8 in range(0, 15, 8):
                nr = min(8, 15 - r8)
                pb = PS[(h * 2 + r8 // 8) % 4]
                for j in range(nr):
                    ro = r8 + j
                    S.mm(pb[:, j * 64:(j + 1) * 64], Gall[:, h * 15 + ro].rearrange("p a b -> p (a b)"), a64)
                for j in range(nr):
                    ro = r8 + j
                    S.tt(BT[:, h, 14 - ro, :], pb[:, j * 64:(j + 1) * 64], colm, ALU.add)
        negblk = P.t("negblk", [128, 64]); S.dma(negblk, C["negblk"])
        NCOMP = 40
        comp_tiles = [P.t(f"comp{i}", [128, 128], BF16) for i in range(NCOMP)]
        comp_map = {}

        def get_comp(h, blocks):
            key = (h, blocks)
            if key in comp_map:
                return comp_map[key]
            idx = len(comp_map)
            assert idx < NCOMP
            tl = comp_tiles[idx]
            for (a, b_), (valid, ro) in zip(((0, 0), (0, 1), (1, 0), (1, 1)), blocks):
                dst = tl[a * 64:(a + 1) * 64, b_ * 64:(b_ + 1) * 64]
                if valid:
                    S.copy(dst, BT[a * 64:(a + 1) * 64, h, 14 - ro, :], eng="pool")
                else:
                    S.copy(dst, negblk[a * 64:(a + 1) * 64, :], eng="pool")
            comp_map[key] = tl
            return tl

        KT = P.t("KT", [128, 2, T], BF16)
        QT = P.t("QT", [128, 2, T], BF16)
        Vb = P.t("Vb", [128, NTILE, 4, 65], BF16)
        S.memset(Vb.rearrange("p a b c -> p (a b c)"), 1.0, eng="pool")
        ldq = [P.t("ldq0", [128, 1088]), P.t("ldq1", [128, 1088])]
        nl = 0
        for c2 in range(2):
            for t0 in range(0, T, 1088):
                b = ldq[nl % 2]; nl += 1
                S.dma(b, PF[544 + c2 * 128: 544 + (c2 + 1) * 128, t0:t0 + 1088], q="sp")
                S.ts(QT[:, c2, t0:t0 + 1088], b, 0.125, None, ALU.mult)
                b = ldq[nl % 2]; nl += 1
                S.dma(b, PF[800 + c2 * 128: 800 + (c2 + 1) * 128, t0:t0 + 1088], q="sp")
                S.copy(KT[:, c2, t0:t0 + 1088], b, eng="act")
        ldv = [P.t("ldv0", [128, 256]), P.t("ldv1", [128, 256])]
        for ti in range(NTILE):
            b = ldv[ti % 2]
            S.dma(b, PT[ti * 128:(ti + 1) * 128, 256:512], q="sp")
            S.copy(Vb[:, ti, :, 0:64], b.rearrange("p (a b) -> p a b", a=4), eng=("dve" if ti % 2 == 0 else "act"))
        Pb = [P.t(f"Pb{i}", [128, 7, 128], BF16) for i in range(3)]
        on_ = [P.t("on0", [128, 4, 65]), P.t("on1", [128, 4, 65])]
        yn = [P.t("yn0", [128, 256]), P.t("yn1", [128, 256])]
        rcn = [P.t("rcn0", [128, 4, 1]), P.t("rcn1", [128, 4, 1])]
        kt_cache = {}

        def na_ktiles(qt):
            if qt in kt_cache:
                return kt_cache[qt]
            if qt < 2:
                ktiles = [(0, None), (1, None)]
            else:
                r0 = (qt - 2) * 2
                rows_needed = set()
                for b_ in range(2):
                    stt_ = min(max(r0 + b_ - 4, 0), 56)
                    rows_needed.update(range(stt_, stt_ + 8))
                kts = sorted(set(r // 2 for r in rows_needed))
                ktiles = []
                for kt in kts:
                    blocks = []
                    for a_ in range(2):
                        for b_ in range(2):
                            krow = kt * 2 + a_; qrow = r0 + b_
                            stt_ = min(max(qrow - 4, 0), 56)
                            valid = stt_ <= krow < stt_ + 8
                            blocks.append((valid, krow - qrow + 7))
                    ktiles.append((kt + 2, tuple(blocks)))
                ktiles += [(0, None), (1, None)]
            kt_cache[qt] = ktiles
            return ktiles

        items = [(qt, h) for qt in range(2 if last else 0, NTILE) for h in range(4)]

        def na_scores(n):
            qt, h = items[n]
            ktiles = na_ktiles(qt); nk = len(ktiles)
            c2 = h // 2; pp = (h % 2) * 64
            psA = PS[(n % 2) * 2]; psB = PS[(n % 2) * 2 + 1]
            pbuf = Pb[n % 3]
            for idx, (kt, blocks) in enumerate(ktiles):
                pdst = (psA if idx < 4 else psB)[:, (idx % 4) * 128:(idx % 4 + 1) * 128]
                S.mm(pdst, KT[pp:pp + 64, c2, kt * 128:(kt + 1) * 128], QT[pp:pp + 64, c2, qt * 128:(qt + 1) * 128],
                     start=True, stop=(blocks is None))
                if blocks is not None:
                    S.mm(pdst, identb, get_comp(h, blocks), start=False, stop=True)
            n1 = min(nk, 4)
            S.act(pbuf[:, 0:n1, :], psA[:, 0:n1 * 128].rearrange("p (a b) -> p a b", a=n1), AF.Exp)
            if nk > 4:
                S.act(pbuf[:, 4:nk, :], psB[:, 0:(nk - 4) * 128].rearrange("p (a b) -> p a b", a=nk - 4), AF.Exp)

        def na_pv(n):
            qt, h = items[n]
            ktiles = na_ktiles(qt); nk = len(ktiles)
            pO = PS[4 + (qt % 2)]
            pbuf = Pb[n % 3]
            for idx, (kt, blocks) in enumerate(ktiles):
                S.mm(pO[:, h * 65:(h + 1) * 65], pbuf[:, idx, :], Vb[:, kt, h, :], start=(idx == 0), stop=(idx == nk - 1))
            if h == 3:
                ob = on_[qt % 2]
                S.evac(ob, pO[:, 0:260].rearrange("p (a b) -> p a b", a=4))
                S.recip(rcn[qt % 2], ob[:, :, 64:65])
                S.tt(yn[qt % 2].rearrange("p (a b) -> p a b", a=4), ob[:, :, 0:64],
                     rcn[qt % 2].to_broadcast([128, 4, 64]), ALU.mult)
                S.dma(YS[qt * 128:(qt + 1) * 128, 256:512], yn[qt % 2], q="act")

        for step in range(len(items) + 1):
            if step < len(items):
                na_scores(step)
            if step >= 1:
                na_pv(step - 1)
        PP = Pool(S)
        wpl = PP.t("wpl", [128, 2, 64]); wplb = PP.t("wplb", [128, 2, 64], BF16)
        S.dma(wpl, W["pool_w"][l].rearrange("(a b) c e -> (b c) a e", b=2))
        S.copy(wplb, wpl)
        pscale = PP.t("pscale", [128, 256]); S.dma(pscale, dram_rows_bcast(W["pool_scale"], l * 256, 256))
        HALO = 32
        SEGW = PAD_LAT + 2048 + HALO
        segs = [(0, SEGW, [(PAD_CTX, 0, NCTX), (PAD_LAT, NCTX, NCTX + 2048 + HALO)], list(range(0, 18))),
                (PAD_LAT + 2048 - HALO, PT_PAD - (PAD_LAT + 2048 - HALO), [(0, NCTX + 2048 - HALO, T)], list(range(18, NTILE)))]
        pU = PP.t("pU", [128, SEGW]); prc = PP.t("prc", [128, SEGW])
        psA_ = PP.t("psA", [128, SEGW]); psB_ = PP.t("psB", [128, SEGW])
        pdb = [PP.t("pdb0", [128, SEGW], BF16), PP.t("pdb1", [128, SEGW], BF16)]
        pst = [PP.t("pst0", [128, 256]), PP.t("pst1", [128, 256])]
        for (seg0, seglen, pieces, tiles_) in segs:
            for tl in range(2):
                U = pU; rc = prc; sA = psA_; sB = psB_
                S.memset(U, 0.0, eng="pool")
                for (loff, c0, c1) in pieces:
                    S.dma(U[:, loff:loff + (c1 - c0)], PF[1056 + tl * 128: 1056 + (tl + 1) * 128, c0:c1], q="sp")
                for hh in range(2):
                    S.dma(rc[hh * 64:(hh + 1) * 64, 0:seglen],
                          dram_rows_bcast(C["poolrc"], (tl * 2 + hh) * PT_PAD + seg0, seglen, parts=64), q="sp")
                S.memset(sA, 0.0, eng="pool"); S.memset(sB, 0.0, eng="pool")
                L0, L1 = 12, seglen - 12
                fins = [None, None]
                for hh in range(2):
                    w = (2, 4, 8, 16)[tl * 2 + hh]
                    ps_ = slice(hh * 64, (hh + 1) * 64)
                    eng = "dve" if hh == 0 else "pool"
                    S.tt(sA[ps_, L0:L1], U[ps_, L0 - 1:L1 - 1], U[ps_, L0:L1], ALU.add, eng=eng)
                    cur, oth = sA, sB
                    sh = 1
                    ww = 2
                    while ww < w:
                        S.tt(oth[ps_, L0:L1], cur[ps_, L0 - sh:L1 - sh], cur[ps_, L0 + sh:L1 + sh], ALU.add, eng=eng)
                        cur, oth = oth, cur
                        sh *= 2
                        ww *= 2
                    S.tt(oth[ps_, 0:seglen], cur[ps_, 0:seglen], rc[ps_, 0:seglen], ALU.mult, eng=eng)
                    S.tt(oth[ps_, 0:seglen], oth[ps_, 0:seglen], U[ps_, 0:seglen], ALU.subtract, eng=eng)
                    fins[hh] = oth
                S.copy(pdb[tl][0:64, 0:seglen], fins[0][0:64, 0:seglen], eng="act")
                S.copy(pdb[tl][64:128, 0:seglen], fins[1][64:128, 0:seglen], eng="act")
            for ti in tiles_:
                if last and ti < 2:
                    continue
                goff = (PAD_CTX + ti * 128) if ti < 2 else (PAD_LAT + (ti - 2) * 128)
                off = goff - seg0
                pbs = (PS[6], PS[7])
                for i in range(4):
                    tl, hh = i // 2, i % 2
                    ps_ = slice(hh * 64, (hh + 1) * 64)
                    S.mm(pbs[hh][:, tl * 64:(tl + 1) * 64], pdb[tl][ps_, off:off + 128], wplb[ps_, tl, :])
                for hh in range(2):
                    S.tt(pst[ti % 2].rearrange("p (tl hh e) -> p tl hh e", tl=2, hh=2)[:, :, hh, :],
                         pbs[hh][:, 0:128].rearrange("p (tl e) -> p tl e", tl=2),
                         pscale.rearrange("p (tl hh e) -> p tl hh e", tl=2, hh=2)[:, :, hh, :], ALU.mult)
                S.dma(YS[ti * 128:(ti + 1) * 128, 768:1024], pst[ti % 2], q="act")
        PP.close()
        P.close()

        if stop_after == 'N':
            raise _Stop()
        PU = Pool(S)
        UTf = PU.t("UTf", [128, 16, NCH], BF16)
        UTb = PU.t("UTb", [128, 16, NCH], BF16)
        P = Pool(S)
        hm = P.t("hm", [128, 4, 128]); S.dma(hm, C["hm"])
        hmb = P.t("hmb", [128, 4, 128], BF16); S.copy(hmb, hm)
        bdm = P.t("bdm", [128, 256]); S.dma(bdm, C["bdm"])
        maskt = [P.t("maskf", [128, 128]), P.t("maskb", [128, 128])]
        S.dma(maskt[0], C["maskf"]); S.dma(maskt[1], C["maskb"])
        gnorm = P.t("gnorm", [128, 4, 64])
        for h in range(4):
            S.dma(gnorm[:, h, :], dram_rows_bcast(W["gla_g_norm"], l * 64, 64))
        OGd = [OG, OGB]
        for d in range(2):
            wg = P.t("wg", [16, 128]); negb = P.t("negb", [128, 1])
            Sbd = P.t("Sbd", [128, 256]); Sbdb = P.t("Sbdb", [128, 256], BF16); stmp = P.t("stmp", [128, 256])
            glr = P.t("glr", [16, 512])
            qr2 = [P.t("qr0", [128, 512]), P.t("qr1", [128, 512])]
            kr2 = [P.t("kr0", [128, 512]), P.t("kr1", [128, 512])]
            e1 = P.t("e1", [128, 512]); sp_ = P.t("sp", [128, 512]); cs_ = P.t("cs", [128, 512]); cb = P.t("cb", [128, 512])
            EQ = P.t("EQ", [128, 512]); EK = P.t("EK", [128, 512]); EH = P.t("EH", [128, 512])
            tots2 = [P.t("tots0", [128, 4, 3]), P.t("tots1", [128, 4, 3])]
            qtb2 = [P.t("qtb0", [128, 512], BF16), P.t("qtb1", [128, 512], BF16)]
            ktb2 = [P.t("ktb0", [128, 512], BF16), P.t("ktb1", [128, 512], BF16)]
            khb2 = [P.t("khb0", [128, 512], BF16), P.t("khb1", [128, 512], BF16)]
            Qbd = [P.t("Qbd0", [128, 4, 128], BF16), P.t("Qbd1", [128, 4, 128], BF16)]
            attm = [P.t("attm0", [128, 4, 128], BF16), P.t("attm1", [128, 4, 128], BF16)]
            vt = [P.t("vt0", [128, 256]), P.t("vt1", [128, 256])]
            vb = [P.t("vb0", [128, 256], BF16), P.t("vb1", [128, 256], BF16)]
            khT = [P.t("khT0", [128, 128], BF16), P.t("khT1", [128, 128], BF16)]
            osb = [P.t("osb0", [128, 256]), P.t("osb1", [128, 256])]
            ps_att = PS[1 + d]; ps_po = PS[3 + d]; ps_st = PS[6 + d]
            S.dma(wg, W["gla_w_gate"][l, d])
            S.dma(negb, bass.AP(W["gla_b_gate"].tensor, (l * 2 + d) * 128, [[1, 128], [1, 1]]))
            S.ts(negb, negb, -1.0, None, ALU.mult)
            S.memset(Sbd, 0.0); S.memset(Sbdb, 0.0)
            gorder = list(range(9)) if d == 0 else [0] + list(range(8, 0, -1))
            nck = 0
            for gix, gi in enumerate(gorder):
                tots = tots2[gix % 2]; qtb = qtb2[gix % 2]; ktb = ktb2[gix % 2]; khb = khb2[gix % 2]
                tok0, n = GROUPS[gi]
                ncg = n // 128
                qr = qr2[gix % 2]; kr = kr2[gix % 2]
                S.dma(qr[:, 0:n], PF[0:128, tok0:tok0 + n], q="sp")
                S.dma(kr[:, 0:n], PF[256:384, tok0:tok0 + n], q="sp")
                S.dma(glr[:, 0:n], PF[512 + 16 * d: 528 + 16 * d, tok0:tok0 + n], q="sp")
                S.mm(PS[0][:, 0:n], wg, glr[:, 0:n])
                S.act(e1[:, 0:n], PS[0][:, 0:n], AF.Exp, bias=negb, scale=-1.0)
                S.act(sp_[:, 0:n], e1[:, 0:n], AF.Ln, bias=1.0)
                for c in range(ncg):
                    sl = slice(c * 128, (c + 1) * 128)
                    S.scan(cs_[:, sl], ones1.to_broadcast([128, 128]), sp_[:, sl], 0.0)
                lastc = cs_[:, 0:n].rearrange("p (c k) -> p c k", k=128)[:, :, 127]
                S.ts(tots[:, 0:ncg, 0], lastc, -1.0 / 16, None, ALU.mult)
                S.ts(tots[:, 0:ncg, 1], lastc, 1.0 / 16, None, ALU.mult)
                S.act(tots[:, 0:ncg, 2], tots[:, 0:ncg, 0], AF.Exp)
                if d == 0:
                    S.act(EQ[:, 0:n], cs_[:, 0:n], AF.Exp, scale=-1.0 / 16)
                    S.act(EK[:, 0:n], cs_[:, 0:n], AF.Exp, scale=1.0 / 16)
                    for c in range(ncg):
                        sl = slice(c * 128, (c + 1) * 128)
                        S.act(EH[:, sl], cs_[:, sl], AF.Exp, scale=1.0 / 16, bias=tots[:, c, 0:1])
                else:
                    S.tt(cb[:, 0:n], cs_[:, 0:n], sp_[:, 0:n], ALU.subtract)
                    S.act(EH[:, 0:n], cb[:, 0:n], AF.Exp, scale=-1.0 / 16)
                    for c in range(ncg):
                        sl = slice(c * 128, (c + 1) * 128)
                        S.act(EQ[:, sl], cb[:, sl], AF.Exp, scale=1.0 / 16, bias=tots[:, c, 0:1])
                        S.act(EK[:, sl], cb[:, sl], AF.Exp, scale=-1.0 / 16, bias=tots[:, c, 1:2])
                S.stt(qtb[:, 0:n], qr[:, 0:n], 32.0 ** -0.5, EQ[:, 0:n], ALU.mult, ALU.mult)
                S.tt(ktb[:, 0:n], kr[:, 0:n], EK[:, 0:n], ALU.mult, eng="pool")
                S.tt(khb[:, 0:n], kr[:, 0:n], EH[:, 0:n], ALU.mult, eng="pool")
                corder = list(range(ncg)) if d == 0 else list(range(ncg - 1, -1, -1))
                for c in corder:
                    sl = slice(c * 128, (c + 1) * 128)
                    tk = tok0 + c * 128
                    b = nck % 2; nck += 1
                    S.dma(vt[b], PT[tk:tk + 128, 0:256], q="sp")
                    S.copy(vb[b], vt[b], eng="act")
                    S.tt(Qbd[b], qtb[:, sl].unsqueeze(1).to_broadcast([128, 4, 128]), hmb, ALU.mult, eng="pool")
                    S.mm(ps_att, ktb[:, sl], Qbd[b].rearrange("p a b -> p (a b)"))
                    S.tt(attm[b], ps_att.rearrange("p (a b) -> p a b", a=4),
                         maskt[d].unsqueeze(1).to_broadcast([128, 4, 128]), ALU.mult)
                    S.mm(ps_po[:, 0:256], qtb[:, sl], Sbdb, start=True, stop=False)
                    for h in range(4):
                        S.mm(ps_po[:, h * 64:(h + 1) * 64], attm[b][:, h, :], vb[b][:, h * 64:(h + 1) * 64],
                             start=False, stop=(h == 3))
                    S.mm(PS[5][:, 0:128], khb[:, sl], identb)
                    S.copy(khT[b], PS[5][:, 0:128], eng="act")
                    S.mm(ps_st[:, 0:256], khT[b], vb[b])
                    S.tt(stmp, ps_st[:, 0:256], bdm, ALU.mult)
                    S.stt(Sbd, Sbd, tots[:, c, 2:3], stmp, ALU.mult, ALU.add)
                    S.copy(Sbdb, Sbd, eng="act")
                    S.copy(osb[b], ps_po[:, 0:256], eng="act")
                    S.dma(OGd[d][tk:tk + 128, :], osb[b], q="act")
        NCB = 3
        cf = [P.t(f"cf{i}", [128, 256]) for i in range(NCB)]
        cbw = [P.t(f"cbw{i}", [128, 256]) for i in range(NCB)]
        csq = [P.t(f"csq{i}", [128, 256]) for i in range(NCB)]
        chs = [P.t(f"chs{i}", [128, 4, 3]) for i in range(NCB)]
        for ti in range(2 if last else 0, NTILE):
            b = ti % NCB
            tsl = slice(ti * 128, (ti + 1) * 128)
            S.dma(cf[b], OG[tsl, :], q="sp")
            S.dma(cbw[b], OGB[tsl, :], q="sp")
            ob = cf[b]; hst = chs[b]
            S.tt(ob, ob, cbw[b], ALU.add, eng="pool")
            S.act(csq[b], ob, AF.Square)
            S.op("dve", "tensor_reduce", hst[:, :, 0], csq[b].rearrange("p (a b) -> p a b", a=4), AX.X, ALU.add,
                 reads=[csq[b]], writes=[hst])
            S.ts(hst[:, :, 1], hst[:, :, 0], 1.0 / 64, EPS, ALU.mult, ALU.add)
            S.act(hst[:, :, 2], hst[:, :, 1], AF.Sqrt)
            S.recip(hst[:, :, 1], hst[:, :, 2])
            o3 = ob.rearrange("p (a b) -> p a b", a=4)
            S.tt(o3, o3, hst[:, :, 1:2].to_broadcast([128, 4, 64]), ALU.mult)
            S.tt(o3, o3, gnorm, ALU.mult, eng="pool")
            S.dma(YS[tsl, 0:256], ob, q="act")
        ublk = [P.t("ublk0", [128, 8, 256]), P.t("ublk1", [128, 8, 256])]
        ublb = [P.t("ublb0", [128, 16, 128], BF16), P.t("ublb1", [128, 16, 128], BF16)]
        blocks = [(0, 32, 0)]
        cpos = NCH_CTX
        while cpos < NCH:
            nb = min(128, NCH - cpos)
            blocks.append((cpos, nb, NCH_CTX + (NCH - (cpos + nb))))
            cpos += nb
        for bi, (c0, nb, bp) in enumerate(blocks):
            ub = ublk[bi % 2]; ubb = ublb[bi % 2]
            S.dma(ub[0:nb], PT[c0 * 8:(c0 + nb) * 8, 512:768].rearrange("(c s) f -> c s f", s=8), q="sp")
            S.copy(ubb[0:nb].rearrange("c g (s h) -> c g s h", s=8), ub[0:nb].rearrange("c s (g h) -> c g s h", g=16))
            for (UTx, perm, pos0) in ((UTf, identb, c0), (UTb, antib, bp)):
                if perm is identb:
                    pm = identb[0:nb, 0:nb]
                else:
                    pm = antib if nb == 128 else anti32b
                for g4 in range(4):
                    pb = PS[0] if g4 % 2 == 0 else PS[5]
                    for gg in range(4):
                        g = g4 * 4 + gg
                        S.mm(pb[:, gg * 128: gg * 128 + nb], ubb[0:nb, g, :], pm)
                    S.evac(UTx[:, g4 * 4:(g4 + 1) * 4, pos0:pos0 + nb],
                           pb.rearrange("p (a b) -> p a b", a=4)[:, :, 0:nb])
        P.close()

        if stop_after == 'G':
            raise _Stop()
        P0 = Pool(S)
        Toep = P0.t("Toep", [128, 2, 16, 128], BF16)
        WstR = P0.t("WstR", [128, 16, 128], BF16)
        WstI = P0.t("WstI", [128, 16, 128], BF16)
        CdR = P0.t("CdR", [128, 16, 128], BF16)
        CdI = P0.t("CdI", [128, 16, 128], BF16)
        th8 = P0.t("th8", [128, 16]); r8t = P0.t("r8t", [128, 16])
        P = Pool(S)
        lamL = P.t("lamL", [16, 2, 128])
        S.dma(lamL[:, 0, :], W["s5_lam_re"][l].rearrange("d (gp m) p -> (d gp) (m p)", m=2))
        S.dma(lamL[:, 1, :], W["s5_lam_im"][l].rearrange("d (gp m) p -> (d gp) (m p)", m=2))
        i16 = ident[0:16, 0:16]
        lam = P.t("lam", [128, 2, 16])
        for ri in range(2):
            S.mm(PS[0][:, ri * 16:(ri + 1) * 16], lamL[:, ri, :], i16)
        S.evac(lam, PS[0][:, 0:32].rearrange("p (a b) -> p a b", a=2))
        ld = P.t("ld", [2, 2, 8])
        S.dma(ld, W["s5_log_dt"][l].rearrange("d (gp m) -> m d gp", m=2), allow_slow_non_contiguous=True)
        selm = P.t("selm", [2, 128]); S.dma(selm, C["selm"])
        S.mm(PS[1][:, 0:16], selm, ld.rearrange("m d g -> m (d g)"))
        dt_ = P.t("dt", [128, 16]); S.act(dt_, PS[1][:, 0:16], AF.Exp)
        sc = {}
        for nm in ["lrd", "mag", "imag", "ang", "sn", "cs", "lbr", "lbi", "ilr", "ili", "den", "rden",
                   "nr", "t1", "t2", "cor", "coi", "th8", "r8"]:
            sc[nm] = P.t("s5_" + nm, [128, 16])
        S.tt(sc["lrd"], lam[:, 0, :], dt_, ALU.mult)
        S.act(sc["mag"], sc["lrd"], AF.Exp)
        S.act(sc["imag"], sc["lrd"], AF.Exp, scale=-1.0)
        S.act(sc["r8"], sc["lrd"], AF.Exp, scale=8.0)
        S.tt(sc["ang"], lam[:, 1, :], dt_, ALU.mult)
        range_reduce(P, sc["sn"], sc["cs"], sc["ang"], [128, 16])
        S.tt(sc["lbr"], sc["mag"], sc["cs"], ALU.mult)
        S.tt(sc["lbi"], sc["mag"], sc["sn"], ALU.mult)
        S.tt(sc["ilr"], sc["imag"], sc["cs"], ALU.mult)
        S.stt(sc["ili"], sc["imag"], -1.0, sc["sn"], ALU.mult, ALU.mult)
        S.tt(sc["den"], lam[:, 0, :], lam[:, 0, :], ALU.mult)
        S.tt(sc["t1"], lam[:, 1, :], lam[:, 1, :], ALU.mult)
        S.tt(sc["den"], sc["den"], sc["t1"], ALU.add)
        S.recip(sc["rden"], sc["den"])
        S.ts(sc["nr"], sc["lbr"], -1.0, None, ALU.add)
        S.tt(sc["t1"], sc["nr"], lam[:, 0, :], ALU.mult)
        S.tt(sc["t2"], sc["lbi"], lam[:, 1, :], ALU.mult)
        S.tt(sc["t1"], sc["t1"], sc["t2"], ALU.add)
        S.tt(sc["cor"], sc["t1"], sc["rden"], ALU.mult)
        S.tt(sc["t1"], sc["lbi"], lam[:, 0, :], ALU.mult)
        S.tt(sc["t2"], sc["nr"], lam[:, 1, :], ALU.mult)
        S.tt(sc["t1"], sc["t1"], sc["t2"], ALU.subtract)
        S.tt(sc["coi"], sc["t1"], sc["rden"], ALU.mult)
        kq = P.t("s5_kq", [128, 16], I32); kqf = P.t("s5_kqf", [128, 16])
        S.ts(kq, sc["ang"], 8.0 / TWO_PI, None, ALU.mult)
        S.copy(kqf, kq)
        S.ts(sc["t1"], sc["ang"], 8.0, None, ALU.mult)
        S.stt(sc["th8"], kqf, -TWO_PI, sc["t1"], ALU.mult, ALU.add)
        pwr = P.t("pwr", [128, 16, 9]); pwi = P.t("pwi", [128, 16, 9])
        ipr = P.t("ipr", [128, 16, 9]); ipi = P.t("ipi", [128, 16, 9])
        for (ar, ai, br, bi, eng_) in ((pwr, pwi, sc["lbr"], sc["lbi"], "dve"), (ipr, ipi, sc["ilr"], sc["ili"], "pool")):
            q1 = P.t("pwq1", [128, 16, 4]); q2 = P.t("pwq2", [128, 16, 4])
            S.memset(ar[:, :, 0], 1.0, eng=eng_); S.memset(ai[:, :, 0], 0.0, eng=eng_)
            S.copy(ar[:, :, 1], br, eng=eng_); S.copy(ai[:, :, 1], bi, eng=eng_)
            for (lo, cnt, kk) in ((2, 1, 1), (3, 2, 2), (5, 4, 4)):
                xr_s = ar[:, :, 1:1 + cnt]; xi_s = ai[:, :, 1:1 + cnt]
                yr_s = ar[:, :, kk:kk + 1].to_broadcast([128, 16, cnt]); yi_s = ai[:, :, kk:kk + 1].to_broadcast([128, 16, cnt])
                t1_ = q1[:, :, 0:cnt]; t2_ = q2[:, :, 0:cnt]
                S.tt(t1_, xr_s, yr_s, ALU.mult, eng=eng_)
                S.tt(t2_, xi_s, yi_s, ALU.mult, eng=eng_)
                S.tt(ar[:, :, lo:lo + cnt], t1_, t2_, ALU.subtract, eng=eng_)
                S.tt(t1_, xr_s, yi_s, ALU.mult, eng=eng_)
                S.tt(t2_, xi_s, yr_s, ALU.mult, eng=eng_)
                S.tt(ai[:, :, lo:lo + cnt], t1_, t2_, ALU.add, eng=eng_)
        rpr = P.t("rpr", [128, 16, 8]); rpi = P.t("rpi", [128, 16, 8])
        for t_ in range(8):
            S.copy(rpr[:, :, t_], pwr[:, :, 8 - t_]); S.copy(rpi[:, :, t_], pwi[:, :, 8 - t_], eng="pool")
        Br = P.t("Br", [128, 16, 16]); Bi = P.t("Bi", [128, 16, 16])
        for (dst, nm) in ((Br, "s5_b_re"), (Bi, "s5_b_im")):
            S.dma(dst.rearrange("p (d g) h -> p d g h", d=2),
                  W[nm][l].rearrange("d (gp m) p h -> (m p) d gp h", m=2))
        Bbr = P.t("Bbr", [128, 16, 16]); Bbi = P.t("Bbi", [128, 16, 16]); tb = P.t("tb", [128, 16, 16])
        cor_b = sc["cor"].unsqueeze(2).to_broadcast([128, 16, 16])
        coi_b = sc["coi"].unsqueeze(2).to_broadcast([128, 16, 16])
        S.tt(Bbr, Br, cor_b, ALU.mult); S.tt(tb, Bi, coi_b, ALU.mult); S.tt(Bbr, Bbr, tb, ALU.subtract)
        S.tt(Bbi, Bi, cor_b, ALU.mult); S.tt(tb, Br, coi_b, ALU.mult); S.tt(Bbi, Bbi, tb, ALU.add)
        CL = P.t("CL", [16, 2, 32, 64])
        S.dma(CL[:, 0], W["s5_c_re"][l].rearrange("d g h p -> h (d g) p"))
        S.dma(CL[:, 1], W["s5_c_im"][l].rearrange("d g h p -> h (d g) p"))
        Cr = P.t("Cr", [128, 16, 16]); Ci = P.t("Ci", [128, 16, 16])
        for ri, dst in ((0, Cr), (1, Ci)):
            for dd in range(2):
                for gp in range(8):
                    for m in range(2):
                        g = gp * 2 + m
                        S.mm(PS[2 + ri][m * 64:(m + 1) * 64, (dd * 8 + gp) * 16:(dd * 8 + gp + 1) * 16],
                             CL[:, ri, dd * 16 + g, :], i16)
            S.evac(dst, PS[2 + ri][:, 0:256].rearrange("p (a b) -> p a b", a=16))
        Ar = P.t("Ar", [128, 16, 8, 16]); Ai = P.t("Ai", [128, 16, 8, 16])
        Cqr = P.t("Cqr", [128, 16, 8, 16]); Cqi = P.t("Cqi", [128, 16, 8, 16])
        Cdr = P.t("Cdr", [128, 16, 8, 16]); Cdi = P.t("Cdi", [128, 16, 8, 16])
        Wsr = P.t("Wsr", [128, 16, 8, 16]); Wsi = P.t("Wsi", [128, 16, 8, 16])
        t4a = P.t("t4a", [128, 8, 8, 16]); t4b = P.t("t4b", [128, 8, 8, 16])

        def cmul(outr, outi, pr, pi_, xr, xi, neg_im=False):
            S.tt(outr, pr, xr, ALU.mult); S.tt(t4a, pi_, xi, ALU.mult, eng="pool")
            S.tt(outr, outr, t4a, ALU.subtract)
            S.tt(outi, pr, xi, ALU.mult); S.tt(t4b, pi_, xr, ALU.mult, eng="pool")
            S.tt(outi, outi, t4b, ALU.add)
            if neg_im:
                S.ts(outi, outi, -1.0, None, ALU.mult, eng="pool")

        for dd in range(2):
            ds_ = slice(dd * 8, (dd + 1) * 8)

            def pw_b(tr, ti_, lo, step):
                a_ = tr[:, ds_, lo:lo + 8]; b_ = ti_[:, ds_, lo:lo + 8]
                return (a_.unsqueeze(3).to_broadcast([128, 8, 8, 16]), b_.unsqueeze(3).to_broadcast([128, 8, 8, 16]))

            def x_b(xr, xi):
                return (xr[:, ds_, :].unsqueeze(2).to_broadcast([128, 8, 8, 16]),
                        xi[:, ds_, :].unsqueeze(2).to_broadcast([128, 8, 8, 16]))
            bbr_, bbi_ = x_b(Bbr, Bbi)
            cr_, ci_ = x_b(Cr, Ci)
            if dd == 0:
                pa = pw_b(ipr, ipi, 0, 1)
                pc = pw_b(pwr, pwi, 0, 1)
                pd = pw_b(pwr, pwi, 1, 1)
            else:
                pa = pw_b(pwr, pwi, 0, 1)
                pc = pw_b(ipr, ipi, 0, 1)
                pd = pw_b(rpr, rpi, 0, 1)
            cmul(Ar[:, ds_], Ai[:, ds_], pa[0], pa[1], bbr_, bbi_)
            cmul(Cqr[:, ds_], Cqi[:, ds_], pc[0], pc[1], cr_, ci_, neg_im=True)
            cmul(Cdr[:, ds_], Cdi[:, ds_], pd[0], pd[1], cr_, ci_, neg_im=True)
            if dd == 0:
                p7r = pwr[:, ds_, 7:8].unsqueeze(3).to_broadcast([128, 8, 8, 16])
                p7i = pwi[:, ds_, 7:8].unsqueeze(3).to_broadcast([128, 8, 8, 16])
                S.tt(Wsr[:, ds_], Ar[:, ds_], p7r, ALU.mult); S.tt(t4a, Ai[:, ds_], p7i, ALU.mult)
                S.tt(Wsr[:, ds_], Wsr[:, ds_], t4a, ALU.subtract)
                S.tt(Wsi[:, ds_], Ar[:, ds_], p7i, ALU.mult); S.tt(t4b, Ai[:, ds_], p7r, ALU.mult)
                S.tt(Wsi[:, ds_], Wsi[:, ds_], t4b, ALU.add)
            else:
                S.copy(Wsr[:, ds_], Ar[:, ds_]); S.copy(Wsi[:, ds_], Ai[:, ds_], eng="pool")
        S.copy(th8, sc["th8"]); S.copy(r8t, sc["r8"])
        S.copy(CdR, Cdr.rearrange("p a b c -> p a (b c)"))
        S.copy(CdI, Cdi.rearrange("p a b c -> p a (b c)"), eng="pool")
        tmk = [P.t("tmf", [128, 128]), P.t("tmb", [128, 128])]
        S.dma(tmk[0], C["tmaskf"]); S.dma(tmk[1], C["tmaskb"])
        for dd in range(2):
            for gp in range(8):
                dg = dd * 8 + gp
                for m in range(2):
                    g = gp * 2 + m
                    ms = slice(m * 64, (m + 1) * 64)
                    pb = PS[(g % 2)]
                    S.mm(pb[:, 0:128], Ar[ms, dg].rearrange("p a b -> p (a b)"), Cqr[ms, dg].rearrange("p a b -> p (a b)"),
                         start=True, stop=False)
                    S.mm(pb[:, 0:128], Ai[ms, dg].rearrange("p a b -> p (a b)"), Cqi[ms, dg].rearrange("p a b -> p (a b)"),
                         start=False, stop=True)
                    S.tt(Toep[:, dd, g, :], pb[:, 0:128], tmk[dd], ALU.mult)
                pb = PS[2 + (gp % 2)]
                S.mm(pb[:, 0:128], Wsr[:, dg].rearrange("p a b -> p (a b)"), ident)
                S.mm(pb[:, 128:256], Wsi[:, dg].rearrange("p a b -> p (a b)"), ident)
                S.copy(WstR[:, dg, :], pb[:, 0:128], eng="act")
                S.copy(WstI[:, dg, :], pb[:, 128:256], eng="act")
        P.close()
        P = Pool(S)
        iota = P.t("iota", [128, NCH]); S.dma(iota, C["ciota"])
        NSB = 2
        sset = []
        for i_ in range(NSB):
            sset.append(dict(
                angc=P.t("angc", [128, NCH]), snc=P.t("snc", [128, NCH]), csc=P.t("csc", [128, NCH]),
                r8b=P.t("r8b", [128, 1]), xr=P.t("xr", [128, NCH]), xi=P.t("xi", [128, NCH]),
                ta=P.t("ta", [128, NCH]), tb=P.t("tbb", [128, NCH]), zr=P.t("zr", [128, NCH]), zi=P.t("zi", [128, NCH]),
                xpr=P.t("xpr", [128, NCH], BF16), xpi=P.t("xpi", [128, NCH], BF16)))
        yst = [P.t("yst0", [128, 8, 256]), P.t("yst1", [128, 8, 256])]
        YB = P.t("YB", [128, 5, 8, 256])
        HALF = NCH // 2
        it_s = 0
        for dd in range(2):
            UTx = UTf if dd == 0 else UTb
            for gp in range(8):
                dg = dd * 8 + gp
                B_ = sset[it_s % NSB]; slot_ = it_s % NSB; it_s += 1
                angc, snc, csc, r8b = B_["angc"], B_["snc"], B_["csc"], B_["r8b"]
                xr_, xi_, ta, tbb, zr, zi, xpr, xpi = B_["xr"], B_["xi"], B_["ta"], B_["tb"], B_["zr"], B_["zi"], B_["xpr"], B_["xpi"]
                for hf in range(2):
                    cs0 = hf * HALF
                    for m in range(2):
                        g = gp * 2 + m
                        S.mm(PS[hf][m * 64:(m + 1) * 64, 0:HALF], WstR[:, dg, m * 64:(m + 1) * 64], UTx[:, g, cs0:cs0 + HALF])
                        S.mm(PS[2 + hf][m * 64:(m + 1) * 64, 0:HALF], WstI[:, dg, m * 64:(m + 1) * 64], UTx[:, g, cs0:cs0 + HALF])
                S.act(angc, iota, AF.Copy, scale=th8[:, dg:dg + 1])
                range_reduce(P, snc, csc, angc, [128, NCH], slot=slot_)
                for hf in range(2):
                    sl = slice(hf * HALF, (hf + 1) * HALF)
                    S.tt(xr_[:, sl], PS[hf][:, 0:HALF], csc[:, sl], ALU.mult)
                    S.tt(ta[:, sl], PS[2 + hf][:, 0:HALF], snc[:, sl], ALU.mult)
                    S.tt(xi_[:, sl], PS[2 + hf][:, 0:HALF], csc[:, sl], ALU.mult)
                    S.tt(tbb[:, sl], PS[hf][:, 0:HALF], snc[:, sl], ALU.mult)
                S.tt(xr_, xr_, ta, ALU.add, eng="pool")
                S.tt(xi_, xi_, tbb, ALU.subtract, eng="pool")
                S.copy(r8b, r8t[:, dg:dg + 1], eng="act")
                S.scan(zr, r8b.to_broadcast([128, NCH]), xr_, 0.0)
                S.scan(zi, r8b.to_broadcast([128, NCH]), xi_, 0.0)
                S.tt(ta, zr, csc, ALU.mult); S.tt(tbb, zi, snc, ALU.mult, eng="pool")
                S.memset(xpr[:, 0:1], 0.0, eng="pool"); S.memset(xpi[:, 0:1], 0.0, eng="pool")
                S.tt(xpr[:, 1:NCH], ta[:, 0:NCH - 1], tbb[:, 0:NCH - 1], ALU.subtract)
                S.tt(ta, zr, snc, ALU.mult, eng="pool"); S.tt(tbb, zi, csc, ALU.mult, eng="pool")
                S.tt(xpi[:, 1:NCH], ta[:, 0:NCH - 1], tbb[:, 0:NCH - 1], ALU.add)
                for m in range(2):
                    g = gp * 2 + m
                    ms = slice(m * 64, (m + 1) * 64)
                    for bi, (c0, nb, bp) in enumerate(blocks):
                        pos0 = c0 if dd == 0 else bp
                        pb = PS[4 + ((bi + m) % 4)]
                        S.mm(pb[0:nb, 0:128], UTx[:, g, pos0:pos0 + nb], Toep[:, dd, g, :], start=True, stop=False)
                        S.mm(pb[0:nb, 0:128], xpr[ms, pos0:pos0 + nb], CdR[ms, dg, :], start=False, stop=False)
                        S.mm(pb[0:nb, 0:128], xpi[ms, pos0:pos0 + nb], CdI[ms, dg, :], start=False, stop=True)
                        S.copy(YB[0:nb, bi, :, g * 16:(g + 1) * 16], pb[0:nb, 0:128].rearrange("c (t h) -> c t h", t=8), eng="act")
            for bi, (c0, nb, bp) in enumerate(blocks):
                if dd == 0:
                    S.dma(YSF[c0 * 8:(c0 + nb) * 8, :].rearrange("(c t) f -> c t f", t=8), YB[0:nb, bi], q="act")
                else:
                    clast = (NCH_CTX - 1 - bp) if bi == 0 else (NCH - 1 - (bp - NCH_CTX))
                    cfirst = clast - nb + 1
                    stg = yst[bi % 2]
                    aF = anti if nb == 128 else anti32
                    for q4 in range(4):
                        pbq = PS[4 + q4]
                        S.mm(pbq[0:nb, :], aF, YB[0:nb, bi].rearrange("c t f -> c (t f)")[:, q4 * 512:(q4 + 1) * 512])
                        S.evac(stg[0:nb].rearrange("c t f -> c (t f)")[:, q4 * 512:(q4 + 1) * 512], pbq[0:nb, :])
                    S.dma(YSB[cfirst * 8:(cfirst + nb) * 8, :].rearrange("(c t) f -> c t f", t=8), stg[0:nb], q="act")
        P.close()
        P0.close()
        PU.close()

        if stop_after == 'S':
            raise _Stop()
        if stop_after == 'P':
            raise _Stop()
        P = Pool(S)
        Wo = P.t("Wo", [128, 8, 1024], BF16)
        wos = [P.t("wos0", [128, 1024]), P.t("wos1", [128, 1024])]
        for k in range(8):
            S.dma(wos[k % 2], W["w_out"][l, k * 128:(k + 1) * 128, :], q=("sp" if k % 2 == 0 else "pool"))
            S.copy(Wo[:, k, :], wos[k % 2], eng=("dve" if k % 2 == 0 else "act"))
        wgl = P.t("wgl", [128, 2, 256]); wglb = P.t("wglb", [128, 2, 256], BF16)
        S.dma(wgl, W["s5_w_glu"][l].rearrange("(k p) n -> p k n", p=128))
        S.copy(wglb, wgl)
        bgl = P.t("bgl", [128, 256]); S.dma(bgl, dram_rows_bcast(W["s5_b_glu"], l * 256, 256))
        dsk = P.t("dsk", [128, 256]); S.dma(dsk, dram_rows_bcast(W["s5_d"], l * 256, 256))
        GG = [P.t("GG0", [128, 1024]), P.t("GG1", [128, 1024])]
        for r_ in range(2):
            S.tt(GG[r_], gpost, MOD[r_][:, 2048:3072], ALU.mult, eng="pool")
        NB = 4
        ys_b = [P.t(f"ys{i}", [128, 1024]) for i in range(NB)]
        gt_b = [P.t(f"gt{i}", [128, 1024]) for i in range(NB)]
        x_b = [P.t(f"xo{i}", [128, 1024]) for i in range(NB)]
        s5a = [P.t(f"s5a{i}", [128, 3, 256]) for i in range(NB)]
        y5_ = [P.t(f"y5{i}", [128, 256]) for i in range(NB)]
        g5_ = [P.t(f"g5{i}", [128, 256]) for i in range(NB)]
        t5_ = [P.t(f"t5{i}", [128, 256]) for i in range(NB)]
        g5b_ = [P.t(f"g5b{i}", [128, 256], BF16) for i in range(NB)]
        g5T_ = [P.t(f"g5T{i}", [128, 2, 128], BF16) for i in range(NB)]
        ybf_ = [P.t(f"ybf{i}", [128, 1024], BF16) for i in range(NB)]
        yT_ = [P.t(f"yT{i}", [128, 8, 128], BF16) for i in range(NB)]
        zt__ = [P.t(f"zt{i}", [128, 1024]) for i in range(NB)]
        junk_ = [P.t(f"junk{i}", [128, 512], BF16) for i in range(NB)]
        st__ = [P.t(f"st{i}", [128, 4]) for i in range(NB)]
        tiles_o = [ti for ti in range(NTILE) if not (last and ti < 2)]

        def o_stage1(n):
            ti = tiles_o[n]; b = n % NB
            tsl = slice(ti * 128, (ti + 1) * 128)
            y5, g5, t5, g5b = y5_[b], g5_[b], t5_[b], g5b_[b]
            S.dma(ys_b[b][:, 0:512], YS[tsl, 0:512], q="sp")
            S.dma(ys_b[b][:, 768:1024], YS[tsl, 768:1024], q="sp")
            S.dma(gt_b[b], PT[tsl, 768:1792], q="sp")
            S.dma(x_b[b], x_src[tsl, :], q="sp")
            S.dma(s5a[b][:, 0, :], YSF[tsl, :], q="sp")
            S.dma(s5a[b][:, 1, :], YSB[tsl, :], q="sp")
            S.dma(s5a[b][:, 2, :], PT[tsl, 512:768], q="sp")
            S.tt(s5a[b][:, 0, :], s5a[b][:, 0, :], s5a[b][:, 1, :], ALU.add, eng="pool")
            S.tt(y5, s5a[b][:, 2, :], dsk, ALU.mult)
            S.tt(y5, y5, s5a[b][:, 0, :], ALU.add)
            S.tt(t5, y5, y5, ALU.mult, eng="pool")
            S.ts(t5, t5, 0.044715, 1.0, ALU.mult, ALU.add, eng="pool")
            S.tt(t5, t5, y5, ALU.mult, eng="pool")
            S.act(t5, t5, AF.Sigmoid, scale=2.0 * float(np.sqrt(2.0 / np.pi)))
            S.tt(g5, y5, t5, ALU.mult)
            S.copy(g5b, g5, eng="pool")
            pg = PS[n % 2]
            for k in range(2):
                S.mm(pg[:, k * 128:(k + 1) * 128], g5b[:, k * 128:(k + 1) * 128], identb)

        def o_stage2(n):
            b = n % NB
            S.copy(g5T_[b], PS[n % 2][:, 0:256].rearrange("p (a b) -> p a b", a=2), eng="act")
            pz = PS[2 + n % 2]
            for k in range(2):
                S.mm(pz[:, 0:256], g5T_[b][:, k, :], wglb[:, k, :], start=(k == 0), stop=(k == 1))

        def o_stage3(n):
            b = n % NB
            t5, g5 = t5_[b], g5_[b]
            pz = PS[2 + n % 2]
            S.tt(t5, pz[:, 0:256], bgl, ALU.add)
            S.act(t5, t5, AF.Sigmoid)
            S.tt(ys_b[b][:, 512:768], g5, t5, ALU.mult)
            S.tt(ybf_[b], ys_b[b], gt_b[b], ALU.mult, eng="pool")
            for half in range(2):
                pt_ = PS[4 + half]
                for kk in range(4):
                    k = half * 4 + kk
                    S.mm(pt_[:, kk * 128:(kk + 1) * 128], ybf_[b][:, k * 128:(k + 1) * 128], identb)
                S.copy(yT_[b][:, half * 4:(half + 1) * 4, :], pt_.rearrange("p (a b) -> p a b", a=4), eng="act")

        def o_stage4(n):
            ti = tiles_o[n]; b = n % NB
            rsel = 1 if ti < 2 else 0
            tsl = slice(ti * 128, (ti + 1) * 128)
            st_, zt_ = st__[b], zt__[b]
            for nn in range(2):
                po = PS[6 + nn]
                for k in range(8):
                    S.mm(po, yT_[b][:, k, :], Wo[:, k, nn * 512:(nn + 1) * 512], start=(k == 0), stop=(k == 7))
                S.act(junk_[b], po, AF.Square, accum_out=st_[:, nn:nn + 1])
            S.tt(st_[:, 2:3], st_[:, 0:1], st_[:, 1:2], ALU.add)
            S.ts(st_[:, 2:3], st_[:, 2:3], 1.0 / D, EPS, ALU.mult, ALU.add)
            S.act(st_[:, 3:4], st_[:, 2:3], AF.Sqrt)
            S.recip(st_[:, 2:3], st_[:, 3:4])
            for nn in range(2):
                po = PS[6 + nn]
                S.stt(zt_[:, nn * 512:(nn + 1) * 512], po, st_[:, 2:3], GG[rsel][:, nn * 512:(nn + 1) * 512], ALU.mult, ALU.mult)
            S.tt(zt_, zt_, x_b[b], ALU.add, eng="pool")
            if last:
                S.dma(out[(ti - 2) * 128:(ti - 1) * 128, :], zt_, q="act")
            else:
                S.dma(xs[tsl, :], zt_, q="act")

        stages_o = [o_stage1, o_stage2, o_stage3, o_stage4]
        nit = len(tiles_o)
        for step in range(nit + len(stages_o) - 1):
            for si, fn in enumerate(stages_o):
                n = step - si
                if 0 <= n < nit:
                    fn(n)
        P.close()

    except _Stop:
        pass
    S.barrier()
    G.es.close()
    return nc


_CACHE = {}


def kernel(**inputs):
    consts = make_consts()
    B = inputs["x"].shape[0]
    if "nc" not in _CACHE:
        _CACHE["nc"] = build(debug=False)
    nc = _CACHE["nc"]
    in_maps = []
    ncores = 8
    for core in range(ncores):
        b = core % B
        m = {}
        m["xin"] = np.ascontiguousarray(np.concatenate([inputs["ctx"][b], inputs["x"][b]], axis=0).astype(np.float32))
        ccv = np.stack([inputs["c"][b], inputs["c_ctx"]], axis=-1).astype(np.float32)
        m["cc"] = np.ascontiguousarray(ccv.reshape(8, 128, 2).transpose(1, 0, 2))
        for n in WEIGHT_NAMES:
            m[n] = np.ascontiguousarray(np.asarray(inputs[n], dtype=np.float32))
        for n, v in consts.items():
            m["k_" + n] = v
        in_maps.append(m)
    res = run_bass_kernel_spmd(nc, in_maps, core_ids=list(range(ncores)))
    outs = [res.results[b]["out"] for b in range(B)]
    return np.stack(outs, axis=0).astype(np.float32)
```

```python
import numpy as np
import ml_dtypes
from contextlib import ExitStack
import concourse.bass as bass
import concourse.mybir as mybir
from concourse.bass_utils import run_bass_kernel_spmd

F32 = mybir.dt.float32
BF16 = mybir.dt.bfloat16
I32 = mybir.dt.int32
ALU = mybir.AluOpType
AF = mybir.ActivationFunctionType
AX = mybir.AxisListType

NCTX = 256
NLAT = 4096
T = NCTX + NLAT
D = 1024
NTILE = T // 128
EPS = 1e-6
PI = float(np.pi)
TWO_PI = float(2 * np.pi)
NCH = T // 8
NCH_CTX = NCTX // 8
import os
NOSCHED = bool(os.environ.get('MK_NOSCHED'))


class Sched:
    NDMA = 12

    def __init__(self, nc):
        self.nc = nc
        self.eng = {"pe": nc.tensor, "dve": nc.vector, "act": nc.scalar,
                    "pool": nc.gpsimd, "sp": nc.sync}
        self.sem = {k: nc.alloc_semaphore("sem_" + k) for k in self.eng}
        self.cnt = {k: 0 for k in self.eng}
        self.seen = {k: {} for k in self.eng}
        self.dq = {}
        for q in ("sp", "pool", "act"):
            self.dq[q] = {"sems": [nc.alloc_semaphore(f"dq_{q}_{i}") for i in range(self.NDMA)], "n": 0}
        self.lastw = {}
        self.readers = {}
        self.ntens = 0
        self.rr = 0
        self.pending = []

    def _wait(self, eng, ev):
        if ev is None:
            return
        if ev[0] == "e":
            _, src, val = ev
            if src == "pe" and eng == "pe":
                return
            key = ("e", src)
            sem = self.sem[src]
        else:
            _, q, slot, val = ev
            key = ("d", q, slot)
            sem = self.dq[q]["sems"][slot]
        if self.seen[eng].get(key, 0) >= val:
            return
        self.seen[eng][key] = val
        self.eng[eng].wait_ge(sem, val)

    def _deps(self, eng, reads, writes):
        for t in reads:
            self._wait(eng, self.lastw.get(t))
        for t in writes:
            self._wait(eng, self.lastw.get(t))
            for ev in self.readers.get(t, {}).values():
                self._wait(eng, ev)

    def _commit(self, ev, reads, writes):
        for t in reads:
            d = self.readers.setdefault(t, {})
            d[ev[0:2] if ev[0] == "e" else ev[0:3]] = ev
        for t in writes:
            self.lastw[t] = ev
            self.readers[t] = {}

    WHOLE_DRAM = ("RP",)

    @classmethod
    def _names(cls, aps):
        out = []
        for a in aps:
            if a is None or isinstance(a, (int, float)):
                continue
            nm = a.tensor.name
            if "DRam" in type(a.tensor).__name__ and nm not in cls.WHOLE_DRAM:
                nm = f"{nm}@{a.offset}:{tuple(map(tuple, a.ap))}"
            out.append(nm)
        return out

    LAT = float(os.environ.get('MK_LAT', '0.35'))

    def op(self, eng, method, *args, reads=(), writes=(), **kw):
        r = self._names(reads)
        w = self._names(writes)
        cost = self._cost(eng, method, args, kw)
        self.pending.append(("op", eng, method, args, kw, r, w, cost))

    def dma(self, out, in_, q=None, **kw):
        if q is None:
            q = "sp"
        r = self._names([in_])
        w = self._names([out])
        nbytes = 1
        for d_ in out.shape:
            nbytes *= d_
        nbytes *= 4
        self.pending.append(("dma", q, None, (out, in_), kw, r, w, 0.15, 2.0 + nbytes / 150e3))

    @staticmethod
    def _free(ap):
        n = 1
        for d_ in ap.shape[1:]:
            n *= d_
        return n

    def _cost(self, eng, method, args, kw):
        try:
            if eng == "pe":
                rhs = args[2]
                n = max(64, self._free(rhs))
                c = n / 2400.0
                if rhs.dtype == F32:
                    c *= 4
                return c + 0.07
            n = self._free(args[0])
            c = n / 960.0 * (1.5 if eng == "dve" else 1.3) + float(os.environ.get('MK_OVH', '0.1'))
            if eng == "pool":
                c = n / 400.0 + 0.2
            if method == "tensor_tensor_scan":
                c = 2 * n / 960.0 + 0.1
            return c
        except Exception:
            return 0.3

    def flush(self):
        ops = self.pending
        self.pending = []
        n = len(ops)
        if n == 0:
            return
        lastw = {}
        readers = {}
        preds = [None] * n
        for i, o in enumerate(ops):
            ps = set()
            for t in o[5]:
                j = lastw.get(t)
                if j is not None:
                    ps.add(j)
            for t in o[6]:
                j = lastw.get(t)
                if j is not None:
                    ps.add(j)
                ps.update(readers.get(t, ()))
            for t in o[5]:
                readers.setdefault(t, []).append(i)
            for t in o[6]:
                lastw[t] = i
                readers[t] = []
            ps.discard(i)
            preds[i] = ps
        succs = [[] for _ in range(n)]
        indeg = [0] * n
        for i in range(n):
            indeg[i] = len(preds[i])
            for p in preds[i]:
                succs[p].append(i)
        fin = [0.0] * n
        rdy = [0.0] * n
        crit = [-1] * n
        epred = [-1] * n
        elast = {e: -1 for e in self.eng}
        stt_ = [0.0] * n
        eng_free = {e: 0.0 for e in self.eng}
        ready = {e: [] for e in self.eng}
        import heapq
        for i in range(n):
            if indeg[i] == 0:
                heapq.heappush(ready[ops[i][1]], i)
        order = []
        remaining = n
        WINDOW = int(os.environ.get('MK_WINDOW', '24'))
        if NOSCHED:
            order = list(range(n))
            remaining = 0
        while remaining:
            best = None
            for e, lst in ready.items():
                if not lst:
                    continue
                cand = heapq.nsmallest(WINDOW, lst)
                for i in cand:
                    st = max(eng_free[e], rdy[i])
                    key = (st, i)
                    if best is None or key < best[0]:
                        best = (key, e, i)
            (st, i), e, _ = best
            ready[e].remove(i)
            heapq.heapify(ready[e])
            o = ops[i]
            dur = o[7]
            epred[i] = elast[e] if eng_free[e] > rdy[i] else -2
            elast[e] = i
            stt_[i] = st
            eng_free[e] = st + dur
            fin[i] = st + (o[8] if o[0] == "dma" else dur)
            order.append(i)
            remaining -= 1
            for s_ in succs[i]:
                if fin[i] + self.LAT > rdy[s_]:
                    rdy[s_] = fin[i] + self.LAT
                    crit[s_] = i
                indeg[s_] -= 1
                if indeg[s_] == 0:
                    heapq.heappush(ready[ops[s_][1]], s_)
        if os.environ.get('MK_CRIT') and n > 2000:
            i = max(range(n), key=lambda k: fin[k])
            chain = []
            while i >= 0:
                chain.append(i)
                i = epred[i] if epred[i] >= 0 else crit[i]
            agg = {}
            for k in chain:
                o = ops[k]
                key = (o[1], o[2] or 'dma', 'engwait' if epred[k] >= 0 else 'data')
                a = agg.setdefault(key, [0, 0.0]); a[0] += 1; a[1] += o[7]
            print('CRIT n=%d len=%d makespan=%.1f' % (n, len(chain), max(fin)))
            for key, a in sorted(agg.items(), key=lambda kv: -kv[1][1])[:14]:
                print('   ', key, a[0], round(a[1], 1))
        if os.environ.get('MK_VERBOSE'):
            busy = {}
            for o in ops:
                busy[o[1]] = busy.get(o[1], 0.0) + o[7]
            print('FLUSH n=%d makespan_us=%.1f busy=%s' % (n, max(fin) if not NOSCHED else -1, {k: round(v) for k, v in busy.items()}), flush=True)
        for i in order:
            o = ops[i]
            if o[0] == "op":
                self._emit_op(o[1], o[2], o[3], o[4], o[5], o[6])
            else:
                self._emit_dma(o[1], o[3][0], o[3][1], o[4], o[5], o[6])

    def _emit_op(self, eng, method, args, kw, r, w):
        self._deps(eng, r, w)
        ins = getattr(self.eng[eng], method)(*args, **kw)
        self.cnt[eng] += 1
        ins.then_inc(self.sem[eng], 1)
        self._commit(("e", eng, self.cnt[eng]), r, w)
        return ins

    def _emit_dma(self, q, out, in_, kw, r, w):
        d = self.dq[q]
        n = d["n"]
        slot = n % self.NDMA
        val = 16 * (n // self.NDMA + 1)
        if n >= self.NDMA:
            self._wait(q, ("d", q, slot, val - 16))
        self._deps(q, r, w)
        ins = self.eng[q].dma_start(out=out, in_=in_, **kw)
        ins.then_inc(d["sems"][slot], 16)
        d["n"] = n + 1
        self._commit(("d", q, slot, val), r, w)
        return ins

    def mm(self, out, lhsT, rhs, start=True, stop=True, **kw):
        return self.op("pe", "matmul", out, lhsT, rhs, start=start, stop=stop,
                       reads=[lhsT, rhs], writes=[out], **kw)

    def act(self, out, in_, func, bias=None, scale=None, accum_out=None):
        kw = {}
        rd = [in_]
        if bias is not None:
            kw["bias"] = bias
            rd.append(bias)
        if scale is not None:
            kw["scale"] = scale
            rd.append(scale)
        wr = [out]
        if accum_out is not None:
            kw["accum_out"] = accum_out
            wr.append(accum_out)
        return self.op("act", "activation", out, in_, func, reads=rd, writes=wr, **kw)

    def tt(self, out, in0, in1, op, eng="dve"):
        return self.op(eng, "tensor_tensor", out, in0, in1, op, reads=[in0, in1], writes=[out])

    def ts(self, out, in0, s1, s2, op0, op1=None, eng="dve"):
        kw = {}
        if op1 is not None:
            kw["op1"] = op1
        return self.op(eng, "tensor_scalar", out, in0, s1, s2, op0, reads=[in0, s1, s2], writes=[out], **kw)

    def stt(self, out, in0, scalar, in1, op0, op1, eng="dve"):
        return self.op(eng, "scalar_tensor_tensor", out, in0, scalar, in1, op0, op1,
                       reads=[in0, scalar, in1], writes=[out])

    def copy(self, out, in_, eng="dve"):
        if eng == "act":
            return self.act(out, in_, AF.Copy)
        return self.op(eng, "tensor_copy", out, in_, reads=[in_], writes=[out])

    def evac(self, out, in_):
        self.rr += 1
        return self.copy(out, in_, eng=("dve" if self.rr % 2 else "act"))

    def memset(self, ap, val, eng="dve"):
        return self.op(eng, "memset", ap, val, reads=[], writes=[ap])

    def scan(self, out, d0, d1, initial, op0=ALU.mult, op1=ALU.add):
        return self.op("dve", "tensor_tensor_scan", out, d0, d1, initial, op0, op1,
                       reads=[d0, d1, initial], writes=[out])

    def recip(self, out, in_):
        return self.op("dve", "reciprocal", out, in_, reads=[in_], writes=[out])

    def barrier(self):
        self.flush()
        for e in self.eng:
            for src in self.eng:
                if self.cnt[src] > 0:
                    self._wait(e, ("e", src, self.cnt[src]))
            for q, d in self.dq.items():
                n = d["n"]
                for slot in range(min(n, self.NDMA)):
                    last_n = ((n - 1 - slot) // self.NDMA) * self.NDMA + slot
                    self._wait(e, ("d", q, slot, 16 * (last_n // self.NDMA + 1)))


class Pool:
    def __init__(self, S):
        self.S = S
        self.es = ExitStack()

    def t(self, name, shape, dtype=F32):
        self.S.ntens += 1
        h = self.es.enter_context(self.S.nc.sbuf_tensor(f"{name}_{self.S.ntens}", list(shape), dtype))
        return h.ap() if hasattr(h, "ap") else h

    def close(self):
        self.S.barrier()
        self.es.close()


def dram_rows_bcast(t_ap, offset, n, parts=128):
    return bass.AP(t_ap.tensor, offset, [[0, parts], [1, n]])


def make_consts():
    c = {}
    c["ident"] = np.eye(128, dtype=np.float32)
    c["anti"] = np.eye(128, dtype=np.float32)[::-1].copy()
    c["anti64"] = np.eye(64, dtype=np.float32)[::-1].copy()
    c["anti32"] = np.eye(32, dtype=np.float32)[::-1].copy()
    sel = np.zeros((2, 2, 128), np.float32)
    sel[0, 0] = 1
    sel[1, 1] = 1
    c["sel"] = sel
    selm = np.zeros((2, 128), np.float32)
    selm[0, :64] = 1
    selm[1, 64:] = 1
    c["selm"] = selm
    j = np.arange(128)
    mf = (j[:, None] <= j[None, :]).astype(np.float32)
    c["maskf"] = mf
    c["maskb"] = mf.T.copy()
    tok = np.arange(NLAT)
    rowp = (tok // 64).astype(np.float32)
    colp = (tok % 64).astype(np.float32)
    pos = np.zeros((128, T), np.float32)
    freq = np.zeros((128, 1), np.float32)
    half = 16
    fr = 10000.0 ** (-np.arange(0, half, 2, dtype=np.float32) / half)
    for p in range(128):
        d = p % 32
        pos[p, NCTX:] = rowp if d < 16 else colp
        freq[p, 0] = fr[d % 8]
    c["pos"] = pos
    c["freq"] = freq
    hm = np.zeros((128, 4, 128), np.float32)
    bdm = np.zeros((128, 4, 64), np.float32)
    for h in range(4):
        hm[h * 32:(h + 1) * 32, h, :] = 1
        bdm[h * 32:(h + 1) * 32, h, :] = 1
    c["hm"] = hm
    c["bdm"] = bdm.reshape(128, 256)
    col = np.arange(64)
    cs = np.clip(col - 8, 0, 48)
    ok = (col[:, None] >= cs[None, :]) & (col[:, None] < cs[None, :] + 16)
    cm = np.where(ok, 0.0, -30000.0).astype(np.float32)
    c["colmask"] = np.concatenate([cm, cm], 0)
    c["negblk"] = np.full((128, 64), -30000.0, np.float32)
    s_idx = np.repeat(np.arange(8), 16)
    c["tmaskf"] = (s_idx[:, None] <= s_idx[None, :]).astype(np.float32)
    c["tmaskb"] = (s_idx[:, None] >= s_idx[None, :]).astype(np.float32)
    c["ciota"] = np.tile(np.arange(NCH, dtype=np.float32)[None, :], (128, 1))
    rc = np.zeros((4, PT_PAD), np.float32)
    for i, w in enumerate((2, 4, 8, 16)):
        for (n, off) in ((NCTX, PAD_CTX), (NLAT, PAD_LAT)):
            t = np.arange(n)
            lo = np.clip(t - w // 2, 0, n)
            hi = np.clip(t - w // 2 + w, 0, n)
            rc[i, off:off + n] = 1.0 / (hi - lo)
    c["poolrc"] = rc
    return c


PAD_CTX = 16
PAD_LAT = 16 + NCTX + 32
PT_PAD = PAD_LAT + NLAT + 16

CONST_SHAPES = None

WEIGHT_NAMES = ["w_mod", "b_mod", "g_pre", "g_post", "w_in", "w_out", "gla_w_gate", "gla_b_gate",
                "gla_g_norm", "na_rpb", "s5_lam_re", "s5_lam_im", "s5_log_dt", "s5_b_re", "s5_b_im",
                "s5_c_re", "s5_c_im", "s5_d", "s5_w_glu", "s5_b_glu", "pool_w", "pool_scale"]
WEIGHT_SHAPES = {
    "w_mod": (2, 1024, 3072), "b_mod": (2, 3072), "g_pre": (2, 1024), "g_post": (2, 1024),
    "w_in": (2, 1024, 2848), "w_out": (2, 1024, 1024), "gla_w_gate": (2, 2, 16, 128),
    "gla_b_gate": (2, 2, 128), "gla_g_norm": (2, 64), "na_rpb": (2, 4, 15, 31),
    "s5_lam_re": (2, 2, 16, 64), "s5_lam_im": (2, 2, 16, 64), "s5_log_dt": (2, 2, 16),
    "s5_b_re": (2, 2, 16, 64, 16), "s5_b_im": (2, 2, 16, 64, 16), "s5_c_re": (2, 2, 16, 16, 64),
    "s5_c_im": (2, 2, 16, 16, 64), "s5_d": (2, 256), "s5_w_glu": (2, 256, 256), "s5_b_glu": (2, 256),
    "pool_w": (2, 4, 64, 64), "pool_scale": (2, 256),
}

FM_BLOCKS = [
    (1184, 128, 0), (2848, 128, 128), (0, 128, 256), (2976, 128, 384), (384, 32, 512),
    (1312, 128, 544), (1440, 128, 672), (416, 128, 800), (544, 128, 928), (1568, 128, 1056), (1696, 128, 1184)]
PF_ROWS = 1312
TM_CHUNKS = [
    (128, 256, 0, False), (672, 256, 256, False), (928, 256, 512, False),
    (1824, 512, 768, True), (2336, 512, 1280, True)]
PT_COLS = 1792
NWCOL = 3104
GROUPS = [(0, 256)] + [(256 + 512 * i, 512) for i in range(8)]


class _Stop(Exception):
    pass


def build(debug=False, nlayers=2, stop_after=None):
    nc = bass.Bass("TRN2", target_bir_lowering=False)
    S = Sched(nc)

    def dram(name, shape, dtype=F32, kind="Internal"):
        return nc.dram_tensor(name, list(shape), dtype, kind=kind).ap()

    dbg_kind = "ExternalOutput" if debug else "Internal"
    xin = dram("xin", [T, D], kind="ExternalInput")
    cc = dram("cc", [128, 8, 2], kind="ExternalInput")
    W = {n: dram(n, WEIGHT_SHAPES[n], kind="ExternalInput") for n in WEIGHT_NAMES}
    consts = make_consts()
    C = {n: dram("k_" + n, v.shape, kind="ExternalInput") for n, v in consts.items()}
    out = dram("out", [NLAT, D], kind="ExternalOutput")
    xs = dram("xs", [T, D], kind=dbg_kind)
    PF = dram("PF", [PF_ROWS, T], kind=dbg_kind)
    PT = dram("PT", [T, PT_COLS], kind=dbg_kind)
    YS = dram("YS", [T, 1024], kind=dbg_kind)
    OG = dram("OG", [T, 256])
    OGB = dram("OGB", [T, 256])
    YSF = dram("YSF", [T, 256], kind=dbg_kind)
    YSB = dram("YSB", [T, 256], kind=dbg_kind)
    COS = dram("COS", [128, T])
    SIN = dram("SIN", [128, T])
    RP = dram("RP", [60, 160])

    PS = []
    for i in range(8):
        h = nc.alloc_psum_tensor(f"psum{i}", [128, 512], F32)
        PS.append(h.ap() if hasattr(h, "ap") else h)

    G = Pool(S)
    ident = G.t("ident", [128, 128]); S.dma(ident, C["ident"])
    identb = G.t("identb", [128, 128], BF16); S.copy(identb, ident)
    anti = G.t("anti", [128, 128]); S.dma(anti, C["anti"])
    antib = G.t("antib", [128, 128], BF16); S.copy(antib, anti)
    anti32 = G.t("anti32", [32, 32]); S.dma(anti32, C["anti32"])
    anti32b = G.t("anti32b", [32, 32], BF16); S.copy(anti32b, anti32)
    anti64 = G.t("anti64", [64, 64]); S.dma(anti64, C["anti64"])
    ones1 = G.t("ones1", [128, 1]); S.memset(ones1, 1.0)
    MOD = [G.t("mod0", [128, 3072]), G.t("mod1", [128, 3072])]
    gpost = G.t("gpost", [128, 1024])

    rr_cache = {}

    def range_reduce(P, out_s, out_c, ang, shape, slot=0):
        key = (id(P), tuple(shape), slot)
        if key not in rr_cache:
            rr_cache[key] = (P.t("rr_ki", shape, I32), P.t("rr_kf", shape), P.t("rr_ph", shape))
        ki, kf, ph = rr_cache[key]
        S.ts(ki, ang, 1.0 / TWO_PI, None, ALU.mult)
        S.copy(kf, ki)
        S.stt(ph, kf, -TWO_PI, ang, ALU.mult, ALU.add)
        S.ts(ph, ph, -PI, PI, ALU.max, ALU.min)
        S.act(out_s, ph, AF.Sin)
        S.act(kf, ph, AF.Sin, scale=0.5)
        S.act(kf, kf, AF.Square)
        S.act(out_c, kf, AF.Identity, bias=1.0, scale=-2.0)

    P = Pool(S)
    freq = P.t("freq", [128, 1]); S.dma(freq, C["freq"])
    for (t0, n) in [(0, 1088), (1088, 1088), (2176, 1088), (3264, 1088)]:
        pos = P.t("pos", [128, n]); S.dma(pos, C["pos"][:, t0:t0 + n])
        ang = P.t("ang", [128, n])
        S.ts(ang, pos, freq, None, ALU.mult)
        sn = P.t("sn", [128, n]); cs_ = P.t("cs", [128, n])
        range_reduce(P, sn, cs_, ang, [128, n])
        S.dma(SIN[:, t0:t0 + n], sn)
        S.dma(COS[:, t0:t0 + n], cs_)
    P.close()

    try:
      for l in range(nlayers):
        x_src = xin if l == 0 else xs
        last = (l == nlayers - 1) and not debug

        P = Pool(S)
        cst = P.t("cst", [128, 8, 2]); S.dma(cst, cc)
        css = P.t("css", [128, 8, 2]); S.act(css, cst, AF.Silu)
        wmb = [P.t("wm0", [128, 3072]), P.t("wm1", [128, 3072])]
        for k in range(8):
            wm = wmb[k % 2]
            S.dma(wm, W["w_mod"][l, k * 128:(k + 1) * 128, :], q=("sp" if k % 2 == 0 else "pool"))
            for n in range(6):
                S.mm(PS[n][0:2, :], css[:, k, :], wm[:, n * 512:(n + 1) * 512], start=(k == 0), stop=(k == 7))
        selt = P.t("selt", [2, 2, 128]); S.dma(selt, C["sel"])
        bm2 = [P.t("bm0", [2, 512]), P.t("bm1", [2, 512])]
        modr2 = [P.t("modr0", [2, 512]), P.t("modr1", [2, 512])]
        for n in range(6):
            bm = bm2[n % 2]; modr = modr2[n % 2]
            S.dma(bm, bass.AP(W["b_mod"].tensor, l * 3072 + n * 512, [[0, 2], [1, 512]]))
            S.tt(modr, PS[n][0:2, :], bm, ALU.add)
            for r in range(2):
                pb = PS[6 + r]
                S.mm(pb, selt[:, r, :], modr)
                S.evac(MOD[r][:, n * 512:(n + 1) * 512], pb)
        gpre = P.t("gpre", [128, 1024])
        S.dma(gpre, dram_rows_bcast(W["g_pre"], l * 1024, 1024))
        S.dma(gpost, dram_rows_bcast(W["g_post"], l * 1024, 1024))
        for r in range(2):
            S.stt(MOD[r][:, 1024:2048], MOD[r][:, 1024:2048], 1.0, gpre, ALU.add, ALU.mult)
        PM = P
        P = Pool(S)
        Wb = P.t("Wb", [128, 8, NWCOL], BF16)
        wst = [P.t("wst0", [128, 2848]), P.t("wst1", [128, 2848])]
        for k in range(8):
            st = wst[k % 2]
            S.dma(st, W["w_in"][l, k * 128:(k + 1) * 128, :], q=("sp" if k % 2 == 0 else "pool"))
            S.copy(Wb[:, k, 0:1424], st[:, 0:1424], eng="dve")
            S.copy(Wb[:, k, 1424:2848], st[:, 1424:2848], eng="act")
            for (c0, r0) in ((1184, 2848), (0, 2976)):
                sv = st[:, c0:c0 + 128].rearrange("p (a t e) -> p a t e", a=8, t=2, e=8)
                dv = Wb[:, k, r0:r0 + 128].rearrange("p (a t e) -> p a t e", a=8, t=2, e=8)
                S.ts(dv[:, :, 0, :], sv[:, :, 1, :], -1.0, None, ALU.mult, eng="pool")
                S.copy(dv[:, :, 1, :], sv[:, :, 0, :], eng="pool")
        xt_b = [P.t("xt0", [128, 1024]), P.t("xt1", [128, 1024])]
        junk = P.t("junk", [128, 1024])
        h32 = P.t("h32", [128, 1024])
        hb_b = [P.t("hb0", [128, 1024], BF16), P.t("hb1", [128, 1024], BF16)]
        hT_b = [P.t("hT0", [128, 8, 512], BF16), P.t("hT1", [128, 8, 512], BF16)]
        fst_b = [P.t(f"fst{i}", [128, 512]) for i in range(4)]
        ropec = [P.t("ropec0", [128, 512]), P.t("ropec1", [128, 512])]
        ropes = [P.t("ropes0", [128, 512]), P.t("ropes1", [128, 512])]
        tst_b = [P.t("tst0", [128, PT_COLS]), P.t("tst1", [128, PT_COLS])]
        stat = [P.t("stat0", [128, 4]), P.t("stat1", [128, 4])]
        nt = 0
        nf = 0
        for gi, (tok0, n) in enumerate(GROUPS):
            hT = hT_b[gi % 2]
            r = 0 if tok0 >= NCTX else 1
            for ti in range(n // 128):
                xt = xt_b[nt % 2]; hb = hb_b[nt % 2]; sv_ = stat[nt % 2]
                S.dma(xt, x_src[tok0 + ti * 128: tok0 + (ti + 1) * 128, :], q="sp")
                S.act(junk, xt, AF.Square, accum_out=sv_[:, 0:1])
                S.ts(sv_[:, 1:2], sv_[:, 0:1], 1.0 / D, EPS, ALU.mult, ALU.add)
                S.act(sv_[:, 2:3], sv_[:, 1:2], AF.Sqrt)
                S.recip(sv_[:, 3:4], sv_[:, 2:3])
                S.stt(h32, xt, sv_[:, 3:4], MOD[r][:, 1024:2048], ALU.mult, ALU.mult)
                S.tt(hb, h32, MOD[r][:, 0:1024], ALU.add, eng="pool")
                for half in range(2):
                    pt_ = PS[half]
                    for kk in range(4):
                        k = half * 4 + kk
                        S.mm(pt_[:, kk * 128:(kk + 1) * 128], hb[:, k * 128:(k + 1) * 128], identb)
                    S.evac(hT[:, half * 4:(half + 1) * 4, ti * 128:(ti + 1) * 128],
                           pt_.rearrange("p (a b) -> p a b", a=4))
                nt += 1
            rc_ = ropec[gi % 2]; rs_ = ropes[gi % 2]
            S.dma(rc_[:, 0:n], COS[:, tok0:tok0 + n], q="sp")
            S.dma(rs_[:, 0:n], SIN[:, tok0:tok0 + n], q="sp")
            held = None
            for bi, (wc, ncol, prow) in enumerate(FM_BLOCKS):
                pb = PS[2 + (bi % 3)]
                for k in range(8):
                    S.mm(pb[0:ncol, 0:n], Wb[:, k, wc:wc + ncol], hT[:, k, 0:n], start=(k == 0), stop=(k == 7))
                fs = fst_b[nf % 4]; nf += 1
                S.evac(fs[0:ncol, 0:n], pb[0:ncol, 0:n])
                if bi in (0, 2):
                    held = (fs, prow)
                    continue
                if bi in (1, 3):
                    f0, prow0 = held
                    S.tt(f0[:, 0:n], f0[:, 0:n], rc_[:, 0:n], ALU.mult, eng="pool")
                    S.tt(fs[:, 0:n], fs[:, 0:n], rs_[:, 0:n], ALU.mult, eng="pool")
                    S.tt(f0[:, 0:n], f0[:, 0:n], fs[:, 0:n], ALU.add)
                    S.dma(PF[prow0:prow0 + 128, tok0:tok0 + n], f0[:, 0:n], q="act")
                    continue
                S.dma(PF[prow:prow + ncol, tok0:tok0 + n], fs[0:ncol, 0:n], q="act")
            for ti in range(n // 128):
                ts_ = tst_b[ti % 2]
                for ci, (wc, ncol, pcol, silu) in enumerate(TM_CHUNKS):
                    pb = PS[5 + (ci % 3)]
                    for k in range(8):
                        S.mm(pb[:, 0:ncol], hT[:, k, ti * 128:(ti + 1) * 128], Wb[:, k, wc:wc + ncol],
                             start=(k == 0), stop=(k == 7))
                    if silu:
                        S.act(ts_[:, pcol:pcol + ncol], pb[:, 0:ncol], AF.Silu)
                    else:
                        S.copy(ts_[:, pcol:pcol + ncol], pb[:, 0:ncol], eng="dve")
                S.dma(PT[tok0 + ti * 128: tok0 + (ti + 1) * 128, :], ts_, q="act")
        P.close()
        PM.close()

        if stop_after == 'A':
            raise _Stop()
        P = Pool(S)
        zt = P.t("zt", [60, 160]); S.memset(zt, 0.0)
        S.dma(RP, zt)
        S.dma(RP[:, 64:95], W["na_rpb"][l].rearrange("h r c -> (h r) c"))
        Gall = P.t("Gall", [64, 60, 2, 64])
        for dup in range(2):
            S.dma(Gall[:, :, dup, :], bass.AP(RP.tensor, 16, [[1, 64], [160, 60], [1, 64]]))
        colm = P.t("colm", [128, 64]); S.dma(colm, C["colmask"])
        a64 = anti64
        BT = P.t("BT", [128, 4, 15, 64])
        for h in range(4):
            for r8 in range(0, 15, 8):
                nr = min(8, 15 - r8)
                pb = PS[(h * 2 + r8 // 8) % 4]
                for j in range(nr):
                    ro = r8 + j
                    S.mm(pb[:, j * 64:(j + 1) * 64], Gall[:, h * 15 + ro].rearrange("p a b -> p (a b)"), a64)
                for j in range(nr):
                    ro = r8 + j
                    S.tt(BT[:, h, 14 - ro, :], pb[:, j * 64:(j + 1) * 64], colm, ALU.add)
        negblk = P.t("negblk", [128, 64]); S.dma(negblk, C["negblk"])
        NCOMP = 40
        comp_tiles = [P.t(f"comp{i}", [128, 128], BF16) for i in range(NCOMP)]
        comp_map = {}

        def get_comp(h, blocks):
            key = (h, blocks)
            if key in comp_map:
                return comp_map[key]
            idx = len(comp_map)
            assert idx < NCOMP
            tl = comp_tiles[idx]
            for (a, b_), (valid, ro) in zip(((0, 0), (0, 1), (1, 0), (1, 1)), blocks):
                dst = tl[a * 64:(a + 1) * 64, b_ * 64:(b_ + 1) * 64]
                if valid:
                    S.copy(dst, BT[a * 64:(a + 1) * 64, h, 14 - ro, :], eng="pool")
                else:
                    S.copy(dst, negblk[a * 64:(a + 1) * 64, :], eng="pool")
            comp_map[key] = tl
            return tl

        KT = P.t("KT", [128, 2, T], BF16)
        QT = P.t("QT", [128, 2, T], BF16)
        Vb = P.t("Vb", [128, NTILE, 4, 65], BF16)
        S.memset(Vb.rearrange("p a b c -> p (a b c)"), 1.0, eng="pool")
        ldq = [P.t("ldq0", [128, 1088]), P.t("ldq1", [128, 1088])]
        nl = 0
        for c2 in range(2):
            for t0 in range(0, T, 1088):
                b = ldq[nl % 2]; nl += 1
                S.dma(b, PF[544 + c2 * 128: 544 + (c2 + 1) * 128, t0:t0 + 1088], q="sp")
                S.ts(QT[:, c2, t0:t0 + 1088], b, 0.125, None, ALU.mult)
                b = ldq[nl % 2]; nl += 1
                S.dma(b, PF[800 + c2 * 128: 800 + (c2 + 1) * 128, t0:t0 + 1088], q="sp")
                S.copy(KT[:, c2, t0:t0 + 1088], b, eng="act")
        ldv = [P.t("ldv0", [128, 256]), P.t("ldv1", [128, 256])]
        for ti in range(NTILE):
            b = ldv[ti % 2]
            S.dma(b, PT[ti * 128:(ti + 1) * 128, 256:512], q="sp")
            S.copy(Vb[:, ti, :, 0:64], b.rearrange("p (a b) -> p a b", a=4), eng=("dve" if ti % 2 == 0 else "act"))
        Pb = [P.t(f"Pb{i}", [128, 7, 128], BF16) for i in range(3)]
        on_ = [P.t("on0", [128, 4, 65]), P.t("on1", [128, 4, 65])]
        yn = [P.t("yn0", [128, 256]), P.t("yn1", [128, 256])]
        rcn = [P.t("rcn0", [128, 4, 1]), P.t("rcn1", [128, 4, 1])]
        kt_cache = {}

        def na_ktiles(qt):
            if qt in kt_cache:
                return kt_cache[qt]
            if qt < 2:
                ktiles = [(0, None), (1, None)]
            else:
                r0 = (qt - 2) * 2
                rows_needed = set()
                for b_ in range(2):
                    stt_ = min(max(r0 + b_ - 4, 0), 56)
                    rows_needed.update(range(stt_, stt_ + 8))
                kts = sorted(set(r // 2 for r in rows_needed))
                ktiles = []
                for kt in kts:
                    blocks = []
                    for a_ in range(2):
                        for b_ in range(2):
                            krow = kt * 2 + a_; qrow = r0 + b_
                            stt_ = min(max(qrow - 4, 0), 56)
                            valid = stt_ <= krow < stt_ + 8
                            blocks.append((valid, krow - qrow + 7))
                    ktiles.append((kt + 2, tuple(blocks)))
                ktiles += [(0, None), (1, None)]
            kt_cache[qt] = ktiles
            return ktiles

        items = [(qt, h) for qt in range(2 if last else 0, NTILE) for h in range(4)]

        def na_scores(n):
            qt, h = items[n]
            ktiles = na_ktiles(qt); nk = len(ktiles)
            c2 = h // 2; pp = (h % 2) * 64
            psA = PS[(n % 2) * 2]; psB = PS[(n % 2) * 2 + 1]
            pbuf = Pb[n % 3]
            for idx, (kt, blocks) in enumerate(ktiles):
                pdst = (psA if idx < 4 else psB)[:, (idx % 4) * 128:(idx % 4 + 1) * 128]
                S.mm(pdst, KT[pp:pp + 64, c2, kt * 128:(kt + 1) * 128], QT[pp:pp + 64, c2, qt * 128:(qt + 1) * 128],
                     start=True, stop=(blocks is None))
                if blocks is not None:
                    S.mm(pdst, identb, get_comp(h, blocks), start=False, stop=True)
            n1 = min(nk, 4)
            S.act(pbuf[:, 0:n1, :], psA[:, 0:n1 * 128].rearrange("p (a b) -> p a b", a=n1), AF.Exp)
            if nk > 4:
                S.act(pbuf[:, 4:nk, :], psB[:, 0:(nk - 4) * 128].rearrange("p (a b) -> p a b", a=nk - 4), AF.Exp)

        def na_pv(n):
            qt, h = items[n]
            ktiles = na_ktiles(qt); nk = len(ktiles)
            pO = PS[4 + (qt % 2)]
            pbuf = Pb[n % 3]
            for idx, (kt, blocks) in enumerate(ktiles):
                S.mm(pO[:, h * 65:(h + 1) * 65], pbuf[:, idx, :], Vb[:, kt, h, :], start=(idx == 0), stop=(idx == nk - 1))
            if h == 3:
                ob = on_[qt % 2]
                S.evac(ob, pO[:, 0:260].rearrange("p (a b) -> p a b", a=4))
                S.recip(rcn[qt % 2], ob[:, :, 64:65])
                S.tt(yn[qt % 2].rearrange("p (a b) -> p a b", a=4), ob[:, :, 0:64],
                     rcn[qt % 2].to_broadcast([128, 4, 64]), ALU.mult)
                S.dma(YS[qt * 128:(qt + 1) * 128, 256:512], yn[qt % 2], q="act")

        for step in range(len(items) + 1):
            if step < len(items):
                na_scores(step)
            if step >= 1:
                na_pv(step - 1)
        PP = Pool(S)
        wpl = PP.t("wpl", [128, 2, 64]); wplb = PP.t("wplb", [128, 2, 64], BF16)
        S.dma(wpl, W["pool_w"][l].rearrange("(a b) c e -> (b c) a e", b=2))
        S.copy(wplb, wpl)
        pscale = PP.t("pscale", [128, 256]); S.dma(pscale, dram_rows_bcast(W["pool_scale"], l * 256, 256))
        HALO = 32
        SEGW = PAD_LAT + 2048 + HALO
        segs = [(0, SEGW, [(PAD_CTX, 0, NCTX), (PAD_LAT, NCTX, NCTX + 2048 + HALO)], list(range(0, 18))),
                (PAD_LAT + 2048 - HALO, PT_PAD - (PAD_LAT + 2048 - HALO), [(0, NCTX + 2048 - HALO, T)], list(range(18, NTILE)))]
        pU = PP.t("pU", [128, SEGW]); prc = PP.t("prc", [128, SEGW])
        psA_ = PP.t("psA", [128, SEGW]); psB_ = PP.t("psB", [128, SEGW])
        pdb = [PP.t("pdb0", [128, SEGW], BF16), PP.t("pdb1", [128, SEGW], BF16)]
        pst = [PP.t("pst0", [128, 256]), PP.t("pst1", [128, 256])]
        for (seg0, seglen, pieces, tiles_) in segs:
            for tl in range(2):
                U = pU; rc = prc; sA = psA_; sB = psB_
                S.memset(U, 0.0, eng="pool")
                for (loff, c0, c1) in pieces:
                    S.dma(U[:, loff:loff + (c1 - c0)], PF[1056 + tl * 128: 1056 + (tl + 1) * 128, c0:c1], q="sp")
                for hh in range(2):
                    S.dma(rc[hh * 64:(hh + 1) * 64, 0:seglen],
                          dram_rows_bcast(C["poolrc"], (tl * 2 + hh) * PT_PAD + seg0, seglen, parts=64), q="sp")
                S.memset(sA, 0.0, eng="pool"); S.memset(sB, 0.0, eng="pool")
                L0, L1 = 12, seglen - 12
                fins = [None, None]
                for hh in range(2):
                    w = (2, 4, 8, 16)[tl * 2 + hh]
                    ps_ = slice(hh * 64, (hh + 1) * 64)
                    eng = "dve" if hh == 0 else "pool"
                    S.tt(sA[ps_, L0:L1], U[ps_, L0 - 1:L1 - 1], U[ps_, L0:L1], ALU.add, eng=eng)
                    cur, oth = sA, sB
                    sh = 1
                    ww = 2
                    while ww < w:
                        S.tt(oth[ps_, L0:L1], cur[ps_, L0 - sh:L1 - sh], cur[ps_, L0 + sh:L1 + sh], ALU.add, eng=eng)
                        cur, oth = oth, cur
                        sh *= 2
                        ww *= 2
                    S.tt(oth[ps_, 0:seglen], cur[ps_, 0:seglen], rc[ps_, 0:seglen], ALU.mult, eng=eng)
                    S.tt(oth[ps_, 0:seglen], oth[ps_, 0:seglen], U[ps_, 0:seglen], ALU.subtract, eng=eng)
                    fins[hh] = oth
                S.copy(pdb[tl][0:64, 0:seglen], fins[0][0:64, 0:seglen], eng="act")
                S.copy(pdb[tl][64:128, 0:seglen], fins[1][64:128, 0:seglen], eng="act")
            for ti in tiles_:
                if last and ti < 2:
                    continue
                goff = (PAD_CTX + ti * 128) if ti < 2 else (PAD_LAT + (ti - 2) * 128)
                off = goff - seg0
                pbs = (PS[6], PS[7])
                for i in range(4):
                    tl, hh = i // 2, i % 2
                    ps_ = slice(hh * 64, (hh + 1) * 64)
                    S.mm(pbs[hh][:, tl * 64:(tl + 1) * 64], pdb[tl][ps_, off:off + 128], wplb[ps_, tl, :])
                for hh in range(2):
                    S.tt(pst[ti % 2].rearrange("p (tl hh e) -> p tl hh e", tl=2, hh=2)[:, :, hh, :],
                         pbs[hh][:, 0:128].rearrange("p (tl e) -> p tl e", tl=2),
                         pscale.rearrange("p (tl hh e) -> p tl hh e", tl=2, hh=2)[:, :, hh, :], ALU.mult)
                S.dma(YS[ti * 128:(ti + 1) * 128, 768:1024], pst[ti % 2], q="act")
        PP.close()
        P.close()

        if stop_after == 'N':
            raise _Stop()
        PU = Pool(S)
        UTf = PU.t("UTf", [128, 16, NCH], BF16)
        UTb = PU.t("UTb", [128, 16, NCH], BF16)
        P = Pool(S)
        hm = P.t("hm", [128, 4, 128]); S.dma(hm, C["hm"])
        hmb = P.t("hmb", [128, 4, 128], BF16); S.copy(hmb, hm)
        bdm = P.t("bdm", [128, 256]); S.dma(bdm, C["bdm"])
        maskt = [P.t("maskf", [128, 128]), P.t("maskb", [128, 128])]
        S.dma(maskt[0], C["maskf"]); S.dma(maskt[1], C["maskb"])
        gnorm = P.t("gnorm", [128, 4, 64])
        for h in range(4):
            S.dma(gnorm[:, h, :], dram_rows_bcast(W["gla_g_norm"], l * 64, 64))
        OGd = [OG, OGB]
        for d in range(2):
            wg = P.t("wg", [16, 128]); negb = P.t("negb", [128, 1])
            Sbd = P.t("Sbd", [128, 256]); Sbdb = P.t("Sbdb", [128, 256], BF16); stmp = P.t("stmp", [128, 256])
            glr = P.t("glr", [16, 512])
            qr2 = [P.t("qr0", [128, 512]), P.t("qr1", [128, 512])]
            kr2 = [P.t("kr0", [128, 512]), P.t("kr1", [128, 512])]
            e1 = P.t("e1", [128, 512]); sp_ = P.t("sp", [128, 512]); cs_ = P.t("cs", [128, 512]); cb = P.t("cb", [128, 512])
            EQ = P.t("EQ", [128, 512]); EK = P.t("EK", [128, 512]); EH = P.t("EH", [128, 512])
            tots2 = [P.t("tots0", [128, 4, 3]), P.t("tots1", [128, 4, 3])]
            qtb2 = [P.t("qtb0", [128, 512], BF16), P.t("qtb1", [128, 512], BF16)]
            ktb2 = [P.t("ktb0", [128, 512], BF16), P.t("ktb1", [128, 512], BF16)]
            khb2 = [P.t("khb0", [128, 512], BF16), P.t("khb1", [128, 512], BF16)]
            Qbd = [P.t("Qbd0", [128, 4, 128], BF16), P.t("Qbd1", [128, 4, 128], BF16)]
            attm = [P.t("attm0", [128, 4, 128], BF16), P.t("attm1", [128, 4, 128], BF16)]
            vt = [P.t("vt0", [128, 256]), P.t("vt1", [128, 256])]
            vb = [P.t("vb0", [128, 256], BF16), P.t("vb1", [128, 256], BF16)]
            khT = [P.t("khT0", [128, 128], BF16), P.t("khT1", [128, 128], BF16)]
            osb = [P.t("osb0", [128, 256]), P.t("osb1", [128, 256])]
            ps_att = PS[1 + d]; ps_po = PS[3 + d]; ps_st = PS[6 + d]
            S.dma(wg, W["gla_w_gate"][l, d])
            S.dma(negb, bass.AP(W["gla_b_gate"].tensor, (l * 2 + d) * 128, [[1, 128], [1, 1]]))
            S.ts(negb, negb, -1.0, None, ALU.mult)
            S.memset(Sbd, 0.0); S.memset(Sbdb, 0.0)
            gorder = list(range(9)) if d == 0 else [0] + list(range(8, 0, -1))
            nck = 0
            for gix, gi in enumerate(gorder):
                tots = tots2[gix % 2]; qtb = qtb2[gix % 2]; ktb = ktb2[gix % 2]; khb = khb2[gix % 2]
                tok0, n = GROUPS[gi]
                ncg = n // 128
                qr = qr2[gix % 2]; kr = kr2[gix % 2]
                S.dma(qr[:, 0:n], PF[0:128, tok0:tok0 + n], q="sp")
                S.dma(kr[:, 0:n], PF[256:384, tok0:tok0 + n], q="sp")
                S.dma(glr[:, 0:n], PF[512 + 16 * d: 528 + 16 * d, tok0:tok0 + n], q="sp")
                S.mm(PS[0][:, 0:n], wg, glr[:, 0:n])
                S.act(e1[:, 0:n], PS[0][:, 0:n], AF.Exp, bias=negb, scale=-1.0)
                S.act(sp_[:, 0:n], e1[:, 0:n], AF.Ln, bias=1.0)
                for c in range(ncg):
                    sl = slice(c * 128, (c + 1) * 128)
                    S.scan(cs_[:, sl], ones1.to_broadcast([128, 128]), sp_[:, sl], 0.0)
                lastc = cs_[:, 0:n].rearrange("p (c k) -> p c k", k=128)[:, :, 127]
                S.ts(tots[:, 0:ncg, 0], lastc, -1.0 / 16, None, ALU.mult)
                S.ts(tots[:, 0:ncg, 1], lastc, 1.0 / 16, None, ALU.mult)
                S.act(tots[:, 0:ncg, 2], tots[:, 0:ncg, 0], AF.Exp)
                if d == 0:
                    S.act(EQ[:, 0:n], cs_[:, 0:n], AF.Exp, scale=-1.0 / 16)
                    S.act(EK[:, 0:n], cs_[:, 0:n], AF.Exp, scale=1.0 / 16)
                    for c in range(ncg):
                        sl = slice(c * 128, (c + 1) * 128)
                        S.act(EH[:, sl], cs_[:, sl], AF.Exp, scale=1.0 / 16, bias=tots[:, c, 0:1])
                else:
                    S.tt(cb[:, 0:n], cs_[:, 0:n], sp_[:, 0:n], ALU.subtract)
                    S.act(EH[:, 0:n], cb[:, 0:n], AF.Exp, scale=-1.0 / 16)
                    for c in range(ncg):
                        sl = slice(c * 128, (c + 1) * 128)
                        S.act(EQ[:, sl], cb[:, sl], AF.Exp, scale=1.0 / 16, bias=tots[:, c, 0:1])
                        S.act(EK[:, sl], cb[:, sl], AF.Exp, scale=-1.0 / 16, bias=tots[:, c, 1:2])
                S.stt(qtb[:, 0:n], qr[:, 0:n], 32.0 ** -0.5, EQ[:, 0:n], ALU.mult, ALU.mult)
                S.tt(ktb[:, 0:n], kr[:, 0:n], EK[:, 0:n], ALU.mult, eng="pool")
                S.tt(khb[:, 0:n], kr[:, 0:n], EH[:, 0:n], ALU.mult, eng="pool")
                corder = list(range(ncg)) if d == 0 else list(range(ncg - 1, -1, -1))
                for c in corder:
                    sl = slice(c * 128, (c + 1) * 128)
                    tk = tok0 + c * 128
                    b = nck % 2; nck += 1
                    S.dma(vt[b], PT[tk:tk + 128, 0:256], q="sp")
                    S.copy(vb[b], vt[b], eng="act")
                    need_o = not (last and tk < NCTX)
                    if need_o:
                        S.tt(Qbd[b], qtb[:, sl].unsqueeze(1).to_broadcast([128, 4, 128]), hmb, ALU.mult, eng="pool")
                        S.mm(ps_att, ktb[:, sl], Qbd[b].rearrange("p a b -> p (a b)"))
                        S.tt(attm[b], ps_att.rearrange("p (a b) -> p a b", a=4),
                             maskt[d].unsqueeze(1).to_broadcast([128, 4, 128]), ALU.mult)
                        S.mm(ps_po[:, 0:256], qtb[:, sl], Sbdb, start=True, stop=False)
                        for h in range(4):
                            S.mm(ps_po[:, h * 64:(h + 1) * 64], attm[b][:, h, :], vb[b][:, h * 64:(h + 1) * 64],
                                 start=False, stop=(h == 3))
                    S.mm(PS[5][:, 0:128], khb[:, sl], identb)
                    S.copy(khT[b], PS[5][:, 0:128], eng="act")
                    S.mm(ps_st[:, 0:256], khT[b], vb[b])
                    S.tt(stmp, ps_st[:, 0:256], bdm, ALU.mult)
                    S.stt(Sbd, Sbd, tots[:, c, 2:3], stmp, ALU.mult, ALU.add)
                    S.copy(Sbdb, Sbd, eng="act")
                    if need_o:
                        S.copy(osb[b], ps_po[:, 0:256], eng="act")
                        S.dma(OGd[d][tk:tk + 128, :], osb[b], q="act")
        NCB = 3
        cf = [P.t(f"cf{i}", [128, 256]) for i in range(NCB)]
        cbw = [P.t(f"cbw{i}", [128, 256]) for i in range(NCB)]
        csq = [P.t(f"csq{i}", [128, 256]) for i in range(NCB)]
        chs = [P.t(f"chs{i}", [128, 4, 3]) for i in range(NCB)]
        for ti in range(2 if last else 0, NTILE):
            b = ti % NCB
            tsl = slice(ti * 128, (ti + 1) * 128)
            S.dma(cf[b], OG[tsl, :], q="sp")
            S.dma(cbw[b], OGB[tsl, :], q="sp")
            ob = cf[b]; hst = chs[b]
            S.tt(ob, ob, cbw[b], ALU.add, eng="pool")
            S.act(csq[b], ob, AF.Square)
            S.op("dve", "tensor_reduce", hst[:, :, 0], csq[b].rearrange("p (a b) -> p a b", a=4), AX.X, ALU.add,
                 reads=[csq[b]], writes=[hst])
            S.ts(hst[:, :, 1], hst[:, :, 0], 1.0 / 64, EPS, ALU.mult, ALU.add)
            S.act(hst[:, :, 2], hst[:, :, 1], AF.Sqrt)
            S.recip(hst[:, :, 1], hst[:, :, 2])
            o3 = ob.rearrange("p (a b) -> p a b", a=4)
            S.tt(o3, o3, hst[:, :, 1:2].to_broadcast([128, 4, 64]), ALU.mult)
            S.tt(o3, o3, gnorm, ALU.mult, eng="pool")
            S.dma(YS[tsl, 0:256], ob, q="act")
        ublk = [P.t("ublk0", [128, 8, 256]), P.t("ublk1", [128, 8, 256])]
        ublb = [P.t("ublb0", [128, 16, 128], BF16), P.t("ublb1", [128, 16, 128], BF16)]
        blocks = [(0, 32, 0)]
        cpos = NCH_CTX
        while cpos < NCH:
            nb = min(128, NCH - cpos)
            blocks.append((cpos, nb, NCH_CTX + (NCH - (cpos + nb))))
            cpos += nb
        for bi, (c0, nb, bp) in enumerate(blocks):
            ub = ublk[bi % 2]; ubb = ublb[bi % 2]
            S.dma(ub[0:nb], PT[c0 * 8:(c0 + nb) * 8, 512:768].rearrange("(c s) f -> c s f", s=8), q="sp")
            S.copy(ubb[0:nb].rearrange("c g (s h) -> c g s h", s=8), ub[0:nb].rearrange("c s (g h) -> c g s h", g=16))
            for (UTx, perm, pos0) in ((UTf, identb, c0), (UTb, antib, bp)):
                if perm is identb:
                    pm = identb[0:nb, 0:nb]
                else:
                    pm = antib if nb == 128 else anti32b
                for g4 in range(4):
                    pb = PS[0] if g4 % 2 == 0 else PS[5]
                    for gg in range(4):
                        g = g4 * 4 + gg
                        S.mm(pb[:, gg * 128: gg * 128 + nb], ubb[0:nb, g, :], pm)
                    S.evac(UTx[:, g4 * 4:(g4 + 1) * 4, pos0:pos0 + nb],
                           pb.rearrange("p (a b) -> p a b", a=4)[:, :, 0:nb])
        P.close()

        if stop_after == 'G':
            raise _Stop()
        P0 = Pool(S)
        Toep = P0.t("Toep", [128, 2, 16, 128], BF16)
        WstR = P0.t("WstR", [128, 16, 128], BF16)
        WstI = P0.t("WstI", [128, 16, 128], BF16)
        CdR = P0.t("CdR", [128, 16, 128], BF16)
        CdI = P0.t("CdI", [128, 16, 128], BF16)
        th8 = P0.t("th8", [128, 16]); r8t = P0.t("r8t", [128, 16])
        P = Pool(S)
        lamL = P.t("lamL", [16, 2, 128])
        S.dma(lamL[:, 0, :], W["s5_lam_re"][l].rearrange("d (gp m) p -> (d gp) (m p)", m=2))
        S.dma(lamL[:, 1, :], W["s5_lam_im"][l].rearrange("d (gp m) p -> (d gp) (m p)", m=2))
        i16 = ident[0:16, 0:16]
        lam = P.t("lam", [128, 2, 16])
        for ri in range(2):
            S.mm(PS[0][:, ri * 16:(ri + 1) * 16], lamL[:, ri, :], i16)
        S.evac(lam, PS[0][:, 0:32].rearrange("p (a b) -> p a b", a=2))
        ld = P.t("ld", [2, 2, 8])
        S.dma(ld, W["s5_log_dt"][l].rearrange("d (gp m) -> m d gp", m=2), allow_slow_non_contiguous=True)
        selm = P.t("selm", [2, 128]); S.dma(selm, C["selm"])
        S.mm(PS[1][:, 0:16], selm, ld.rearrange("m d g -> m (d g)"))
        dt_ = P.t("dt", [128, 16]); S.act(dt_, PS[1][:, 0:16], AF.Exp)
        sc = {}
        for nm in ["lrd", "mag", "imag", "ang", "sn", "cs", "lbr", "lbi", "ilr", "ili", "den", "rden",
                   "nr", "t1", "t2", "cor", "coi", "th8", "r8"]:
            sc[nm] = P.t("s5_" + nm, [128, 16])
        S.tt(sc["lrd"], lam[:, 0, :], dt_, ALU.mult)
        S.act(sc["mag"], sc["lrd"], AF.Exp)
        S.act(sc["imag"], sc["lrd"], AF.Exp, scale=-1.0)
        S.act(sc["r8"], sc["lrd"], AF.Exp, scale=8.0)
        S.tt(sc["ang"], lam[:, 1, :], dt_, ALU.mult)
        range_reduce(P, sc["sn"], sc["cs"], sc["ang"], [128, 16])
        S.tt(sc["lbr"], sc["mag"], sc["cs"], ALU.mult)
        S.tt(sc["lbi"], sc["mag"], sc["sn"], ALU.mult)
        S.tt(sc["ilr"], sc["imag"], sc["cs"], ALU.mult)
        S.stt(sc["ili"], sc["imag"], -1.0, sc["sn"], ALU.mult, ALU.mult)
        S.tt(sc["den"], lam[:, 0, :], lam[:, 0, :], ALU.mult)
        S.tt(sc["t1"], lam[:, 1, :], lam[:, 1, :], ALU.mult)
        S.tt(sc["den"], sc["den"], sc["t1"], ALU.add)
        S.recip(sc["rden"], sc["den"])
        S.ts(sc["nr"], sc["lbr"], -1.0, None, ALU.add)
        S.tt(sc["t1"], sc["nr"], lam[:, 0, :], ALU.mult)
        S.tt(sc["t2"], sc["lbi"], lam[:, 1, :], ALU.mult)
        S.tt(sc["t1"], sc["t1"], sc["t2"], ALU.add)
        S.tt(sc["cor"], sc["t1"], sc["rden"], ALU.mult)
        S.tt(sc["t1"], sc["lbi"], lam[:, 0, :], ALU.mult)
        S.tt(sc["t2"], sc["nr"], lam[:, 1, :], ALU.mult)
        S.tt(sc["t1"], sc["t1"], sc["t2"], ALU.subtract)
        S.tt(sc["coi"], sc["t1"], sc["rden"], ALU.mult)
        kq = P.t("s5_kq", [128, 16], I32); kqf = P.t("s5_kqf", [128, 16])
        S.ts(kq, sc["ang"], 8.0 / TWO_PI, None, ALU.mult)
        S.copy(kqf, kq)
        S.ts(sc["t1"], sc["ang"], 8.0, None, ALU.mult)
        S.stt(sc["th8"], kqf, -TWO_PI, sc["t1"], ALU.mult, ALU.add)
        pwr = P.t("pwr", [128, 16, 9]); pwi = P.t("pwi", [128, 16, 9])
        ipr = P.t("ipr", [128, 16, 9]); ipi = P.t("ipi", [128, 16, 9])
        for (ar, ai, br, bi, eng_) in ((pwr, pwi, sc["lbr"], sc["lbi"], "dve"), (ipr, ipi, sc["ilr"], sc["ili"], "pool")):
            q1 = P.t("pwq1", [128, 16, 4]); q2 = P.t("pwq2", [128, 16, 4])
            S.memset(ar[:, :, 0], 1.0, eng=eng_); S.memset(ai[:, :, 0], 0.0, eng=eng_)
            S.copy(ar[:, :, 1], br, eng=eng_); S.copy(ai[:, :, 1], bi, eng=eng_)
            for (lo, cnt, kk) in ((2, 1, 1), (3, 2, 2), (5, 4, 4)):
                xr_s = ar[:, :, 1:1 + cnt]; xi_s = ai[:, :, 1:1 + cnt]
                yr_s = ar[:, :, kk:kk + 1].to_broadcast([128, 16, cnt]); yi_s = ai[:, :, kk:kk + 1].to_broadcast([128, 16, cnt])
                t1_ = q1[:, :, 0:cnt]; t2_ = q2[:, :, 0:cnt]
                S.tt(t1_, xr_s, yr_s, ALU.mult, eng=eng_)
                S.tt(t2_, xi_s, yi_s, ALU.mult, eng=eng_)
                S.tt(ar[:, :, lo:lo + cnt], t1_, t2_, ALU.subtract, eng=eng_)
                S.tt(t1_, xr_s, yi_s, ALU.mult, eng=eng_)
                S.tt(t2_, xi_s, yr_s, ALU.mult, eng=eng_)
                S.tt(ai[:, :, lo:lo + cnt], t1_, t2_, ALU.add, eng=eng_)
        rpr = P.t("rpr", [128, 16, 8]); rpi = P.t("rpi", [128, 16, 8])
        for t_ in range(8):
            S.copy(rpr[:, :, t_], pwr[:, :, 8 - t_]); S.copy(rpi[:, :, t_], pwi[:, :, 8 - t_], eng="pool")
        Br = P.t("Br", [128, 16, 16]); Bi = P.t("Bi", [128, 16, 16])
        for (dst, nm) in ((Br, "s5_b_re"), (Bi, "s5_b_im")):
            S.dma(dst.rearrange("p (d g) h -> p d g h", d=2),
                  W[nm][l].rearrange("d (gp m) p h -> (m p) d gp h", m=2))
        Bbr = P.t("Bbr", [128, 16, 16]); Bbi = P.t("Bbi", [128, 16, 16]); tb = P.t("tb", [128, 16, 16])
        cor_b = sc["cor"].unsqueeze(2).to_broadcast([128, 16, 16])
        coi_b = sc["coi"].unsqueeze(2).to_broadcast([128, 16, 16])
        S.tt(Bbr, Br, cor_b, ALU.mult); S.tt(tb, Bi, coi_b, ALU.mult); S.tt(Bbr, Bbr, tb, ALU.subtract)
        S.tt(Bbi, Bi, cor_b, ALU.mult); S.tt(tb, Br, coi_b, ALU.mult); S.tt(Bbi, Bbi, tb, ALU.add)
        CL = P.t("CL", [16, 2, 32, 64])
        S.dma(CL[:, 0], W["s5_c_re"][l].rearrange("d g h p -> h (d g) p"))
        S.dma(CL[:, 1], W["s5_c_im"][l].rearrange("d g h p -> h (d g) p"))
        Cr = P.t("Cr", [128, 16, 16]); Ci = P.t("Ci", [128, 16, 16])
        for ri, dst in ((0, Cr), (1, Ci)):
            for dd in range(2):
                for gp in range(8):
                    for m in range(2):
                        g = gp * 2 + m
                        S.mm(PS[2 + ri][m * 64:(m + 1) * 64, (dd * 8 + gp) * 16:(dd * 8 + gp + 1) * 16],
                             CL[:, ri, dd * 16 + g, :], i16)
            S.evac(dst, PS[2 + ri][:, 0:256].rearrange("p (a b) -> p a b", a=16))
        Ar = P.t("Ar", [128, 16, 8, 16]); Ai = P.t("Ai", [128, 16, 8, 16])
        Cqr = P.t("Cqr", [128, 16, 8, 16]); Cqi = P.t("Cqi", [128, 16, 8, 16])
        Cdr = P.t("Cdr", [128, 16, 8, 16]); Cdi = P.t("Cdi", [128, 16, 8, 16])
        Wsr = P.t("Wsr", [128, 16, 8, 16]); Wsi = P.t("Wsi", [128, 16, 8, 16])
        t4a = P.t("t4a", [128, 8, 8, 16]); t4b = P.t("t4b", [128, 8, 8, 16])

        def cmul(outr, outi, pr, pi_, xr, xi, neg_im=False):
            S.tt(outr, pr, xr, ALU.mult); S.tt(t4a, pi_, xi, ALU.mult, eng="pool")
            S.tt(outr, outr, t4a, ALU.subtract)
            S.tt(outi, pr, xi, ALU.mult); S.tt(t4b, pi_, xr, ALU.mult, eng="pool")
            S.tt(outi, outi, t4b, ALU.add)
            if neg_im:
                S.ts(outi, outi, -1.0, None, ALU.mult, eng="pool")

        for dd in range(2):
            ds_ = slice(dd * 8, (dd + 1) * 8)

            def pw_b(tr, ti_, lo, step):
                a_ = tr[:, ds_, lo:lo + 8]; b_ = ti_[:, ds_, lo:lo + 8]
                return (a_.unsqueeze(3).to_broadcast([128, 8, 8, 16]), b_.unsqueeze(3).to_broadcast([128, 8, 8, 16]))

            def x_b(xr, xi):
                return (xr[:, ds_, :].unsqueeze(2).to_broadcast([128, 8, 8, 16]),
                        xi[:, ds_, :].unsqueeze(2).to_broadcast([128, 8, 8, 16]))
            bbr_, bbi_ = x_b(Bbr, Bbi)
            cr_, ci_ = x_b(Cr, Ci)
            if dd == 0:
                pa = pw_b(ipr, ipi, 0, 1)
                pc = pw_b(pwr, pwi, 0, 1)
                pd = pw_b(pwr, pwi, 1, 1)
            else:
                pa = pw_b(pwr, pwi, 0, 1)
                pc = pw_b(ipr, ipi, 0, 1)
                pd = pw_b(rpr, rpi, 0, 1)
            cmul(Ar[:, ds_], Ai[:, ds_], pa[0], pa[1], bbr_, bbi_)
            cmul(Cqr[:, ds_], Cqi[:, ds_], pc[0], pc[1], cr_, ci_, neg_im=True)
            cmul(Cdr[:, ds_], Cdi[:, ds_], pd[0], pd[1], cr_, ci_, neg_im=True)
            if dd == 0:
                p7r = pwr[:, ds_, 7:8].unsqueeze(3).to_broadcast([128, 8, 8, 16])
                p7i = pwi[:, ds_, 7:8].unsqueeze(3).to_broadcast([128, 8, 8, 16])
                S.tt(Wsr[:, ds_], Ar[:, ds_], p7r, ALU.mult); S.tt(t4a, Ai[:, ds_], p7i, ALU.mult)
                S.tt(Wsr[:, ds_], Wsr[:, ds_], t4a, ALU.subtract)
                S.tt(Wsi[:, ds_], Ar[:, ds_], p7i, ALU.mult); S.tt(t4b, Ai[:, ds_], p7r, ALU.mult)
                S.tt(Wsi[:, ds_], Wsi[:, ds_], t4b, ALU.add)
            else:
                S.copy(Wsr[:, ds_], Ar[:, ds_]); S.copy(Wsi[:, ds_], Ai[:, ds_], eng="pool")
        S.copy(th8, sc["th8"]); S.copy(r8t, sc["r8"])
        S.copy(CdR, Cdr.rearrange("p a b c -> p a (b c)"))
        S.copy(CdI, Cdi.rearrange("p a b c -> p a (b c)"), eng="pool")
        tmk = [P.t("tmf", [128, 128]), P.t("tmb", [128, 128])]
        S.dma(tmk[0], C["tmaskf"]); S.dma(tmk[1], C["tmaskb"])
        for dd in range(2):
            for gp in range(8):
                dg = dd * 8 + gp
                for m in range(2):
                    g = gp * 2 + m
                    ms = slice(m * 64, (m + 1) * 64)
                    pb = PS[(g % 2)]
                    S.mm(pb[:, 0:128], Ar[ms, dg].rearrange("p a b -> p (a b)"), Cqr[ms, dg].rearrange("p a b -> p (a b)"),
                         start=True, stop=False)
                    S.mm(pb[:, 0:128], Ai[ms, dg].rearrange("p a b -> p (a b)"), Cqi[ms, dg].rearrange("p a b -> p (a b)"),
                         start=False, stop=True)
                    S.tt(Toep[:, dd, g, :], pb[:, 0:128], tmk[dd], ALU.mult)
                pb = PS[2 + (gp % 2)]
                S.mm(pb[:, 0:128], Wsr[:, dg].rearrange("p a b -> p (a b)"), ident)
                S.mm(pb[:, 128:256], Wsi[:, dg].rearrange("p a b -> p (a b)"), ident)
                S.copy(WstR[:, dg, :], pb[:, 0:128], eng="act")
                S.copy(WstI[:, dg, :], pb[:, 128:256], eng="act")
        P.close()
        P = Pool(S)
        iota = P.t("iota", [128, NCH]); S.dma(iota, C["ciota"])
        NSB = 2
        sset = []
        for i_ in range(NSB):
            sset.append(dict(
                angc=P.t("angc", [128, NCH]), snc=P.t("snc", [128, NCH]), csc=P.t("csc", [128, NCH]),
                r8b=P.t("r8b", [128, 1]), xr=P.t("xr", [128, NCH]), xi=P.t("xi", [128, NCH]),
                ta=P.t("ta", [128, NCH]), tb=P.t("tbb", [128, NCH]), zr=P.t("zr", [128, NCH]), zi=P.t("zi", [128, NCH]),
                xpr=P.t("xpr", [128, NCH], BF16), xpi=P.t("xpi", [128, NCH], BF16)))
        yst = [P.t("yst0", [128, 8, 256]), P.t("yst1", [128, 8, 256])]
        YB = P.t("YB", [128, 5, 8, 256])
        HALF = NCH // 2
        it_s = 0
        for dd in range(2):
            UTx = UTf if dd == 0 else UTb
            for gp in range(8):
                dg = dd * 8 + gp
                B_ = sset[it_s % NSB]; slot_ = it_s % NSB; it_s += 1
                angc, snc, csc, r8b = B_["angc"], B_["snc"], B_["csc"], B_["r8b"]
                xr_, xi_, ta, tbb, zr, zi, xpr, xpi = B_["xr"], B_["xi"], B_["ta"], B_["tb"], B_["zr"], B_["zi"], B_["xpr"], B_["xpi"]
                for hf in range(2):
                    cs0 = hf * HALF
                    for m in range(2):
                        g = gp * 2 + m
                        S.mm(PS[hf][m * 64:(m + 1) * 64, 0:HALF], WstR[:, dg, m * 64:(m + 1) * 64], UTx[:, g, cs0:cs0 + HALF])
                        S.mm(PS[2 + hf][m * 64:(m + 1) * 64, 0:HALF], WstI[:, dg, m * 64:(m + 1) * 64], UTx[:, g, cs0:cs0 + HALF])
                S.act(angc, iota, AF.Copy, scale=th8[:, dg:dg + 1])
                range_reduce(P, snc, csc, angc, [128, NCH], slot=slot_)
                for hf in range(2):
                    sl = slice(hf * HALF, (hf + 1) * HALF)
                    S.tt(xr_[:, sl], PS[hf][:, 0:HALF], csc[:, sl], ALU.mult)
                    S.tt(ta[:, sl], PS[2 + hf][:, 0:HALF], snc[:, sl], ALU.mult)
                    S.tt(xi_[:, sl], PS[2 + hf][:, 0:HALF], csc[:, sl], ALU.mult)
                    S.tt(tbb[:, sl], PS[hf][:, 0:HALF], snc[:, sl], ALU.mult)
                S.tt(xr_, xr_, ta, ALU.add, eng="pool")
                S.tt(xi_, xi_, tbb, ALU.subtract, eng="pool")
                S.copy(r8b, r8t[:, dg:dg + 1], eng="act")
                S.scan(zr, r8b.to_broadcast([128, NCH]), xr_, 0.0)
                S.scan(zi, r8b.to_broadcast([128, NCH]), xi_, 0.0)
                S.tt(ta, zr, csc, ALU.mult); S.tt(tbb, zi, snc, ALU.mult, eng="pool")
                S.memset(xpr[:, 0:1], 0.0, eng="pool"); S.memset(xpi[:, 0:1], 0.0, eng="pool")
                S.tt(xpr[:, 1:NCH], ta[:, 0:NCH - 1], tbb[:, 0:NCH - 1], ALU.subtract)
                S.tt(ta, zr, snc, ALU.mult, eng="pool"); S.tt(tbb, zi, csc, ALU.mult, eng="pool")
                S.tt(xpi[:, 1:NCH], ta[:, 0:NCH - 1], tbb[:, 0:NCH - 1], ALU.add)
                for m in range(2):
                    g = gp * 2 + m
                    ms = slice(m * 64, (m + 1) * 64)
                    for bi, (c0, nb, bp) in enumerate(blocks):
                        pos0 = c0 if dd == 0 else bp
                        pb = PS[4 + ((bi + m) % 4)]
                        S.mm(pb[0:nb, 0:128], UTx[:, g, pos0:pos0 + nb], Toep[:, dd, g, :], start=True, stop=False)
                        S.mm(pb[0:nb, 0:128], xpr[ms, pos0:pos0 + nb], CdR[ms, dg, :], start=False, stop=False)
                        S.mm(pb[0:nb, 0:128], xpi[ms, pos0:pos0 + nb], CdI[ms, dg, :], start=False, stop=True)
                        S.copy(YB[0:nb, bi, :, g * 16:(g + 1) * 16], pb[0:nb, 0:128].rearrange("c (t h) -> c t h", t=8), eng="act")
            for bi, (c0, nb, bp) in enumerate(blocks):
                if dd == 0:
                    S.dma(YSF[c0 * 8:(c0 + nb) * 8, :].rearrange("(c t) f -> c t f", t=8), YB[0:nb, bi], q="act")
                else:
                    clast = (NCH_CTX - 1 - bp) if bi == 0 else (NCH - 1 - (bp - NCH_CTX))
                    cfirst = clast - nb + 1
                    stg = yst[bi % 2]
                    aF = anti if nb == 128 else anti32
                    for q4 in range(4):
                        pbq = PS[4 + q4]
                        S.mm(pbq[0:nb, :], aF, YB[0:nb, bi].rearrange("c t f -> c (t f)")[:, q4 * 512:(q4 + 1) * 512])
                        S.evac(stg[0:nb].rearrange("c t f -> c (t f)")[:, q4 * 512:(q4 + 1) * 512], pbq[0:nb, :])
                    S.dma(YSB[cfirst * 8:(cfirst + nb) * 8, :].rearrange("(c t) f -> c t f", t=8), stg[0:nb], q="act")
        P.close()
        P0.close()
        PU.close()

        if stop_after == 'S':
            raise _Stop()
        if stop_after == 'P':
            raise _Stop()
        P = Pool(S)
        Wo = P.t("Wo", [128, 8, 1024], BF16)
        wos = [P.t("wos0", [128, 1024]), P.t("wos1", [128, 1024])]
        for k in range(8):
            S.dma(wos[k % 2], W["w_out"][l, k * 128:(k + 1) * 128, :], q=("sp" if k % 2 == 0 else "pool"))
            S.copy(Wo[:, k, :], wos[k % 2], eng=("dve" if k % 2 == 0 else "act"))
        wgl = P.t("wgl", [128, 2, 256]); wglb = P.t("wglb", [128, 2, 256], BF16)
        S.dma(wgl, W["s5_w_glu"][l].rearrange("(k p) n -> p k n", p=128))
        S.copy(wglb, wgl)
        bgl = P.t("bgl", [128, 256]); S.dma(bgl, dram_rows_bcast(W["s5_b_glu"], l * 256, 256))
        dsk = P.t("dsk", [128, 256]); S.dma(dsk, dram_rows_bcast(W["s5_d"], l * 256, 256))
        GG = [P.t("GG0", [128, 1024]), P.t("GG1", [128, 1024])]
        for r_ in range(2):
            S.tt(GG[r_], gpost, MOD[r_][:, 2048:3072], ALU.mult, eng="pool")
        NB = 4
        ys_b = [P.t(f"ys{i}", [128, 1024]) for i in range(NB)]
        gt_b = [P.t(f"gt{i}", [128, 1024]) for i in range(NB)]
        x_b = [P.t(f"xo{i}", [128, 1024]) for i in range(NB)]
        s5a = [P.t(f"s5a{i}", [128, 3, 256]) for i in range(NB)]
        y5_ = [P.t(f"y5{i}", [128, 256]) for i in range(NB)]
        g5_ = [P.t(f"g5{i}", [128, 256]) for i in range(NB)]
        t5_ = [P.t(f"t5{i}", [128, 256]) for i in range(NB)]
        g5b_ = [P.t(f"g5b{i}", [128, 256], BF16) for i in range(NB)]
        g5T_ = [P.t(f"g5T{i}", [128, 2, 128], BF16) for i in range(NB)]
        ybf_ = [P.t(f"ybf{i}", [128, 1024], BF16) for i in range(NB)]
        yT_ = [P.t(f"yT{i}", [128, 8, 128], BF16) for i in range(NB)]
        zt__ = [P.t(f"zt{i}", [128, 1024]) for i in range(NB)]
        junk_ = [P.t(f"junk{i}", [128, 512], BF16) for i in range(NB)]
        st__ = [P.t(f"st{i}", [128, 4]) for i in range(NB)]
        tiles_o = [ti for ti in range(NTILE) if not (last and ti < 2)]

        def o_stage1(n):
            ti = tiles_o[n]; b = n % NB
            tsl = slice(ti * 128, (ti + 1) * 128)
            y5, g5, t5, g5b = y5_[b], g5_[b], t5_[b], g5b_[b]
            S.dma(ys_b[b][:, 0:512], YS[tsl, 0:512], q="sp")
            S.dma(ys_b[b][:, 768:1024], YS[tsl, 768:1024], q="sp")
            S.dma(gt_b[b], PT[tsl, 768:1792], q="sp")
            S.dma(x_b[b], x_src[tsl, :], q="sp")
            S.dma(s5a[b][:, 0, :], YSF[tsl, :], q="sp")
            S.dma(s5a[b][:, 1, :], YSB[tsl, :], q="sp")
            S.dma(s5a[b][:, 2, :], PT[tsl, 512:768], q="sp")
            S.tt(s5a[b][:, 0, :], s5a[b][:, 0, :], s5a[b][:, 1, :], ALU.add, eng="pool")
            S.tt(y5, s5a[b][:, 2, :], dsk, ALU.mult)
            S.tt(y5, y5, s5a[b][:, 0, :], ALU.add)
            S.tt(t5, y5, y5, ALU.mult, eng="pool")
            S.ts(t5, t5, 0.044715, 1.0, ALU.mult, ALU.add, eng="pool")
            S.tt(t5, t5, y5, ALU.mult, eng="pool")
            S.act(t5, t5, AF.Sigmoid, scale=2.0 * float(np.sqrt(2.0 / np.pi)))
            S.tt(g5, y5, t5, ALU.mult)
            S.copy(g5b, g5, eng="pool")
            pg = PS[n % 2]
            for k in range(2):
                S.mm(pg[:, k * 128:(k + 1) * 128], g5b[:, k * 128:(k + 1) * 128], identb)

        def o_stage2(n):
            b = n % NB
            S.copy(g5T_[b], PS[n % 2][:, 0:256].rearrange("p (a b) -> p a b", a=2), eng="act")
            pz = PS[2 + n % 2]
            for k in range(2):
                S.mm(pz[:, 0:256], g5T_[b][:, k, :], wglb[:, k, :], start=(k == 0), stop=(k == 1))

        def o_stage3(n):
            b = n % NB
            t5, g5 = t5_[b], g5_[b]
            pz = PS[2 + n % 2]
            S.tt(t5, pz[:, 0:256], bgl, ALU.add)
            S.act(t5, t5, AF.Sigmoid)
            S.tt(ys_b[b][:, 512:768], g5, t5, ALU.mult)
            S.tt(ybf_[b], ys_b[b], gt_b[b], ALU.mult, eng="pool")
            for half in range(2):
                pt_ = PS[4 + half]
                for kk in range(4):
                    k = half * 4 + kk
                    S.mm(pt_[:, kk * 128:(kk + 1) * 128], ybf_[b][:, k * 128:(k + 1) * 128], identb)
                S.copy(yT_[b][:, half * 4:(half + 1) * 4, :], pt_.rearrange("p (a b) -> p a b", a=4), eng="act")

        def o_stage4(n):
            ti = tiles_o[n]; b = n % NB
            rsel = 1 if ti < 2 else 0
            tsl = slice(ti * 128, (ti + 1) * 128)
            st_, zt_ = st__[b], zt__[b]
            for nn in range(2):
                po = PS[6 + nn]
                for k in range(8):
                    S.mm(po, yT_[b][:, k, :], Wo[:, k, nn * 512:(nn + 1) * 512], start=(k == 0), stop=(k == 7))
                S.act(junk_[b], po, AF.Square, accum_out=st_[:, nn:nn + 1])
            S.tt(st_[:, 2:3], st_[:, 0:1], st_[:, 1:2], ALU.add)
            S.ts(st_[:, 2:3], st_[:, 2:3], 1.0 / D, EPS, ALU.mult, ALU.add)
            S.act(st_[:, 3:4], st_[:, 2:3], AF.Sqrt)
            S.recip(st_[:, 2:3], st_[:, 3:4])
            for nn in range(2):
                po = PS[6 + nn]
                S.stt(zt_[:, nn * 512:(nn + 1) * 512], po, st_[:, 2:3], GG[rsel][:, nn * 512:(nn + 1) * 512], ALU.mult, ALU.mult)
            S.tt(zt_, zt_, x_b[b], ALU.add, eng="pool")
            if last:
                S.dma(out[(ti - 2) * 128:(ti - 1) * 128, :], zt_, q="act")
            else:
                S.dma(xs[tsl, :], zt_, q="act")

        stages_o = [o_stage1, o_stage2, o_stage3, o_stage4]
        nit = len(tiles_o)
        for step in range(nit + len(stages_o) - 1):
            for si, fn in enumerate(stages_o):
                n = step - si
                if 0 <= n < nit:
                    fn(n)
        P.close()

    except _Stop:
        pass
    S.barrier()
    G.es.close()
    return nc


_CACHE = {}


def kernel(**inputs):
    consts = make_consts()
    B = inputs["x"].shape[0]
    if "nc" not in _CACHE:
        _CACHE["nc"] = build(debug=False)
    nc = _CACHE["nc"]
    in_maps = []
    ncores = 8
    for core in range(ncores):
        b = core % B
        m = {}
        m["xin"] = np.ascontiguousarray(np.concatenate([inputs["ctx"][b], inputs["x"][b]], axis=0).astype(np.float32))
        ccv = np.stack([inputs["c"][b], inputs["c_ctx"]], axis=-1).astype(np.float32)
        m["cc"] = np.ascontiguousarray(ccv.reshape(8, 128, 2).transpose(1, 0, 2))
        for n in WEIGHT_NAMES:
            m[n] = np.ascontiguousarray(np.asarray(inputs[n], dtype=np.float32))
        for n, v in consts.items():
            m["k_" + n] = v
        in_maps.append(m)
    res = run_bass_kernel_spmd(nc, in_maps, core_ids=list(range(ncores)))
    outs = [res.results[b]["out"] for b in range(B)]
    return np.stack(outs, axis=0).astype(np.float32)
```
